# Optimizing a Trainium2 kernel written in Bass

```python
import math
import jax, jax.numpy as jnp
from jax import lax
import numpy as np


D_MODEL = 1024
BATCH = 8
SEQ = 8192
DEPTH = 1
DEC_BATCH = 128
DEC_SEQ = 8
PAST_LEN = 8192
PAGE_SIZE = 128

N_HEADS = 8
HEAD_DIM = 64
D_ATTN = N_HEADS * HEAD_DIM
D_RNN = D_MODEL // 2
N_RNN_BLOCKS = 8
RNN_BLOCK = D_RNN // N_RNN_BLOCKS
CONV_WIDTH = 4
LRU_C = 8.0
D_MIX = D_ATTN + D_RNN
IN_COLS = 3 * D_ATTN + 2 * D_RNN
D_FF = ((8 * D_MODEL // 3 + 255) // 256) * 256
DILATED = ((128, 1), (512, 4), (2048, 16))
MAX_WINDOW = 2048
BLK = 128
EPS = 1e-6

kernel_name = 'dilated_swa_rglru_macaron_hybrid'


def rmsnorm(x, g):
    xf = x.astype(jnp.float32)
    y = xf * lax.rsqrt(jnp.mean(xf * xf, axis=-1, keepdims=True) + EPS)
    return (y * g.astype(jnp.float32)).astype(x.dtype)


def swiglu(x, wg, wu, wd):
    return (jax.nn.silu(x @ wg) * (x @ wu)) @ wd


def _fold(t, dil, sp):
    b, s = t.shape[:2]
    rest = t.shape[2:]
    t = jnp.pad(t, [(0, 0), (0, sp - s)] + [(0, 0)] * len(rest))
    t = jnp.moveaxis(t.reshape(b, sp // dil, dil, *rest), 2, 1)
    return t.reshape(b, dil, sp // (dil * BLK), BLK, *rest)


def _unfold(t, s):
    b, dil, nb, blk = t.shape[:4]
    rest = t.shape[4:]
    t = jnp.moveaxis(t.reshape(b, dil, nb * blk, *rest), 1, 2)
    return t.reshape(b, dil * nb * blk, *rest)[:, :s]


def branch_prompt(q, k, v, win, dil):
    b, s, h, c = q.shape
    span = dil * BLK
    sp = -(-s // span) * span
    nb = sp // span
    qb, kb, vb = _fold(q, dil, sp), _fold(k, dil, sp), _fold(v, dil, sp)

    def with_prev(t):
        prev = jnp.pad(t[:, :, :-1], ((0, 0), (0, 0), (1, 0), (0, 0), (0, 0), (0, 0)))
        return jnp.concatenate([prev, t], axis=3)

    kk, vv = with_prev(kb), with_prev(vb)
    sc = jnp.einsum('bgnqhc,bgnkhc->bgnhqk', qb, kk, preferred_element_type=jnp.float32)
    qi = jnp.arange(BLK)[:, None]
    kj = jnp.arange(2 * BLK)[None, :]
    dist = BLK + qi - kj
    band = (dist >= 0) & (dist <= win // dil)
    has_prev = (jnp.arange(nb)[:, None, None] > 0) | (kj[None] >= BLK)
    valid = band[None] & has_prev
    sc = jnp.where(valid[None, None, :, None], sc, -jnp.inf)
    m = sc.max(-1)
    p = jnp.exp(sc - m[..., None])
    den = p.sum(-1)
    o = jnp.einsum('bgnhqk,bgnkhc->bgnqhc', p, vv.astype(jnp.float32))
    m = jnp.moveaxis(m, 3, 4)
    den = jnp.moveaxis(den, 3, 4)
    return _unfold(o, s), _unfold(m, s), _unfold(den, s)


def branch_sample(q, kcat, vcat, win, dil):
    t = q.shape[1]
    past = kcat.shape[1] - t
    idx = past + jnp.arange(t)[:, None] - dil * jnp.arange(win // dil + 1)[None, :]
    valid = idx >= 0
    idx = jnp.maximum(idx, 0)
    kg = kcat[:, idx]
    vg = vcat[:, idx]
    sc = jnp.einsum('bthc,btjhc->bthj', q, kg, preferred_element_type=jnp.float32)
    sc = jnp.where(valid[None, :, None, :], sc, -jnp.inf)
    m = sc.max(-1)
    p = jnp.exp(sc - m[..., None])
    den = p.sum(-1)
    o = jnp.einsum('bthj,btjhc->bthc', p, vg.astype(jnp.float32))
    return o, m, den


def merge_branches(parts):
    o = jnp.stack([pt[0] for pt in parts])
    m = jnp.stack([pt[1] for pt in parts])
    d = jnp.stack([pt[2] for pt in parts])
    w = jnp.exp(m - m.max(0, keepdims=True))
    return (w[..., None] * o).sum(0) / (w * d).sum(0)[..., None]


def conv_rglru(u, conv_buf, h0, conv_w, conv_b, w_a, b_a, w_x, b_x, lam):
    b, t, r = u.shape
    full = jnp.concatenate([conv_buf.astype(u.dtype), u], axis=1)
    xc = conv_b + conv_w[0] * full[:, 0:t]
    for j in range(1, CONV_WIDTH):
        xc = xc + conv_w[j] * full[:, j:j + t]
    new_buf = full[:, t:]
    xf = xc.astype(jnp.float32)
    xb = xf.reshape(b, t, N_RNN_BLOCKS, RNN_BLOCK)
    rg = jax.nn.sigmoid(jnp.einsum('btnc,ncd->btnd', xb, w_a.astype(jnp.float32)) + b_a.astype(jnp.float32)).reshape(b, t, r)
    ig = jax.nn.sigmoid(jnp.einsum('btnc,ncd->btnd', xb, w_x.astype(jnp.float32)) + b_x.astype(jnp.float32)).reshape(b, t, r)
    log_a = -LRU_C * rg * jax.nn.softplus(-lam.astype(jnp.float32))
    a = jnp.exp(log_a)
    bt = jnp.sqrt(-jnp.expm1(2.0 * log_a)) * ig * xf
    bt = bt.at[:, 0].add(a[:, 0] * h0.astype(jnp.float32))

    def comb(lhs, rhs):
        a1, b1 = lhs
        a2, b2 = rhs
        return a1 * a2, a2 * b1 + b2

    _, hs = lax.associative_scan(comb, (a, bt), axis=1)
    return hs.astype(u.dtype), hs[:, -1].astype(u.dtype), new_buf


def decoder_layer(x, conv_buf, h0, k_past, v_past, lw):
    (ln1, w1g, w1u, w1d, ln_m, w_in, conv_w, conv_b, w_a, b_a, w_x, b_x, lam,
     w_out, ln2, w2g, w2u, w2d) = lw
    b, t, _ = x.shape
    x = x + 0.5 * swiglu(rmsnorm(x, ln1), w1g, w1u, w1d)
    z = rmsnorm(x, ln_m) @ w_in
    q, k, v, u, g = jnp.split(z, [D_ATTN, 2 * D_ATTN, 3 * D_ATTN, 3 * D_ATTN + D_RNN], axis=-1)
    q = q.reshape(b, t, N_HEADS, HEAD_DIM) * (HEAD_DIM ** -0.5)
    k = k.reshape(b, t, N_HEADS, HEAD_DIM)
    v = v.reshape(b, t, N_HEADS, HEAD_DIM)
    if k_past is None:
        parts = [branch_prompt(q, k, v, win, dil) for win, dil in DILATED]
        keep = min(MAX_WINDOW, t)
        k_state, v_state = k[:, t - keep:], v[:, t - keep:]
    else:
        kcat = jnp.concatenate([k_past.astype(k.dtype), k], axis=1)
        vcat = jnp.concatenate([v_past.astype(v.dtype), v], axis=1)
        parts = [branch_sample(q, kcat, vcat, win, dil) for win, dil in DILATED]
        k_state, v_state = k, v
    attn = merge_branches(parts).astype(x.dtype).reshape(b, t, D_ATTN)
    rnn, h_last, new_buf = conv_rglru(u, conv_buf, h0, conv_w, conv_b, w_a, b_a, w_x, b_x, lam)
    x = x + jnp.concatenate([attn, rnn * jax.nn.gelu(g)], axis=-1) @ w_out
    x = x + 0.5 * swiglu(rmsnorm(x, ln2), w2g, w2u, w2d)
    return x, k_state, v_state, h_last, new_buf


def setup_inputs(seed: int = 0) -> dict:
    key = jax.random.key(seed)
    ks = jax.random.split(key, 32)
    w_buf = min(MAX_WINDOW, PAST_LEN)

    def nrm(k, shape, scale):
        return jax.random.normal(k, shape, jnp.float32) * scale

    a0 = jax.random.uniform(ks[18], (DEPTH, D_RNN), jnp.float32, 0.9, 0.999)
    sig = a0 ** (1.0 / LRU_C)
    lru_lambda = jnp.log(sig) - jnp.log1p(-sig)
    return {
        'x_prompt': nrm(ks[0], (BATCH, SEQ, D_MODEL), 1.0),
        'x_sample': nrm(ks[1], (DEC_BATCH, DEC_SEQ, D_MODEL), 1.0),
        'cache_k_win': nrm(ks[2], (DEPTH, DEC_BATCH, w_buf, N_HEADS, HEAD_DIM), 1.0),
        'cache_v_win': nrm(ks[3], (DEPTH, DEC_BATCH, w_buf, N_HEADS, HEAD_DIM), 1.0),
        'state_lru_h': nrm(ks[4], (DEPTH, DEC_BATCH, D_RNN), 0.5),
        'state_lru_conv': nrm(ks[5], (DEPTH, DEC_BATCH, CONV_WIDTH - 1, D_RNN), 1.0),
        'ln_ffn1': 1.0 + nrm(ks[6], (DEPTH, D_MODEL), 0.02),
        'w_ffn1_gate': nrm(ks[7], (DEPTH, D_MODEL, D_FF), D_MODEL ** -0.5),
        'w_ffn1_up': nrm(ks[8], (DEPTH, D_MODEL, D_FF), D_MODEL ** -0.5),
        'w_ffn1_down': nrm(ks[9], (DEPTH, D_FF, D_MODEL), D_FF ** -0.5),
        'ln_mix': 1.0 + nrm(ks[10], (DEPTH, D_MODEL), 0.02),
        'w_in': nrm(ks[11], (DEPTH, D_MODEL, IN_COLS), D_MODEL ** -0.5),
        'conv_w': nrm(ks[12], (DEPTH, CONV_WIDTH, D_RNN), CONV_WIDTH ** -0.5),
        'conv_b': nrm(ks[13], (DEPTH, D_RNN), 0.01),
        'w_gate_a': nrm(ks[14], (DEPTH, N_RNN_BLOCKS, RNN_BLOCK, RNN_BLOCK), RNN_BLOCK ** -0.5),
        'b_gate_a': nrm(ks[15], (DEPTH, N_RNN_BLOCKS, RNN_BLOCK), 0.01),
        'w_gate_x': nrm(ks[16], (DEPTH, N_RNN_BLOCKS, RNN_BLOCK, RNN_BLOCK), RNN_BLOCK ** -0.5),
        'b_gate_x': nrm(ks[17], (DEPTH, N_RNN_BLOCKS, RNN_BLOCK), 0.01),
        'lru_lambda': lru_lambda,
        'w_out': nrm(ks[19], (DEPTH, D_MIX, D_MODEL), D_MIX ** -0.5),
        'ln_ffn2': 1.0 + nrm(ks[20], (DEPTH, D_MODEL), 0.02),
        'w_ffn2_gate': nrm(ks[21], (DEPTH, D_MODEL, D_FF), D_MODEL ** -0.5),
        'w_ffn2_up': nrm(ks[22], (DEPTH, D_MODEL, D_FF), D_MODEL ** -0.5),
        'w_ffn2_down': nrm(ks[23], (DEPTH, D_FF, D_MODEL), D_FF ** -0.5),
        'ln_final': 1.0 + nrm(ks[24], (D_MODEL,), 0.02),
    }


def reference(x_prompt, x_sample, cache_k_win, cache_v_win, state_lru_h, state_lru_conv,
              ln_ffn1, w_ffn1_gate, w_ffn1_up, w_ffn1_down, ln_mix, w_in, conv_w, conv_b,
              w_gate_a, b_gate_a, w_gate_x, b_gate_x, lru_lambda, w_out,
              ln_ffn2, w_ffn2_gate, w_ffn2_up, w_ffn2_down, ln_final):
    xp, xs = x_prompt, x_sample
    bp = xp.shape[0]
    kp_l, vp_l, hp_l, cp_l = [], [], [], []
    ks_l, vs_l, hs_l, cs_l = [], [], [], []
    for l in range(DEPTH):
        lw = (ln_ffn1[l], w_ffn1_gate[l], w_ffn1_up[l], w_ffn1_down[l], ln_mix[l], w_in[l],
              conv_w[l], conv_b[l], w_gate_a[l], b_gate_a[l], w_gate_x[l], b_gate_x[l],
              lru_lambda[l], w_out[l], ln_ffn2[l], w_ffn2_gate[l], w_ffn2_up[l], w_ffn2_down[l])
        zero_buf = jnp.zeros((bp, CONV_WIDTH - 1, D_RNN), xp.dtype)
        zero_h = jnp.zeros((bp, D_RNN), xp.dtype)
        xp, kp, vp, hp, cp = decoder_layer(xp, zero_buf, zero_h, None, None, lw)
        xs, kn, vn, hn, cn = decoder_layer(xs, state_lru_conv[l], state_lru_h[l],
                                           cache_k_win[l], cache_v_win[l], lw)
        kp_l.append(kp); vp_l.append(vp); hp_l.append(hp); cp_l.append(cp)
        ks_l.append(kn); vs_l.append(vn); hs_l.append(hn); cs_l.append(cn)
    y_prompt = rmsnorm(xp, ln_final)
    y_sample = rmsnorm(xs, ln_final)
    return (y_prompt, y_sample,
            jnp.stack(kp_l), jnp.stack(vp_l), jnp.stack(hp_l), jnp.stack(cp_l),
            jnp.stack(ks_l), jnp.stack(vs_l), jnp.stack(hs_l), jnp.stack(cs_l))
```

```python
from contextlib import ExitStack
import numpy as np
import ml_dtypes
import concourse.bass as bass
import concourse.mybir as mybir
from concourse.bass_utils import run_bass_kernel_spmd

F32 = mybir.dt.float32
BF16 = mybir.dt.bfloat16
AF = mybir.ActivationFunctionType
ALU = mybir.AluOpType

D = 1024
DFF = 2816
NFC = DFF // 128
NH = 8
HD = 64
DR = 512
INC = 2560
EPS = 1e-6
SEM_CAP = 30000
TT = 256
NSUB = TT // 128
WBUF = 2048
ARENA_WORDS = 51 * 1024
GELU_C = 1.5957691216057308


class _Op:
    __slots__ = ("eng", "fn", "reads", "writes", "stream", "is_dma", "deps", "signal", "sig")

    def __init__(self, eng, fn, reads, writes, stream):
        self.eng = eng
        self.fn = fn
        self.reads = tuple(reads)
        self.writes = tuple(writes)
        self.stream = stream
        self.is_dma = stream is not None
        self.deps = []
        self.signal = False
        self.sig = None


class Prog:
    ENGS = ("pe", "act", "dve", "pool", "sp")

    def __init__(self):
        self.ops = []
        self.res_w = {}
        self.res_r = {}
        self.bar_deps = []
        self.bar_pending = set()

    def barrier(self):
        last = {}
        for i, op in enumerate(self.ops):
            last[("dma", op.stream) if op.is_dma else ("eng", op.eng)] = i
        self.bar_deps = sorted(last.values())
        self.bar_pending = set(self.ENGS)
        self.res_w = {}
        self.res_r = {}

    def add(self, eng, fn, reads=(), writes=(), stream=None):
        op = _Op(eng, fn, reads, writes, stream)
        idx = len(self.ops)
        raw = set()
        other = set()
        for r in op.reads:
            raw.update(self.res_w.get(r, ()))
        for w in op.writes:
            other.update(self.res_w.get(w, ()))
            other.update(self.res_r.get(w, ()))
        deps = set()
        for d in raw | other:
            dop = self.ops[d]
            if dop.is_dma or op.is_dma or dop.eng != op.eng:
                deps.add(d)
            elif d in raw:
                deps.add(d)
        if eng in self.bar_pending:
            self.bar_pending.discard(eng)
            for d in self.bar_deps:
                dop = self.ops[d]
                if dop.is_dma or dop.eng != eng:
                    deps.add(d)
        op.deps = sorted(deps)
        for d in op.deps:
            self.ops[d].signal = True
        for r in op.reads:
            self.res_r.setdefault(r, []).append(idx)
        for w in op.writes:
            if self.res_r.get(w):
                self.res_w[w] = [idx]
                self.res_r[w] = []
            else:
                self.res_w.setdefault(w, []).append(idx)
        self.ops.append(op)
        return idx

    def finalize_outputs(self):
        last = {}
        for i, op in enumerate(self.ops):
            if op.is_dma:
                last[op.stream] = i
        for op in self.ops:
            if op.is_dma:
                op.signal = True
        self.final_dma = sorted(last.values())

    def emit(self, nc, sem_ctx):
        cnt = {}
        for op in self.ops:
            if not op.signal:
                continue
            key = ("dma", op.stream) if op.is_dma else ("eng", op.eng)
            c = cnt.get(key, 0)
            epoch, within = divmod(c, SEM_CAP if not op.is_dma else SEM_CAP // 16)
            cnt[key] = c + 1
            semname = "%s_%s_%d" % (key[0], key[1], epoch)
            val = (within + 1) * (16 if op.is_dma else 1)
            op.sig = (semname, val)
        per_eng = {e: [] for e in self.ENGS}
        for i, op in enumerate(self.ops):
            per_eng[op.eng].append(i)
        sems = {}

        def sem(name):
            if name not in sems:
                sems[name] = sem_ctx(name)
            return sems[name]

        for op in self.ops:
            if op.sig is not None:
                sem(op.sig[0])
        final_dma = self.final_dma

        with nc.Block() as block:
            def body(ename):
                def _run(engine):
                    seen = {}

                    def wait(i):
                        sname, val = self.ops[i].sig
                        if seen.get(sname, 0) >= val:
                            return
                        seen[sname] = val
                        engine.wait_ge(sem(sname), val)

                    for i in per_eng[ename]:
                        op = self.ops[i]
                        for d in op.deps:
                            wait(d)
                        ins = op.fn(engine)
                        if op.sig is not None:
                            ins.then_inc(sem(op.sig[0]), 16 if op.is_dma else 1)
                    if ename == "sp":
                        for i in final_dma:
                            wait(i)
                return _run

            block.tensor(body("pe"))
            block.scalar(body("act"))
            block.vector(body("dve"))
            block.gpsimd(body("pool"))
            block.sync(body("sp"))


def _bcast_rows(ap1d, nparts):
    n = ap1d.shape[-1]
    return bass.AP(ap1d.tensor, ap1d.offset, [[0, nparts], [1, n]])


class Builder:
    def __init__(self, S, NB, phases=("ffn1", "mix", "ffn2")):
        assert S % 2048 == 0 and (NB * 8) % 128 == 0
        self.S = S
        self.NB = NB
        self.NS = NB * 8
        self.NT = S + self.NS
        self.phases = phases
        self.nc = bass.Bass("TRN2", target_bir_lowering=False)
        self.P = Prog()
        self.stack = None
        self._stg_i = 0

    def dram_in(self, name, shape, dt=F32):
        return self.nc.dram_tensor(name, list(shape), dt, kind="ExternalInput").ap()

    def dram_out(self, name, shape, dt=F32):
        return self.nc.dram_tensor(name, list(shape), dt, kind="ExternalOutput").ap()

    def dram_tmp(self, name, shape, dt):
        return self.nc.dram_tensor(name, list(shape), dt, kind="Internal").ap()

    def alloc(self, shape, dt):
        esz = 4 if dt == F32 else 2
        n = 1
        for s in shape[1:]:
            n *= s
        nbytes = (n * esz + 31) // 32 * 32
        w0 = self.arena_off
        nw = nbytes // 4
        assert w0 + nw <= ARENA_WORDS, "SBUF arena overflow: %d" % (w0 + nw)
        self.arena_off = w0 + nw
        v = self.arena[:, w0:w0 + nw]
        if dt != F32:
            v = v.bitcast(dt)
        v = v[:, 0:n]
        if len(shape) == 3:
            v = v.rearrange("p (a b) -> p a b", a=shape[1])
        elif len(shape) == 4:
            v = v.rearrange("p (a b c) -> p a b c", a=shape[1], b=shape[2])
        return v

    def tiles(self):
        out = []
        t = 0
        while t < self.S:
            n = min(TT, self.S - t)
            out.append((t, n))
            t += n
        if self.NS:
            out.append((self.S, self.NS))
        return out

    def load_cast(self, dst_ap, src_ap, width, scale, key, idx):
        P = self.P
        slot = self._stg_i % 2
        self._stg_i += 1
        np_ = dst_ap.shape[0]
        view = self.stg[slot][0:np_, 0:width]
        if len(dst_ap.shape) == 3:
            view = view.rearrange("p (a b) -> p a b", a=dst_ap.shape[1])
        P.add("sp", lambda e, v=view, s=src_ap: e.dma_start(out=v, in_=s),
              writes=[("stg", slot)], stream="stg%d" % slot)
        rd = [("stg", slot)] + ([] if isinstance(scale, float) else ["gains"])
        if idx % 2 == 1:
            P.add("act", lambda e, o=dst_ap, v=view, sc=scale: e.mul(o, v, sc), reads=rd, writes=[key])
        else:
            P.add("dve", lambda e, o=dst_ap, v=view, sc=scale: e.tensor_scalar_mul(o, v, sc), reads=rd, writes=[key])

    def t_load(self, x_src, tl, i):
        t0, n = tl[i]
        ns = n // 128
        slot = i % 2
        src = x_src[t0:t0 + n, :].rearrange("(s p) d -> p s d", p=128)
        self.P.add("sp", lambda e, o=self.xbuf[slot][:, 0:ns, :], s=src: e.dma_start(out=o, in_=s),
                   writes=[("x", slot)], stream="x%d" % slot)

    def t_prep(self, tl, i):
        P = self.P
        t0, n = tl[i]
        ns = n // 128
        slot = i % 2
        xs = self.xbuf[slot]
        for s in range(ns):
            P.add("act", lambda e, s=s, xs=xs: e.activation(
                out=self.junk[:, :], in_=xs[:, s, :], func=AF.Square, scale=1.0 / 32.0,
                accum_out=self.ss[:, s:s + 1]),
                reads=[("x", slot)], writes=["junk", "ss"])
        P.add("act", lambda e: e.activation(out=self.rstd[:, 0:ns], in_=self.ss[:, 0:ns], func=AF.Sqrt,
                                            bias=self.epsc[:, 0:1], scale=1.0),
              reads=["ss", "epsc"], writes=["rstd"])
        P.add("dve", lambda e: e.reciprocal(self.rstd[:, 0:ns], self.rstd[:, 0:ns]),
              reads=["rstd"], writes=["rstd"])
        for s in range(ns):
            eng = "dve" if s % 2 == 0 else "pool"
            P.add(eng, lambda e, s=s, xs=xs: e.tensor_scalar_mul(self.xn[:, s, :], xs[:, s, :],
                                                                 self.rstd[:, s:s + 1]),
                  reads=[("x", slot), "rstd"], writes=[("xn", s)])

    def t_transposes(self, tl, i):
        P = self.P
        t0, n = tl[i]
        ns = n // 128
        for s in range(ns):
            bank = self.psT[s % 2]
            for kc in range(8):
                P.add("pe", lambda e, s=s, kc=kc, bank=bank: e.transpose(
                    out=bank[:, kc * 128:(kc + 1) * 128], in_=self.xn[:, s, kc * 128:(kc + 1) * 128],
                    identity=self.ident[:, :]),
                    reads=[("xn", s), "ident"], writes=[("psT", s % 2)])
            src = bank[:, :].rearrange("p (k t) -> p k t", k=8)
            dst = self.xnT[:, :, s * 128:(s + 1) * 128]
            if s % 2 == 0:
                P.add("act", lambda e, d=dst, sr=src: e.copy(d, sr), reads=[("psT", s % 2)], writes=["xnT"])
            else:
                P.add("dve", lambda e, d=dst, sr=src: e.tensor_copy(d, sr), reads=[("psT", s % 2)],
                      writes=["xnT"])

    def common_tile_bufs(self):
        self.xbuf = [self.alloc([128, NSUB, D], F32), self.alloc([128, NSUB, D], F32)]
        self.xn = self.alloc([128, NSUB, D], BF16)
        self.xnT = self.alloc([128, 8, TT], BF16)
        self.junk = self.alloc([128, D], BF16)
        self.ss = self.alloc([128, 4], F32)
        self.rstd = self.alloc([128, 4], F32)
        self.ss2 = self.alloc([128, 4], F32)
        self.rstd2 = self.alloc([128, 4], F32)
        self.stg = [self.alloc([128, DFF // 2], F32), self.alloc([128, DFF // 2], F32)]

    def ffn_phase(self, tag, x_src, x_dst, wg, wu, wd, gcol, final_norm):
        P = self.P
        self.arena_off = self.arena_mark
        self.common_tile_bufs()
        self.wA = self.alloc([128, 8, DFF], BF16)
        self.wB = self.alloc([128, 8, DFF], BF16)
        self.wC = self.alloc([128, NFC, D], BF16)
        self.aT = self.alloc([128, NFC, TT], BF16)
        self.sg = [self.alloc([128, TT], F32), self.alloc([128, TT], F32)]
        psG = [self.pb[0], self.pb[1]]
        psU = [self.pb[2], self.pb[3]]
        psD = [self.pb[4], self.pb[5]]
        self.psT = [self.pb[6].bitcast(BF16), self.pb[7].bitcast(BF16)]
        HW = DFF // 2
        k = 0
        for kc in range(8):
            for hh in range(2):
                self.load_cast(self.wA[:, kc, hh * HW:(hh + 1) * HW],
                               wg[kc * 128:(kc + 1) * 128, hh * HW:(hh + 1) * HW],
                               HW, self.gains[:, gcol * 8 + kc:gcol * 8 + kc + 1], "wA", k)
                k += 1
        for kc in range(8):
            for hh in range(2):
                self.load_cast(self.wB[:, kc, hh * HW:(hh + 1) * HW],
                               wu[kc * 128:(kc + 1) * 128, hh * HW:(hh + 1) * HW],
                               HW, self.gains[:, gcol * 8 + kc:gcol * 8 + kc + 1], "wB", k)
                k += 1
        for fc in range(NFC):
            self.load_cast(self.wC[:, fc, :], wd[fc * 128:(fc + 1) * 128, :], D, 0.5, "wC", k)
            k += 1
        tl = self.tiles()
        xbuf = self.xbuf

        def gate_up(i):
            t0, n = tl[i]
            for fc in range(NFC):
                pb = fc % 2
                pg, pu = psG[pb], psU[pb]
                for kc in range(8):
                    P.add("pe", lambda e, fc=fc, kc=kc, pg=pg: e.matmul(
                        pg[:, 0:n], lhsT=self.wA[:, kc, fc * 128:(fc + 1) * 128], rhs=self.xnT[:, kc, 0:n],
                        start=(kc == 0), stop=(kc == 7)),
                        reads=["wA", "xnT"], writes=[("psG", pb)])
                for kc in range(8):
                    P.add("pe", lambda e, fc=fc, kc=kc, pu=pu: e.matmul(
                        pu[:, 0:n], lhsT=self.wB[:, kc, fc * 128:(fc + 1) * 128], rhs=self.xnT[:, kc, 0:n],
                        start=(kc == 0), stop=(kc == 7)),
                        reads=["wB", "xnT"], writes=[("psU", pb)])
                sg = self.sg[pb]
                P.add("act", lambda e, pg=pg, sg=sg: e.activation(out=sg[:, 0:n], in_=pg[:, 0:n], func=AF.Silu),
                      reads=[("psG", pb)], writes=[("sg", pb)])
                P.add("dve", lambda e, fc=fc, pu=pu, sg=sg: e.tensor_tensor(
                    out=self.aT[:, fc, 0:n], in0=sg[:, 0:n], in1=pu[:, 0:n], op=ALU.mult),
                    reads=[("psU", pb), ("sg", pb)], writes=[("aT", fc)])

        def down(i):
            t0, n = tl[i]
            ns = n // 128
            slot = i % 2
            xs = xbuf[slot]
            j = 0
            for s in range(ns):
                for h in range(2):
                    pb = j % 2
                    j += 1
                    pd = psD[pb]
                    for fc in range(NFC):
                        P.add("pe", lambda e, fc=fc, s=s, h=h, pd=pd: e.matmul(
                            pd[:, :], lhsT=self.aT[:, fc, s * 128:(s + 1) * 128],
                            rhs=self.wC[:, fc, h * 512:(h + 1) * 512], start=(fc == 0), stop=(fc == NFC - 1)),
                            reads=["wC", ("aT", fc)], writes=[("psD", pb)])
                    P.add("dve", lambda e, s=s, h=h, pd=pd, xs=xs: e.tensor_tensor(
                        out=xs[:, s, h * 512:(h + 1) * 512], in0=pd[:, :], in1=xs[:, s, h * 512:(h + 1) * 512],
                        op=ALU.add),
                        reads=[("psD", pb), ("x", slot)], writes=[("x", slot)])
            if final_norm:
                for s in range(ns):
                    P.add("act", lambda e, s=s, xs=xs: e.activation(
                        out=self.junk[:, :], in_=xs[:, s, :], func=AF.Square, scale=1.0 / 32.0,
                        accum_out=self.ss2[:, s:s + 1]),
                        reads=[("x", slot)], writes=["junk", "ss2"])
                P.add("act", lambda e: e.activation(out=self.rstd2[:, 0:ns], in_=self.ss2[:, 0:ns], func=AF.Sqrt,
                                                    bias=self.epsc[:, 0:1], scale=1.0),
                      reads=["ss2", "epsc"], writes=["rstd2"])
                P.add("dve", lambda e: e.reciprocal(self.rstd2[:, 0:ns], self.rstd2[:, 0:ns]),
                      reads=["rstd2"], writes=["rstd2"])
                for s in range(ns):
                    P.add("dve", lambda e, s=s, xs=xs: e.scalar_tensor_tensor(
                        out=xs[:, s, :], in0=xs[:, s, :], scalar=self.rstd2[:, s:s + 1], in1=self.lnf[:, :],
                        op0=ALU.mult, op1=ALU.mult),
                        reads=[("x", slot), "rstd2", "lnf"], writes=[("x", slot)])
            dst = x_dst[t0:t0 + n, :].rearrange("(s p) d -> p s d", p=128)
            P.add("sp", lambda e, d=dst, o=xbuf[slot][:, 0:ns, :]: e.dma_start(out=d, in_=o),
                  reads=[("x", slot)], stream="xo%d" % slot)

        nt = len(tl)
        self.t_load(x_src, tl, 0)
        if nt > 1:
            self.t_load(x_src, tl, 1)
        self.t_prep(tl, 0)
        self.t_transposes(tl, 0)
        for i in range(nt):
            gate_up(i)
            if i + 1 < nt:
                self.t_prep(tl, i + 1)
            down(i)
            if i + 2 < nt:
                self.t_load(x_src, tl, i + 2)
            if i + 1 < nt:
                self.t_transposes(tl, i + 1)
        P.barrier()

    def mix_in_phase(self, x1, zT, rnnT, vtok, io):
        P = self.P
        S, NB, NS = self.S, self.NB, self.NS
        self.arena_off = self.arena_mark
        self.common_tile_bufs()
        Win = self.alloc([128, 8, INC], BF16)
        WaBD = self.alloc([128, 4, 128], BF16)
        WxBD = self.alloc([128, 4, 128], BF16)
        cw = self.alloc([128, 4, 4], F32)
        cvec = self.alloc([128, 4, 4], F32)
        lamc = self.alloc([128, 4], F32)
        onec = self.alloc([128, 1], F32)
        zst = [self.alloc([128, 12, TT], BF16), self.alloc([128, 12, TT], BF16)]
        ubuf = self.alloc([128, 4, TT + 3], F32)
        ubs = self.alloc([128, 4, NB, 11], F32)
        h0s = self.alloc([128, 4, NB], F32)
        hprev = self.alloc([128, 4], F32)
        hfin = self.alloc([128, 4, NB], F32)
        gs = self.alloc([128, 4, TT], F32)
        gt = self.alloc([128, 4, TT], F32)
        tmp = [self.alloc([128, TT], F32) for _ in range(8)]
        xcb = self.alloc([128, TT], BF16)
        rst = [self.alloc([128, 4, TT], BF16), self.alloc([128, 4, TT], BF16)]
        tok = [self.alloc([128, 512], F32), self.alloc([128, 512], F32)]
        tokb = self.alloc([128, 512], BF16)
        psZ = [self.pb[0], self.pb[1], self.pb[2], self.pb[3]]
        psA, psX = self.pb[4], self.pb[5]
        self.psT = [self.pb[6].bitcast(BF16), self.pb[7].bitcast(BF16)]

        k = 0
        for kc in range(8):
            for hh in range(2):
                self.load_cast(Win[:, kc, hh * 1280:(hh + 1) * 1280],
                               io["w_in"][kc * 128:(kc + 1) * 128, hh * 1280:(hh + 1) * 1280],
                               1280, self.gains[:, 8 + kc:8 + kc + 1], "Win", k)
                k += 1
        self.load_cast(WaBD, io["wabd"], 512, 1.0, "WaBD", 0)
        self.load_cast(WxBD, io["wxbd"], 512, 1.0, "WxBD", 1)
        P.add("sp", lambda e: e.dma_start(out=cw, in_=io["cw"]), writes=["cw"], stream="c_cw")
        P.add("sp", lambda e: e.dma_start(out=cvec, in_=io["cvec"]), writes=["cvec"], stream="c_cvec")
        P.add("sp", lambda e: e.dma_start(out=ubs[:, :, :, 0:3], in_=io["convst"]), writes=["ubs"], stream="c_ubs")
        P.add("sp", lambda e: e.dma_start(out=h0s, in_=io["h0"]), writes=["h0s"], stream="c_h0s")
        P.add("dve", lambda e: e.memset(onec, 1.0), writes=["onec"])
        P.add("dve", lambda e: e.memset(hprev, 0.0), writes=["hprev"])
        P.add("dve", lambda e: e.memset(ubuf[:, :, 0:3], 0.0), writes=[("ub", c) for c in range(4)])
        P.add("act", lambda e: e.activation(out=lamc, in_=cvec[:, 3, :], func=AF.Exp, scale=-1.0),
              reads=["cvec"], writes=["lamc"])
        P.add("act", lambda e: e.activation(out=lamc, in_=lamc, func=AF.Ln, bias=onec[:, 0:1], scale=1.0),
              reads=["lamc", "onec"], writes=["lamc"])
        P.add("dve", lambda e: e.tensor_scalar_mul(lamc, lamc, -8.0), reads=["lamc"], writes=["lamc"])

        tl = self.tiles()
        nt = len(tl)
        nprompt = nt - 1 if NS else nt

        def zproj(i):
            t0, n = tl[i]
            is_s = (t0 >= S)
            zs = zst[i % 2]
            j = 0
            for c in range(20):
                pz = psZ[j % 4]
                pk = ("psZ", j % 4)
                j += 1
                for kc in range(8):
                    P.add("pe", lambda e, c=c, kc=kc, pz=pz: e.matmul(
                        pz[:, 0:n], lhsT=Win[:, kc, c * 128:(c + 1) * 128], rhs=self.xnT[:, kc, 0:n],
                        start=(kc == 0), stop=(kc == 7)),
                        reads=["Win", "xnT"], writes=[pk])
                if c < 4:
                    P.add("act", lambda e, c=c, pz=pz: e.mul(zs[:, c, 0:n], pz[:, 0:n], 0.125),
                          reads=[pk], writes=[("zst", i % 2)])
                elif c < 12:
                    if c % 2 == 0:
                        P.add("act", lambda e, c=c, pz=pz: e.copy(zs[:, c, 0:n], pz[:, 0:n]),
                              reads=[pk], writes=[("zst", i % 2)])
                    else:
                        P.add("dve", lambda e, c=c, pz=pz: e.tensor_copy(zs[:, c, 0:n], pz[:, 0:n]),
                              reads=[pk], writes=[("zst", i % 2)])
                elif c < 16:
                    cc = c - 12
                    if is_s:
                        P.add("act", lambda e, cc=cc, pz=pz: e.copy(
                            ubs[:, cc, :, 3:11], pz[:, 0:n].rearrange("p (b t) -> p b t", t=8)),
                            reads=[pk], writes=[("ubs", cc)])
                    else:
                        P.add("act", lambda e, cc=cc, pz=pz: e.copy(ubuf[:, cc, 3:3 + n], pz[:, 0:n]),
                              reads=[pk], writes=[("ub", cc)])
                else:
                    cc = c - 16
                    P.add("act", lambda e, cc=cc, pz=pz: e.copy(gs[:, cc, 0:n], pz[:, 0:n]),
                          reads=[pk], writes=[("gs", cc)])
            dst = zT[:, :, t0:t0 + n].rearrange("c p t -> p c t")
            P.add("sp", lambda e, d=dst, o=zs[:, :, 0:n]: e.dma_start(out=d, in_=o),
                  reads=[("zst", i % 2)], stream="zo%d" % (i % 2))

        def gelu_gate(i):
            t0, n = tl[i]
            for cc in range(4):
                g = gs[:, cc, 0:n]
                t = tmp[0][:, 0:n]
                P.add("dve", lambda e, g=g, t=t: e.tensor_tensor(out=t, in0=g, in1=g, op=ALU.mult),
                      reads=[("gs", cc)], writes=["t0"])
                P.add("dve", lambda e, t=t: e.tensor_scalar(out=t, in0=t, scalar1=0.044715, scalar2=1.0,
                                                            op0=ALU.mult, op1=ALU.add),
                      reads=["t0"], writes=["t0"])
                P.add("dve", lambda e, g=g, t=t: e.tensor_tensor(out=t, in0=t, in1=g, op=ALU.mult),
                      reads=["t0", ("gs", cc)], writes=["t0"])
                P.add("act", lambda e, t=t: e.activation(out=t, in_=t, func=AF.Sigmoid, scale=GELU_C),
                      reads=["t0"], writes=["t0"])
                P.add("dve", lambda e, g=g, t=t, cc=cc: e.tensor_tensor(out=gt[:, cc, 0:n], in0=t, in1=g,
                                                                        op=ALU.mult),
                      reads=["t0", ("gs", cc)], writes=[("gt", cc)])

        def lru(i):
            t0, n = tl[i]
            is_s = (t0 >= S)
            rs = rst[i % 2]
            for cc in range(4):
                xc = tmp[1][:, 0:n]
                if is_s:
                    xc3 = xc.rearrange("p (b t) -> p b t", t=8)
                    uv = [ubs[:, cc, :, j:j + 8] for j in range(4)]
                    ukey = ("ubs", cc)
                else:
                    xc3 = xc
                    uv = [ubuf[:, cc, j:j + n] for j in range(4)]
                    ukey = ("ub", cc)
                P.add("dve", lambda e, cc=cc, xc3=xc3, uv=uv: e.tensor_scalar(
                    out=xc3, in0=uv[0], scalar1=cw[:, cc, 0:1], scalar2=cvec[:, 0, cc:cc + 1],
                    op0=ALU.mult, op1=ALU.add),
                    reads=[ukey, "cw", "cvec"], writes=["xc"])
                for j in range(1, 4):
                    P.add("dve", lambda e, cc=cc, j=j, xc3=xc3, uv=uv: e.scalar_tensor_tensor(
                        out=xc3, in0=uv[j], scalar=cw[:, cc, j:j + 1], in1=xc3, op0=ALU.mult, op1=ALU.add),
                        reads=[ukey, "cw", "xc"], writes=["xc"])
                if not is_s:
                    P.add("pool", lambda e, cc=cc: e.tensor_copy(ubuf[:, cc, 0:3], ubuf[:, cc, n:n + 3]),
                          reads=[ukey], writes=[ukey])
                P.add("act", lambda e, xc=xc: e.copy(xcb[:, 0:n], xc), reads=["xc"], writes=["xcb"])
                P.add("pe", lambda e, cc=cc: e.matmul(psA[:, 0:n], lhsT=WaBD[:, cc, :], rhs=xcb[:, 0:n],
                                                      start=True, stop=True),
                      reads=["WaBD", "xcb"], writes=["psA"])
                P.add("pe", lambda e, cc=cc: e.matmul(psX[:, 0:n], lhsT=WxBD[:, cc, :], rhs=xcb[:, 0:n],
                                                      start=True, stop=True),
                      reads=["WxBD", "xcb"], writes=["psX"])
                rg = tmp[2][:, 0:n]
                ig = tmp[3][:, 0:n]
                P.add("act", lambda e, cc=cc, rg=rg: e.activation(out=rg, in_=psA[:, 0:n], func=AF.Sigmoid,
                                                                  bias=cvec[:, 1, cc:cc + 1], scale=1.0),
                      reads=["psA", "cvec"], writes=["rg"])
                P.add("act", lambda e, cc=cc, ig=ig: e.activation(out=ig, in_=psX[:, 0:n], func=AF.Sigmoid,
                                                                  bias=cvec[:, 2, cc:cc + 1], scale=1.0),
                      reads=["psX", "cvec"], writes=["ig"])
                a = tmp[4][:, 0:n]
                th = tmp[5][:, 0:n]
                P.add("act", lambda e, cc=cc, a=a, rg=rg: e.activation(out=a, in_=rg, func=AF.Exp,
                                                                       scale=lamc[:, cc:cc + 1]),
                      reads=["rg", "lamc"], writes=["a"])
                P.add("act", lambda e, cc=cc, th=th, rg=rg: e.activation(out=th, in_=rg, func=AF.Tanh,
                                                                         scale=lamc[:, cc:cc + 1]),
                      reads=["rg", "lamc"], writes=["th"])
                om = tmp[6][:, 0:n]
                P.add("dve", lambda e, th=th, om=om: e.tensor_scalar(out=om, in0=th, scalar1=-1.0, scalar2=1.0,
                                                                     op0=ALU.mult, op1=ALU.add),
                      reads=["th"], writes=["om"])
                P.add("dve", lambda e, om=om: e.reciprocal(om, om), reads=["om"], writes=["om"])
                P.add("dve", lambda e, th=th, om=om: e.scalar_tensor_tensor(
                    out=om, in0=th, scalar=-2.0, in1=om, op0=ALU.mult, op1=ALU.mult),
                    reads=["th", "om"], writes=["om"])
                P.add("act", lambda e, om=om: e.activation(out=om, in_=om, func=AF.Sqrt), reads=["om"],
                      writes=["om"])
                bt = tmp[7][:, 0:n]
                P.add("dve", lambda e, om=om, ig=ig, bt=bt: e.tensor_tensor(out=bt, in0=om, in1=ig, op=ALU.mult),
                      reads=["om", "ig"], writes=["bt"])
                P.add("dve", lambda e, xc=xc, bt=bt: e.tensor_tensor(out=bt, in0=bt, in1=xc, op=ALU.mult),
                      reads=["bt", "xc"], writes=["bt"])
                hs = tmp[2][:, 0:n]
                if is_s:
                    a3 = a.rearrange("p (b t) -> p b t", t=8)
                    bt3 = bt.rearrange("p (b t) -> p b t", t=8)
                    t3 = tmp[3][:, 0:NB]
                    P.add("dve", lambda e, cc=cc, a3=a3, t3=t3: e.tensor_tensor(
                        out=t3, in0=a3[:, :, 0], in1=h0s[:, cc, :], op=ALU.mult),
                        reads=["a", "h0s", "bt"], writes=["ig"])
                    P.add("dve", lambda e, bt3=bt3, t3=t3: e.tensor_tensor(
                        out=bt3[:, :, 0], in0=bt3[:, :, 0], in1=t3, op=ALU.add),
                        reads=["ig", "bt"], writes=["bt"])
                    P.add("dve", lambda e, a3=a3: e.memset(a3[:, :, 0], 0.0), reads=["ig"], writes=["a"])
                    P.add("dve", lambda e, a=a, bt=bt, hs=hs: e.tensor_tensor_scan(
                        out=hs, data0=a, data1=bt, initial=0.0, op0=ALU.mult, op1=ALU.add),
                        reads=["a", "bt", "th"], writes=["rg"])
                    hs3 = hs.rearrange("p (b t) -> p b t", t=8)
                    P.add("dve", lambda e, cc=cc, hs3=hs3: e.tensor_copy(hfin[:, cc, :], hs3[:, :, 7]),
                          reads=["rg"], writes=["hfin"])
                else:
                    P.add("dve", lambda e, cc=cc, a=a, bt=bt, hs=hs: e.tensor_tensor_scan(
                        out=hs, data0=a, data1=bt, initial=hprev[:, cc:cc + 1], op0=ALU.mult, op1=ALU.add),
                        reads=["a", "bt", "hprev", "th"], writes=["rg"])
                    P.add("dve", lambda e, cc=cc, hs=hs: e.tensor_copy(hprev[:, cc:cc + 1], hs[:, n - 1:n]),
                          reads=["rg"], writes=["hprev"])
                P.add("dve", lambda e, cc=cc, hs=hs: e.tensor_tensor(out=rs[:, cc, 0:n], in0=hs,
                                                                     in1=gt[:, cc, 0:n], op=ALU.mult),
                      reads=["rg", ("gt", cc)], writes=[("rst", i % 2)])
            dst = rnnT[:, :, t0:t0 + n].rearrange("c p t -> p c t")
            P.add("sp", lambda e, d=dst, o=rs[:, :, 0:n]: e.dma_start(out=d, in_=o),
                  reads=[("rst", i % 2)], stream="ro%d" % (i % 2))
            if (not is_s) and i == nprompt - 1:
                P.add("sp", lambda e: e.dma_start(out=io["lru_h_p"], in_=hprev), reads=["hprev"], stream="o_hp")
            if is_s:
                P.add("sp", lambda e: e.dma_start(out=io["lru_h_s"], in_=hfin), reads=["hfin"], stream="o_hs")

        tk = [0]

        def tok_outputs(i):
            t0, n = tl[i]
            is_s = (t0 >= S)
            if not is_s and t0 < S - WBUF:
                return
            ns = n // 128
            for s in range(ns):
                tt = t0 + s * 128
                last = (not is_s) and (tt + 128 == S)
                sects = [0, 1] + ([2] if (is_s or last) else [])
                for sec in sects:
                    j = tk[0]
                    tk[0] += 1
                    pz = psZ[j % 4]
                    pk = ("psZ", j % 4)
                    c0 = 512 * (1 + sec)
                    for kc in range(8):
                        P.add("pe", lambda e, kc=kc, pz=pz, s=s, c0=c0: e.matmul(
                            pz[:, :], lhsT=self.xnT[:, kc, s * 128:(s + 1) * 128], rhs=Win[:, kc, c0:c0 + 512],
                            start=(kc == 0), stop=(kc == 7)),
                            reads=["Win", "xnT"], writes=[pk])
                    tb = tok[j % 2]
                    tkey = ("tok", j % 2)
                    P.add("act", lambda e, tb=tb, pz=pz: e.copy(tb, pz[:, :]), reads=[pk], writes=[tkey])
                    if is_s:
                        if sec < 2:
                            dst = (io["knew"], io["vnew"])[sec]
                            P.add("sp", lambda e, d=dst, tb=tb: e.dma_start(out=d, in_=tb), reads=[tkey],
                                  stream="tk%d" % (j % 2))
                            if sec == 1:
                                P.add("dve", lambda e, tb=tb: e.tensor_copy(tokb, tb), reads=[tkey],
                                      writes=["tokb"])
                                P.add("sp", lambda e: e.dma_start(out=vtok, in_=tokb), reads=["tokb"],
                                      stream="tkb")
                        else:
                            for b in range(NB):
                                P.add("sp", lambda e, b=b, tb=tb: e.dma_start(
                                    out=io["conv_s"][b], in_=tb[b * 8 + 5:b * 8 + 8, :]),
                                    reads=[tkey], stream="tk%d" % (j % 2))
                    else:
                        if sec < 2:
                            r0 = tt - (S - WBUF)
                            dst = (io["kwin"], io["vwin"])[sec][r0:r0 + 128, :]
                            P.add("sp", lambda e, d=dst, tb=tb: e.dma_start(out=d, in_=tb), reads=[tkey],
                                  stream="tk%d" % (j % 2))
                        else:
                            P.add("sp", lambda e, tb=tb: e.dma_start(out=io["conv_p"], in_=tb[125:128, :]),
                                  reads=[tkey], stream="tk%d" % (j % 2))

        self.t_load(x1, tl, 0)
        if nt > 1:
            self.t_load(x1, tl, 1)
        self.t_prep(tl, 0)
        self.t_transposes(tl, 0)
        for i in range(nt):
            zproj(i)
            tok_outputs(i)
            if i + 2 < nt:
                self.t_load(x1, tl, i + 2)
            if i + 1 < nt:
                self.t_prep(tl, i + 1)
                self.t_transposes(tl, i + 1)
            gelu_gate(i)
            lru(i)
        P.barrier()

    def attn_prompt_phase(self, zT, attnT, io):
        P = self.P
        S = self.S
        self.arena_off = self.arena_mark
        qT = self.alloc([128, S], BF16)
        kT = self.alloc([128, S], BF16)
        vT = self.alloc([128, S], BF16)
        acc = self.alloc([128, 2, S], F32)
        mask = self.alloc([128, 2, 256], BF16)
        onesf = self.alloc([128, 64], F32)
        pT = [self.alloc([128, 2, 256], BF16), self.alloc([128, 2, 256], BF16)]
        vblk = [self.alloc([128, 2, 65], BF16), self.alloc([128, 2, 65], BF16)]
        ast = [self.alloc([128, 512], BF16), self.alloc([128, 512], BF16)]
        psS2 = [self.pbig[:, 0:1024], self.pbig[:, 1024:2048]]
        pv4 = self.pb[4].bitcast(BF16)
        psV = [pv4[:, 0:128], pv4[:, 512:640]]
        psO = [self.pb[5], self.pb[6]]
        psB = [self.pb[7], self.pb[7]]
        P.add("sp", lambda e: e.dma_start(out=mask, in_=io["maskp"]), writes=["mask"], stream="c_mask")
        P.add("dve", lambda e: e.memset(onesf, 1.0), writes=["onesf"])
        for v in range(2):
            P.add("dve", lambda e, v=v: e.memset(vblk[v][:, :, 64:65], 1.0), writes=[("vblk", v)])
        it = [0]
        import os
        DBG = int(os.environ.get("ATTN_DBG", "9"))
        SKIP = os.environ.get("ATTN_SKIP", "").split(",")
        for c in range(int(os.environ.get("ATTN_C", "4"))):
            for nm, buf, ch in (("qT", qT, c), ("kT", kT, 4 + c), ("vT", vT, 8 + c)):
                for h0 in range(0, S, 4096):
                    h1 = min(S, h0 + 4096)
                    P.add("sp", lambda e, buf=buf, ch=ch, h0=h0, h1=h1: e.dma_start(
                        out=buf[:, h0:h1], in_=zT[ch, :, h0:h1]), writes=[nm], stream="ld_" + nm)
            if "accms" not in SKIP:
                P.add("pool", lambda e: e.memset(acc[0:65, :, :], 0.0), writes=["acc"])
            for d in [int(x) for x in os.environ.get("ATTN_D", "1,4,16").split(",")]:
                span = 128 * d
                nsb = S // span
                for nb in range(min(nsb, int(os.environ.get("ATTN_NB", "9999")))):
                    nq = 256 if nb + 1 < nsb else 128
                    for r in range(d):
                        base = nb * span + r
                        j = it[0]
                        it[0] += 1
                        pb = j % 2
                        ks = slice(base, base + d * 127 + 1, d)
                        qs = slice(base, base + d * (nq - 1) + 1, d)
                        for e2 in [int(x) for x in os.environ.get("ATTN_E", "0,1").split(",")]:
                            rows = slice(e2 * 64, (e2 + 1) * 64)
                            P.add("pe", lambda e, e2=e2, rows=rows, ks=ks, qs=qs, pb=pb, nq=nq: e.matmul(
                                psS2[pb][:, e2 * 512:e2 * 512 + nq], lhsT=kT[rows, ks], rhs=qT[rows, qs],
                                start=True, stop=True),
                                reads=["kT", "qT"], writes=[("psS", pb)])
                        if DBG >= 2:
                            P.add("pe", lambda e, ks=ks, pb=pb: e.transpose(
                                out=psV[pb], in_=vT[:, ks], identity=self.ident[:, :]),
                                reads=["vT", "ident"], writes=[("psV", pb)])
                        sv = psS2[pb].rearrange("p (e q) -> p e q", e=2)[:, :, 0:nq]
                        if "exp" in SKIP:
                            continue
                        P.add("act", lambda e, pb=pb, sv=sv, nq=nq: e.activation(
                            out=pT[pb][:, :, 0:nq], in_=sv, func=AF.Exp),
                            reads=[("psS", pb)], writes=[("pT", pb)])
                        if "mask" not in SKIP:
                            P.add("pool", lambda e, pb=pb, nq=nq: e.tensor_tensor(
                                out=pT[pb][:, :, 0:nq], in0=pT[pb][:, :, 0:nq], in1=mask[:, :, 0:nq], op=ALU.mult),
                                reads=[("pT", pb), "mask"], writes=[("pT", pb)])
                        if DBG >= 2:
                            P.add("act", lambda e, pb=pb: e.copy(
                                vblk[pb][:, :, 0:64], psV[pb].rearrange("p (e c) -> p e c", e=2)),
                                reads=[("psV", pb)], writes=[("vblk", pb)])
                        if DBG < 3:
                            continue
                        for e2 in range(2):
                            P.add("pe", lambda e, e2=e2, pb=pb, nq=nq: e.matmul(
                                psO[pb][0:65, e2 * 256:e2 * 256 + nq], lhsT=vblk[pb][:, e2, :],
                                rhs=pT[pb][:, e2, 0:nq], start=True, stop=True),
                                reads=[("vblk", pb), ("pT", pb)], writes=[("psO", pb)])
                        av = acc[0:65, :, qs]
                        ov = psO[pb][0:65, :].rearrange("p (e q) -> p e q", e=2)[:, :, 0:nq]
                        P.add("dve", lambda e, av=av, ov=ov: e.tensor_tensor(out=av, in0=ov, in1=av, op=ALU.add),
                              reads=[("psO", pb), "acc"], writes=["acc"])
            if DBG < 4:
                continue
            for e2 in range(2):
                P.add("dve", lambda e, e2=e2: e.reciprocal(acc[64:65, e2, :], acc[64:65, e2, :]),
                      reads=["acc"], writes=["acc"])
            k = 0
            for e2 in range(2):
                for p0 in range(0, S, 512):
                    pbb = k % 2
                    k += 1
                    P.add("pe", lambda e, e2=e2, p0=p0, pbb=pbb: e.matmul(
                        psB[pbb][0:64, :], lhsT=onesf[64:65, 0:64], rhs=acc[64:65, e2, p0:p0 + 512],
                        start=True, stop=True),
                        reads=["onesf", "acc"], writes=["psB"])
                    P.add("dve", lambda e, e2=e2, p0=p0, pbb=pbb: e.tensor_tensor(
                        out=ast[pbb][0:64, :], in0=acc[0:64, e2, p0:p0 + 512], in1=psB[pbb][0:64, :],
                        op=ALU.mult),
                        reads=["acc", "psB"], writes=[("ast", pbb)])
                    P.add("sp", lambda e, e2=e2, p0=p0, pbb=pbb, c=c: e.dma_start(
                        out=attnT[2 * c + e2, :, p0:p0 + 512], in_=ast[pbb][0:64, :]),
                        reads=[("ast", pbb)], stream="ao%d" % pbb)
        P.barrier()

    def attn_sample_phase(self, zT, vtok, attnT, io):
        P = self.P
        S, NB, NS = self.S, self.NB, self.NS
        self.arena_off = self.arena_mark
        NBLK = WBUF // 128
        stgk = [self.alloc([128, 8, 512], F32) for _ in range(4)]
        kb = self.alloc([128, NBLK, 512], BF16)
        kTs = self.alloc([128, 4, WBUF + 8], BF16)
        vb = [self.alloc([128, NBLK + 1, 8, 65], BF16), self.alloc([128, NBLK + 1, 8, 65], BF16)]
        qTs = self.alloc([128, 4, NS], BF16)
        kTn = self.alloc([128, 4, NS], BF16)
        vnb = self.alloc([128, 512], BF16)
        masks = self.alloc([128, NBLK + 1, 8, 8], BF16)
        pTs = [self.alloc([128, NBLK + 1, 8, 8], BF16), self.alloc([128, NBLK + 1, 8, 8], BF16)]
        accs = self.alloc([128, 8, NS], F32)
        onesf = self.alloc([128, 64], F32)
        ast = [self.alloc([128, NS], BF16), self.alloc([128, NS], BF16)]
        psK = [self.pb[0].bitcast(BF16), self.pb[1].bitcast(BF16)]
        psMain = [self.pb[2], self.pb[3]]
        psNew = [self.pb[4], self.pb[5]]
        psO = [self.pb[6][:, 0:64], self.pb[6][:, 256:320]]
        psB = self.pb[7]
        P.add("sp", lambda e: e.dma_start(out=masks, in_=io["masks"]), writes=["masks"], stream="c_masks")
        P.add("sp", lambda e: e.dma_start(out=qTs, in_=zT[0:4, :, S:S + NS].rearrange("c p t -> p c t")),
              writes=["qTs"], stream="c_qTs")
        P.add("sp", lambda e: e.dma_start(out=kTn, in_=zT[4:8, :, S:S + NS].rearrange("c p t -> p c t")),
              writes=["kTn"], stream="c_kTn")
        P.add("dve", lambda e: e.memset(onesf, 1.0), writes=["onesf"])
        for v in range(2):
            P.add("pool", lambda e, v=v: e.memset(vb[v][:, :, :, 64:65], 1.0), writes=[("vb", v)])
        sk = [0]

        def load_half(src, b, half):
            j = sk[0]
            sk[0] += 1
            slot = j % 4
            sv = src[b, half * 1024:(half + 1) * 1024, :].rearrange("(k p) d -> p k d", p=128)
            P.add("sp", lambda e, slot=slot, sv=sv: e.dma_start(out=stgk[slot], in_=sv),
                  writes=[("stgk", slot)], stream="sk%d" % slot)
            return slot

        def sview(blk, h):
            par, hh = h % 2, h // 2
            if blk < NBLK:
                off = blk * 32 + hh * 8
                return psMain[par][:, off:off + 8]
            return psNew[par][:, hh * 8:hh * 8 + 8]

        for b in range(NB):
            vv = vb[b % 2]
            vkey = ("vb", b % 2)
            for half in range(2):
                slot = load_half(io["kcache"], b, half)
                eng = "dve" if half == 0 else "pool"
                P.add(eng, lambda e, slot=slot, half=half: e.tensor_copy(kb[:, half * 8:(half + 1) * 8, :],
                                                                         stgk[slot]),
                      reads=[("stgk", slot)], writes=[("kb", half)])
            for half in range(2):
                slot = load_half(io["vcache"], b, half)
                dstv = vv[:, half * 8:(half + 1) * 8, :, 0:64]
                srcv = stgk[slot].rearrange("p k (h c) -> p k h c", h=8)
                if half == 0:
                    P.add("act", lambda e, dstv=dstv, srcv=srcv: e.copy(dstv, srcv),
                          reads=[("stgk", slot)], writes=[vkey])
                else:
                    P.add("dve", lambda e, dstv=dstv, srcv=srcv: e.tensor_copy(dstv, srcv),
                          reads=[("stgk", slot)], writes=[vkey])
            P.add("sp", lambda e, b=b: e.dma_start(out=vnb[0:8, :], in_=vtok[b * 8:(b + 1) * 8, :]),
                  writes=["vnb"], stream="vn")
            P.add("dve", lambda e, vv=vv: e.tensor_copy(vv[0:8, NBLK, :, 0:64],
                                                        vnb[0:8, :].rearrange("p (h c) -> p h c", h=8)),
                  reads=["vnb"], writes=[vkey])
            P.add("pool", lambda e, b=b: e.tensor_copy(kTs[:, :, WBUF:WBUF + 8], kTn[:, :, b * 8:(b + 1) * 8]),
                  reads=["kTn"], writes=["kTs"])
            tj = 0
            for blk in range(NBLK):
                pk = tj % 2
                tj += 1
                for c in range(4):
                    P.add("pe", lambda e, blk=blk, c=c, pk=pk: e.transpose(
                        out=psK[pk][:, c * 128:(c + 1) * 128], in_=kb[:, blk, c * 128:(c + 1) * 128],
                        identity=self.ident[:, :]),
                        reads=[("kb", blk // 8), "ident"], writes=[("psK", pk)])
                src = psK[pk][:, 0:512].rearrange("p (c k) -> p c k", c=4)
                dst = kTs[:, :, blk * 128:(blk + 1) * 128]
                if blk % 2 == 0:
                    P.add("act", lambda e, d=dst, s=src: e.copy(d, s), reads=[("psK", pk)], writes=["kTs"])
                else:
                    P.add("dve", lambda e, d=dst, s=src: e.tensor_copy(d, s), reads=[("psK", pk)],
                          writes=["kTs"])
            pp = pTs[b % 2]
            pkey = ("pTs", b % 2)
            for blk in range(NBLK + 1):
                nk = 128 if blk < NBLK else 8
                for h in range(8):
                    rows = slice((h % 2) * 64, (h % 2) * 64 + 64)
                    P.add("pe", lambda e, blk=blk, h=h, rows=rows, nk=nk, b=b: e.matmul(
                        sview(blk, h)[0:nk, :], lhsT=kTs[rows, h // 2, blk * 128:blk * 128 + nk],
                        rhs=qTs[rows, h // 2, b * 8:(b + 1) * 8], start=True, stop=True),
                        reads=["kTs", "qTs"], writes=["psSs"])
            for par in range(2):
                P.add("act", lambda e, par=par, pp=pp: e.activation(
                    out=pp[:, 0:NBLK, par::2, :],
                    in_=psMain[par].rearrange("p (k h t) -> p k h t", k=NBLK, h=4), func=AF.Exp),
                    reads=["psSs"], writes=[pkey])
                P.add("act", lambda e, par=par, pp=pp: e.activation(
                    out=pp[0:8, NBLK, par::2, :],
                    in_=psNew[par][0:8, 0:32].rearrange("p (h t) -> p h t", h=4), func=AF.Exp),
                    reads=["psSs"], writes=[pkey])
            P.add("pool", lambda e, pp=pp: e.tensor_tensor(out=pp[:, 0:NBLK, :, :], in0=pp[:, 0:NBLK, :, :],
                                                           in1=masks[:, 0:NBLK, :, :], op=ALU.mult),
                  reads=[pkey, "masks"], writes=[pkey])
            P.add("dve", lambda e, pp=pp: e.tensor_tensor(out=pp[0:8, NBLK, :, :], in0=pp[0:8, NBLK, :, :],
                                                          in1=masks[0:8, NBLK, :, :], op=ALU.mult),
                  reads=[pkey, "masks"], writes=[pkey])
            po = psO[b % 2]
            for h in range(8):
                for blk in range(NBLK + 1):
                    nk = 128 if blk < NBLK else 8
                    P.add("pe", lambda e, blk=blk, h=h, nk=nk, po=po, vv=vv, pp=pp: e.matmul(
                        po[0:65, h * 8:(h + 1) * 8], lhsT=vv[0:nk, blk, h, :], rhs=pp[0:nk, blk, h, :],
                        start=(blk == 0), stop=(blk == NBLK)),
                        reads=[vkey, pkey], writes=[("psOs", b % 2)])
            P.add("dve", lambda e, b=b, po=po: e.tensor_copy(
                accs[0:65, :, b * 8:(b + 1) * 8], po[0:65, :].rearrange("p (h t) -> p h t", h=8)),
                reads=[("psOs", b % 2)], writes=["accs"])
        for h in range(8):
            P.add("dve", lambda e, h=h: e.reciprocal(accs[64:65, h, :], accs[64:65, h, :]),
                  reads=["accs"], writes=["accs"])
            P.add("pe", lambda e, h=h: e.matmul(psB[0:64, 0:NS], lhsT=onesf[64:65, 0:64], rhs=accs[64:65, h, :],
                                                start=True, stop=True),
                  reads=["onesf", "accs"], writes=["psB"])
            P.add("dve", lambda e, h=h: e.tensor_tensor(out=ast[h % 2][0:64, :], in0=accs[0:64, h, :],
                                                        in1=psB[0:64, 0:NS], op=ALU.mult),
                  reads=["accs", "psB"], writes=[("ast", h % 2)])
            P.add("sp", lambda e, h=h: e.dma_start(out=attnT[h, :, S:S + NS], in_=ast[h % 2][0:64, :]),
                  reads=[("ast", h % 2)], stream="aso%d" % (h % 2))
        P.barrier()

    def mix_out_phase(self, x1, attnT, rnnT, x2, io):
        P = self.P
        self.arena_off = self.arena_mark
        self.common_tile_bufs()
        WoA = self.alloc([128, 8, D], BF16)
        WoR = self.alloc([128, 4, D], BF16)
        at = [self.alloc([128, 8, TT], BF16), self.alloc([128, 8, TT], BF16)]
        rt = [self.alloc([128, 4, TT], BF16), self.alloc([128, 4, TT], BF16)]
        psD = [self.pb[4], self.pb[5]]
        wo = io["w_out"]
        for h in range(8):
            self.load_cast(WoA[0:64, h, :], wo[h * 64:(h + 1) * 64, :], D, 1.0, "WoA", h)
        for cc in range(4):
            self.load_cast(WoR[:, cc, :], wo[512 + cc * 128:512 + (cc + 1) * 128, :], D, 1.0, "WoR", cc)
        tl = self.tiles()
        nt = len(tl)

        def load(i):
            t0, n = tl[i]
            slot = i % 2
            self.t_load(x1, tl, i)
            P.add("sp", lambda e: e.dma_start(out=at[slot][0:64, :, 0:n],
                                              in_=attnT[:, :, t0:t0 + n].rearrange("h c t -> c h t")),
                  writes=[("at", slot)], stream="at%d" % slot)
            P.add("sp", lambda e: e.dma_start(out=rt[slot][:, :, 0:n],
                                              in_=rnnT[:, :, t0:t0 + n].rearrange("c p t -> p c t")),
                  writes=[("rt", slot)], stream="rt%d" % slot)

        load(0)
        if nt > 1:
            load(1)
        j = 0
        for i in range(nt):
            t0, n = tl[i]
            ns = n // 128
            slot = i % 2
            xs = self.xbuf[slot]
            for s in range(ns):
                for hf in range(2):
                    pd = psD[j % 2]
                    pk = ("psD", j % 2)
                    j += 1
                    for h in range(8):
                        P.add("pe", lambda e, h=h, s=s, hf=hf, pd=pd, slot=slot: e.matmul(
                            pd[:, :], lhsT=at[slot][0:64, h, s * 128:(s + 1) * 128],
                            rhs=WoA[0:64, h, hf * 512:(hf + 1) * 512], start=(h == 0), stop=False),
                            reads=["WoA", ("at", slot)], writes=[pk])
                    for cc in range(4):
                        P.add("pe", lambda e, cc=cc, s=s, hf=hf, pd=pd, slot=slot: e.matmul(
                            pd[:, :], lhsT=rt[slot][:, cc, s * 128:(s + 1) * 128],
                            rhs=WoR[:, cc, hf * 512:(hf + 1) * 512], start=False, stop=(cc == 3)),
                            reads=["WoR", ("rt", slot)], writes=[pk])
                    P.add("dve", lambda e, s=s, hf=hf, pd=pd, xs=xs: e.tensor_tensor(
                        out=xs[:, s, hf * 512:(hf + 1) * 512], in0=pd[:, :],
                        in1=xs[:, s, hf * 512:(hf + 1) * 512], op=ALU.add),
                        reads=[pk, ("x", slot)], writes=[("x", slot)])
            dst = x2[t0:t0 + n, :].rearrange("(s p) d -> p s d", p=128)
            P.add("sp", lambda e, d=dst, o=xs[:, 0:ns, :]: e.dma_start(out=d, in_=o),
                  reads=[("x", slot)], stream="xo%d" % slot)
            if i + 2 < nt:
                load(i + 2)
        P.barrier()

    def build(self):
        nc = self.nc
        P = self.P
        NT, S, NB, NS = self.NT, self.S, self.NB, self.NS
        xin = self.dram_in("xin", [NT, D])
        w1g = self.dram_in("w1g", [D, DFF]); w1u = self.dram_in("w1u", [D, DFF]); w1d = self.dram_in("w1d", [DFF, D])
        w2g = self.dram_in("w2g", [D, DFF]); w2u = self.dram_in("w2u", [D, DFF]); w2d = self.dram_in("w2d", [DFF, D])
        gains_d = self.dram_in("gains", [128, 24])
        lnf_d = self.dram_in("lnf", [D])
        ident_d = self.dram_in("ident", [128, 128], BF16)
        io = {}
        io["w_in"] = self.dram_in("w_in", [D, INC])
        io["w_out"] = self.dram_in("w_out", [D, D])
        io["wabd"] = self.dram_in("wabd", [128, 4, 128])
        io["wxbd"] = self.dram_in("wxbd", [128, 4, 128])
        io["cw"] = self.dram_in("cw", [128, 4, 4])
        io["cvec"] = self.dram_in("cvec", [128, 4, 4])
        io["convst"] = self.dram_in("convst", [128, 4, NB, 3])
        io["h0"] = self.dram_in("h0", [128, 4, NB])
        io["kcache"] = self.dram_in("kcache", [NB, WBUF, 512])
        io["vcache"] = self.dram_in("vcache", [NB, WBUF, 512])
        io["maskp"] = self.dram_in("maskp", [128, 2, 256], BF16)
        io["masks"] = self.dram_in("masks", [128, 17, 8, 8], BF16)
        y = self.dram_out("y", [NT, D])
        io["kwin"] = self.dram_out("kwin", [WBUF, 512])
        io["vwin"] = self.dram_out("vwin", [WBUF, 512])
        io["lru_h_p"] = self.dram_out("lru_h_p", [128, 4])
        io["conv_p"] = self.dram_out("conv_p", [3, 512])
        io["knew"] = self.dram_out("knew", [NS, 512])
        io["vnew"] = self.dram_out("vnew", [NS, 512])
        io["lru_h_s"] = self.dram_out("lru_h_s", [128, 4, NB])
        io["conv_s"] = self.dram_out("conv_s", [NB, 3, 512])
        x1 = self.dram_tmp("x1", [NT, D], F32)
        x2 = self.dram_tmp("x2", [NT, D], F32)
        zT = self.dram_tmp("zT", [12, 128, NT], BF16)
        rnnT = self.dram_tmp("rnnT", [4, 128, NT], BF16)
        attnT = self.dram_tmp("attnT", [8, 64, NT], BF16)
        vtok = self.dram_tmp("vtok", [NS, 512], BF16)
        with ExitStack() as stack:
            self.stack = stack
            self.arena = stack.enter_context(nc.sbuf_tensor("arena", [128, ARENA_WORDS], F32))
            self.pbig = stack.enter_context(nc.psum_tensor("pbig", [128, 4096], F32))
            self.pb = [self.pbig[:, i * 512:(i + 1) * 512] for i in range(8)]
            self.arena_off = 0
            self.gains = self.alloc([128, 24], F32)
            self.lnf = self.alloc([128, D], F32)
            self.ident = self.alloc([128, 128], BF16)
            self.epsc = self.alloc([128, 1], F32)
            self.arena_mark = self.arena_off

            P.add("dve", lambda e: e.memset(self.epsc, EPS), writes=["epsc"])
            P.add("sp", lambda e: e.dma_start(out=self.gains, in_=gains_d), writes=["gains"], stream="c_gains")
            P.add("sp", lambda e: e.dma_start(out=self.lnf, in_=_bcast_rows(lnf_d, 128)), writes=["lnf"],
                  stream="c_lnf")
            P.add("sp", lambda e: e.dma_start(out=self.ident, in_=ident_d), writes=["ident"], stream="c_ident")
            P.barrier()

            src = xin
            if "ffn1" in self.phases:
                self.ffn_phase("f1", src, x1, w1g, w1u, w1d, 0, final_norm=False)
                src = x1
            ph = self.phases
            if "mix" in ph or "mix_in" in ph:
                self.mix_in_phase(src, zT, rnnT, vtok, io)
            if "mix" in ph or "attn_p" in ph:
                self.attn_prompt_phase(zT, attnT, io)
            if "mix" in ph or "attn_s" in ph:
                self.attn_sample_phase(zT, vtok, attnT, io)
            if "mix" in ph or "mix_out" in ph:
                self.mix_out_phase(src, attnT, rnnT, x2, io)
                src = x2
            if "ffn2" in self.phases:
                self.ffn_phase("f2", src, y, w2g, w2u, w2d, 2, final_norm=True)

            P.finalize_outputs()
            P.emit(nc, lambda name: stack.enter_context(nc.semaphore(name)))
        return nc


def _fm(v):
    return np.ascontiguousarray(np.asarray(v, np.float32).reshape(-1, 128).T)


def _blockdiag(w):
    out = np.zeros((128, 4, 128), np.float32)
    for c in range(4):
        out[0:64, c, 0:64] = w[2 * c]
        out[64:128, c, 64:128] = w[2 * c + 1]
    return out


def _mask_prompt():
    j = np.arange(128)[:, None]
    i = np.arange(128)[None, :]
    m = np.concatenate([(j <= i), (j >= i)], axis=1).astype(np.float32)
    return np.ascontiguousarray(np.broadcast_to(m[:, None, :], (128, 2, 256))).astype(ml_dtypes.bfloat16)


def _mask_sample():
    m = np.zeros((128, 17, 8, 8), np.float32)
    p = np.arange(128)
    for blk in range(17):
        for t in range(8):
            s = blk * 128 + p
            dd = WBUF + t - s
            mult = ((dd >= 0) & (dd <= 128)).astype(np.float32)
            mult += ((dd >= 0) & (dd % 4 == 0) & (dd <= 512))
            mult += ((dd >= 0) & (dd % 16 == 0) & (dd <= 2048))
            if blk == 16:
                mult = np.where(p < 8, mult, 0.0)
            m[:, blk, :, t] = mult[:, None]
    return m.astype(ml_dtypes.bfloat16)


def make_in_maps(inputs, n_cores, S, NB):
    f = lambda a: np.asarray(a, np.float32)
    xp, xs = f(inputs["x_prompt"]), f(inputs["x_sample"])
    shared = dict(
        w1g=f(inputs["w_ffn1_gate"])[0], w1u=f(inputs["w_ffn1_up"])[0], w1d=f(inputs["w_ffn1_down"])[0],
        w2g=f(inputs["w_ffn2_gate"])[0], w2u=f(inputs["w_ffn2_up"])[0], w2d=f(inputs["w_ffn2_down"])[0],
        gains=np.ascontiguousarray(np.concatenate(
            [_fm(inputs["ln_ffn1"][0]), _fm(inputs["ln_mix"][0]), _fm(inputs["ln_ffn2"][0])], axis=1)),
        lnf=f(inputs["ln_final"]),
        ident=np.eye(128, dtype=np.float32).astype(ml_dtypes.bfloat16),
        w_in=f(inputs["w_in"])[0], w_out=f(inputs["w_out"])[0],
        wabd=_blockdiag(f(inputs["w_gate_a"])[0]), wxbd=_blockdiag(f(inputs["w_gate_x"])[0]),
        cw=np.ascontiguousarray(f(inputs["conv_w"])[0].reshape(4, 4, 128).transpose(2, 1, 0)),
        cvec=np.ascontiguousarray(np.stack(
            [_fm(inputs["conv_b"][0]), _fm(f(inputs["b_gate_a"])[0].reshape(-1)),
             _fm(f(inputs["b_gate_x"])[0].reshape(-1)), _fm(inputs["lru_lambda"][0])], axis=1)),
        maskp=_mask_prompt(), masks=_mask_sample(),
    )
    maps = []
    for c in range(n_cores):
        bs = slice(c * NB, (c + 1) * NB)
        m = dict(shared)
        m["xin"] = np.ascontiguousarray(np.concatenate([xp[c, :S], xs[bs].reshape(NB * 8, D)], axis=0))
        cs = f(inputs["state_lru_conv"])[0, bs]
        m["convst"] = np.ascontiguousarray(cs.reshape(NB, 3, 4, 128).transpose(3, 2, 0, 1))
        hh = f(inputs["state_lru_h"])[0, bs]
        m["h0"] = np.ascontiguousarray(hh.reshape(NB, 4, 128).transpose(2, 1, 0))
        m["kcache"] = np.ascontiguousarray(f(inputs["cache_k_win"])[0, bs].reshape(NB, WBUF, 512))
        m["vcache"] = np.ascontiguousarray(f(inputs["cache_v_win"])[0, bs].reshape(NB, WBUF, 512))
        maps.append(m)
    return maps


def assemble(results, n_cores, S, NB):
    ys = [r["y"] for r in results]
    y_prompt = np.stack([y[:S] for y in ys])
    y_sample = np.concatenate([y[S:].reshape(NB, 8, D) for y in ys])
    kwin = np.stack([r["kwin"].reshape(WBUF, NH, HD) for r in results])[None]
    vwin = np.stack([r["vwin"].reshape(WBUF, NH, HD) for r in results])[None]
    hp = np.stack([r["lru_h_p"].T.reshape(DR) for r in results])[None]
    cp = np.stack([r["conv_p"] for r in results])[None]
    knew = np.concatenate([r["knew"].reshape(NB, 8, NH, HD) for r in results])[None]
    vnew = np.concatenate([r["vnew"].reshape(NB, 8, NH, HD) for r in results])[None]
    hs = np.concatenate([r["lru_h_s"].transpose(2, 1, 0).reshape(NB, DR) for r in results])[None]
    cs = np.concatenate([r["conv_s"] for r in results])[None]
    outs = (y_prompt, y_sample, kwin, vwin, hp, cp, knew, vnew, hs, cs)
    return tuple(np.ascontiguousarray(o, dtype=np.float32) for o in outs)


_NC_CACHE = {}


def kernel(**inputs):
    n_cores = 8
    S = inputs["x_prompt"].shape[1]
    NB = inputs["x_sample"].shape[0] // n_cores
    key = (S, NB)
    if key not in _NC_CACHE:
        _NC_CACHE[key] = Builder(S, NB).build()
    nc = _NC_CACHE[key]
    maps = make_in_maps(inputs, n_cores, S, NB)
    res = run_bass_kernel_spmd(nc, maps, core_ids=list(range(n_cores)))
    return assemble(res.results, n_cores, S, NB)
```

```python
from contextlib import ExitStack
import numpy as np
import ml_dtypes
import concourse.bass as bass
import concourse.mybir as mybir
from concourse.bass_utils import run_bass_kernel_spmd

F32 = mybir.dt.float32
BF16 = mybir.dt.bfloat16
AF = mybir.ActivationFunctionType
ALU = mybir.AluOpType

D = 1024
DFF = 2816
NFC = DFF // 128
NH = 8
HD = 64
DR = 512
INC = 2560
EPS = 1e-6
SEM_CAP = 30000
TT = 256
NSUB = TT // 128
WBUF = 2048
ARENA_WORDS = 51 * 1024
GELU_C = 1.5957691216057308


class _Op:
    __slots__ = ("eng", "fn", "reads", "writes", "stream", "is_dma", "deps", "signal", "sig")

    def __init__(self, eng, fn, reads, writes, stream):
        self.eng = eng
        self.fn = fn
        self.reads = tuple(reads)
        self.writes = tuple(writes)
        self.stream = stream
        self.is_dma = stream is not None
        self.deps = []
        self.signal = False
        self.sig = None


class Prog:
    ENGS = ("pe", "act", "dve", "pool", "sp")

    def __init__(self):
        self.ops = []
        self.res_w = {}
        self.res_r = {}
        self.bar_deps = []
        self.bar_pending = set()

    def barrier(self):
        last = {}
        for i, op in enumerate(self.ops):
            last[("dma", op.stream) if op.is_dma else ("eng", op.eng)] = i
        self.bar_deps = sorted(last.values())
        self.bar_pending = set(self.ENGS)
        self.res_w = {}
        self.res_r = {}

    def add(self, eng, fn, reads=(), writes=(), stream=None):
        op = _Op(eng, fn, reads, writes, stream)
        idx = len(self.ops)
        raw = set()
        other = set()
        for r in op.reads:
            raw.update(self.res_w.get(r, ()))
        for w in op.writes:
            other.update(self.res_w.get(w, ()))
            other.update(self.res_r.get(w, ()))
        deps = set()
        for d in raw | other:
            dop = self.ops[d]
            if dop.is_dma or op.is_dma or dop.eng != op.eng:
                deps.add(d)
            elif d in raw:
                deps.add(d)
        if eng in self.bar_pending:
            self.bar_pending.discard(eng)
            for d in self.bar_deps:
                dop = self.ops[d]
                if dop.is_dma or dop.eng != eng:
                    deps.add(d)
        op.deps = sorted(deps)
        for d in op.deps:
            self.ops[d].signal = True
        for r in op.reads:
            self.res_r.setdefault(r, []).append(idx)
        for w in op.writes:
            if self.res_r.get(w):
                self.res_w[w] = [idx]
                self.res_r[w] = []
            else:
                self.res_w.setdefault(w, []).append(idx)
        self.ops.append(op)
        return idx

    def finalize_outputs(self):
        last = {}
        for i, op in enumerate(self.ops):
            if op.is_dma:
                last[op.stream] = i
        for op in self.ops:
            if op.is_dma:
                op.signal = True
        self.final_dma = sorted(last.values())

    def emit(self, nc, sem_ctx):
        cnt = {}
        for op in self.ops:
            if not op.signal:
                continue
            key = ("dma", op.stream) if op.is_dma else ("eng", op.eng)
            c = cnt.get(key, 0)
            epoch, within = divmod(c, SEM_CAP if not op.is_dma else SEM_CAP // 16)
            cnt[key] = c + 1
            semname = "%s_%s_%d" % (key[0], key[1], epoch)
            val = (within + 1) * (16 if op.is_dma else 1)
            op.sig = (semname, val)
        per_eng = {e: [] for e in self.ENGS}
        for i, op in enumerate(self.ops):
            per_eng[op.eng].append(i)
        sems = {}

        def sem(name):
            if name not in sems:
                sems[name] = sem_ctx(name)
            return sems[name]

        for op in self.ops:
            if op.sig is not None:
                sem(op.sig[0])
        final_dma = self.final_dma

        with nc.Block() as block:
            def body(ename):
                def _run(engine):
                    seen = {}

                    def wait(i):
                        sname, val = self.ops[i].sig
                        if seen.get(sname, 0) >= val:
                            return
                        seen[sname] = val
                        engine.wait_ge(sem(sname), val)

                    for i in per_eng[ename]:
                        op = self.ops[i]
                        for d in op.deps:
                            wait(d)
                        ins = op.fn(engine)
                        if op.sig is not None:
                            ins.then_inc(sem(op.sig[0]), 16 if op.is_dma else 1)
                    if ename == "sp":
                        for i in final_dma:
                            wait(i)
                return _run

            block.tensor(body("pe"))
            block.scalar(body("act"))
            block.vector(body("dve"))
            block.gpsimd(body("pool"))
            block.sync(body("sp"))


def _bcast_rows(ap1d, nparts):
    n = ap1d.shape[-1]
    return bass.AP(ap1d.tensor, ap1d.offset, [[0, nparts], [1, n]])


class Builder:
    def __init__(self, S, NB, phases=("ffn1", "mix", "ffn2")):
        assert S % 2048 == 0 and (NB * 8) % 128 == 0
        self.S = S
        self.NB = NB
        self.NS = NB * 8
        self.NT = S + self.NS
        self.phases = phases
        self.nc = bass.Bass("TRN2", target_bir_lowering=False)
        self.P = Prog()
        self.stack = None
        self._stg_i = 0

    def dram_in(self, name, shape, dt=F32):
        return self.nc.dram_tensor(name, list(shape), dt, kind="ExternalInput").ap()

    def dram_out(self, name, shape, dt=F32):
        return self.nc.dram_tensor(name, list(shape), dt, kind="ExternalOutput").ap()

    def dram_tmp(self, name, shape, dt):
        return self.nc.dram_tensor(name, list(shape), dt, kind="Internal").ap()

    def alloc(self, shape, dt):
        esz = 4 if dt == F32 else 2
        n = 1
        for s in shape[1:]:
            n *= s
        nbytes = (n * esz + 31) // 32 * 32
        w0 = self.arena_off
        nw = nbytes // 4
        assert w0 + nw <= ARENA_WORDS, "SBUF arena overflow: %d" % (w0 + nw)
        self.arena_off = w0 + nw
        v = self.arena[:, w0:w0 + nw]
        if dt != F32:
            v = v.bitcast(dt)
        v = v[:, 0:n]
        if len(shape) == 3:
            v = v.rearrange("p (a b) -> p a b", a=shape[1])
        elif len(shape) == 4:
            v = v.rearrange("p (a b c) -> p a b c", a=shape[1], b=shape[2])
        return v

    def tiles(self):
        out = []
        t = 0
        while t < self.S:
            n = min(TT, self.S - t)
            out.append((t, n))
            t += n
        if self.NS:
            out.append((self.S, self.NS))
        return out

    def load_cast(self, dst_ap, src_ap, width, scale, key, idx):
        P = self.P
        slot = self._stg_i % 2
        self._stg_i += 1
        np_ = dst_ap.shape[0]
        view = self.stg[slot][0:np_, 0:width]
        if len(dst_ap.shape) == 3:
            view = view.rearrange("p (a b) -> p a b", a=dst_ap.shape[1])
        P.add("sp", lambda e, v=view, s=src_ap: e.dma_start(out=v, in_=s),
              writes=[("stg", slot)], stream="stg%d" % slot)
        rd = [("stg", slot)] + ([] if isinstance(scale, float) else ["gains"])
        if idx % 2 == 1:
            P.add("act", lambda e, o=dst_ap, v=view, sc=scale: e.mul(o, v, sc), reads=rd, writes=[key])
        else:
            P.add("dve", lambda e, o=dst_ap, v=view, sc=scale: e.tensor_scalar_mul(o, v, sc), reads=rd, writes=[key])

    def t_load(self, x_src, tl, i):
        t0, n = tl[i]
        ns = n // 128
        slot = i % 2
        src = x_src[t0:t0 + n, :].rearrange("(s p) d -> p s d", p=128)
        self.P.add("sp", lambda e, o=self.xbuf[slot][:, 0:ns, :], s=src: e.dma_start(out=o, in_=s),
                   writes=[("x", slot)], stream="x%d" % slot)

    def t_prep(self, tl, i):
        P = self.P
        t0, n = tl[i]
        ns = n // 128
        slot = i % 2
        xs = self.xbuf[slot]
        for s in range(ns):
            P.add("act", lambda e, s=s, xs=xs: e.activation(
                out=self.junk[:, :], in_=xs[:, s, :], func=AF.Square, scale=1.0 / 32.0,
                accum_out=self.ss[:, s:s + 1]),
                reads=[("x", slot)], writes=["junk", "ss"])
        P.add("act", lambda e: e.activation(out=self.rstd[:, 0:ns], in_=self.ss[:, 0:ns], func=AF.Sqrt,
                                            bias=self.epsc[:, 0:1], scale=1.0),
              reads=["ss", "epsc"], writes=["rstd"])
        P.add("dve", lambda e: e.reciprocal(self.rstd[:, 0:ns], self.rstd[:, 0:ns]),
              reads=["rstd"], writes=["rstd"])
        for s in range(ns):
            if s % 2 == 0:
                P.add("dve", lambda e, s=s, xs=xs: e.tensor_scalar_mul(self.xn[:, s, :], xs[:, s, :],
                                                                       self.rstd[:, s:s + 1]),
                      reads=[("x", slot), "rstd"], writes=[("xn", s)])
            else:
                P.add("act", lambda e, s=s, xs=xs: e.mul(self.xn[:, s, :], xs[:, s, :], self.rstd[:, s:s + 1]),
                      reads=[("x", slot), "rstd"], writes=[("xn", s)])

    def t_transposes(self, tl, i):
        P = self.P
        t0, n = tl[i]
        ns = n // 128
        for s in range(ns):
            bank = self.psT[s % 2]
            for kc in range(8):
                P.add("pe", lambda e, s=s, kc=kc, bank=bank: e.transpose(
                    out=bank[:, kc * 128:(kc + 1) * 128], in_=self.xn[:, s, kc * 128:(kc + 1) * 128],
                    identity=self.ident[:, :]),
                    reads=[("xn", s), "ident"], writes=[("psT", s % 2)])
            src = bank[:, :].rearrange("p (k t) -> p k t", k=8)
            dst = self.xnT[:, :, s * 128:(s + 1) * 128]
            if s % 2 == 0:
                P.add("act", lambda e, d=dst, sr=src: e.copy(d, sr), reads=[("psT", s % 2)], writes=["xnT"])
            else:
                P.add("dve", lambda e, d=dst, sr=src: e.tensor_copy(d, sr), reads=[("psT", s % 2)],
                      writes=["xnT"])

    def common_tile_bufs(self):
        self.xbuf = [self.alloc([128, NSUB, D], F32), self.alloc([128, NSUB, D], F32)]
        self.xn = self.alloc([128, NSUB, D], BF16)
        self.xnT = self.alloc([128, 8, TT], BF16)
        self.junk = self.alloc([128, D], BF16)
        self.ss = self.alloc([128, 4], F32)
        self.rstd = self.alloc([128, 4], F32)
        self.ss2 = self.alloc([128, 4], F32)
        self.rstd2 = self.alloc([128, 4], F32)
        self.stg = [self.alloc([128, DFF // 2], F32), self.alloc([128, DFF // 2], F32)]

    def ffn_phase(self, tag, x_src, x_dst, wg, wu, wd, gcol, final_norm):
        P = self.P
        self.arena_off = self.arena_mark
        self.common_tile_bufs()
        self.wA = self.alloc([128, 8, DFF], BF16)
        self.wB = self.alloc([128, 8, DFF], BF16)
        self.wC = self.alloc([128, NFC, D], BF16)
        self.aT = self.alloc([128, NFC, TT], BF16)
        self.sg = [self.alloc([128, TT], F32), self.alloc([128, TT], F32)]
        psG = [self.pb[0], self.pb[1]]
        psU = [self.pb[2], self.pb[3]]
        psD = [self.pb[4], self.pb[5]]
        self.psT = [self.pb[6].bitcast(BF16), self.pb[7].bitcast(BF16)]
        HW = DFF // 2
        k = 0
        for kc in range(8):
            for hh in range(2):
                self.load_cast(self.wA[:, kc, hh * HW:(hh + 1) * HW],
                               wg[kc * 128:(kc + 1) * 128, hh * HW:(hh + 1) * HW],
                               HW, self.gains[:, gcol * 8 + kc:gcol * 8 + kc + 1], "wA", k)
                k += 1
        for kc in range(8):
            for hh in range(2):
                self.load_cast(self.wB[:, kc, hh * HW:(hh + 1) * HW],
                               wu[kc * 128:(kc + 1) * 128, hh * HW:(hh + 1) * HW],
                               HW, self.gains[:, gcol * 8 + kc:gcol * 8 + kc + 1], "wB", k)
                k += 1
        for fc in range(NFC):
            self.load_cast(self.wC[:, fc, :], wd[fc * 128:(fc + 1) * 128, :], D, 0.5, "wC", k)
            k += 1
        tl = self.tiles()
        xbuf = self.xbuf

        def gate_up(i):
            t0, n = tl[i]
            for fc in range(NFC):
                pb = fc % 2
                pg, pu = psG[pb], psU[pb]
                for kc in range(8):
                    P.add("pe", lambda e, fc=fc, kc=kc, pg=pg: e.matmul(
                        pg[:, 0:n], lhsT=self.wA[:, kc, fc * 128:(fc + 1) * 128], rhs=self.xnT[:, kc, 0:n],
                        start=(kc == 0), stop=(kc == 7)),
                        reads=["wA", "xnT"], writes=[("psG", pb)])
                for kc in range(8):
                    P.add("pe", lambda e, fc=fc, kc=kc, pu=pu: e.matmul(
                        pu[:, 0:n], lhsT=self.wB[:, kc, fc * 128:(fc + 1) * 128], rhs=self.xnT[:, kc, 0:n],
                        start=(kc == 0), stop=(kc == 7)),
                        reads=["wB", "xnT"], writes=[("psU", pb)])
                sg = self.sg[pb]
                P.add("act", lambda e, pg=pg, sg=sg: e.activation(out=sg[:, 0:n], in_=pg[:, 0:n], func=AF.Silu),
                      reads=[("psG", pb)], writes=[("sg", pb)])
                P.add("dve", lambda e, fc=fc, pu=pu, sg=sg: e.tensor_tensor(
                    out=self.aT[:, fc, 0:n], in0=sg[:, 0:n], in1=pu[:, 0:n], op=ALU.mult),
                    reads=[("psU", pb), ("sg", pb)], writes=[("aT", fc)])

        def down(i):
            t0, n = tl[i]
            ns = n // 128
            slot = i % 2
            xs = xbuf[slot]
            j = 0
            for s in range(ns):
                for h in range(2):
                    pb = j % 2
                    j += 1
                    pd = psD[pb]
                    for fc in range(NFC):
                        P.add("pe", lambda e, fc=fc, s=s, h=h, pd=pd: e.matmul(
                            pd[:, :], lhsT=self.aT[:, fc, s * 128:(s + 1) * 128],
                            rhs=self.wC[:, fc, h * 512:(h + 1) * 512], start=(fc == 0), stop=(fc == NFC - 1)),
                            reads=["wC", ("aT", fc)], writes=[("psD", pb)])
                    P.add("dve", lambda e, s=s, h=h, pd=pd, xs=xs: e.tensor_tensor(
                        out=xs[:, s, h * 512:(h + 1) * 512], in0=pd[:, :], in1=xs[:, s, h * 512:(h + 1) * 512],
                        op=ALU.add),
                        reads=[("psD", pb), ("x", slot)], writes=[("x", slot)])
            if final_norm:
                for s in range(ns):
                    P.add("act", lambda e, s=s, xs=xs: e.activation(
                        out=self.junk[:, :], in_=xs[:, s, :], func=AF.Square, scale=1.0 / 32.0,
                        accum_out=self.ss2[:, s:s + 1]),
                        reads=[("x", slot)], writes=["junk", "ss2"])
                P.add("act", lambda e: e.activation(out=self.rstd2[:, 0:ns], in_=self.ss2[:, 0:ns], func=AF.Sqrt,
                                                    bias=self.epsc[:, 0:1], scale=1.0),
                      reads=["ss2", "epsc"], writes=["rstd2"])
                P.add("dve", lambda e: e.reciprocal(self.rstd2[:, 0:ns], self.rstd2[:, 0:ns]),
                      reads=["rstd2"], writes=["rstd2"])
                for s in range(ns):
                    P.add("dve", lambda e, s=s, xs=xs: e.scalar_tensor_tensor(
                        out=xs[:, s, :], in0=xs[:, s, :], scalar=self.rstd2[:, s:s + 1], in1=self.lnf[:, :],
                        op0=ALU.mult, op1=ALU.mult),
                        reads=[("x", slot), "rstd2", "lnf"], writes=[("x", slot)])
            dst = x_dst[t0:t0 + n, :].rearrange("(s p) d -> p s d", p=128)
            P.add("sp", lambda e, d=dst, o=xbuf[slot][:, 0:ns, :]: e.dma_start(out=d, in_=o),
                  reads=[("x", slot)], stream="xo%d" % slot)

        nt = len(tl)
        self.t_load(x_src, tl, 0)
        if nt > 1:
            self.t_load(x_src, tl, 1)
        self.t_prep(tl, 0)
        self.t_transposes(tl, 0)
        for i in range(nt):
            gate_up(i)
            if i + 1 < nt:
                self.t_prep(tl, i + 1)
            down(i)
            if i + 2 < nt:
                self.t_load(x_src, tl, i + 2)
            if i + 1 < nt:
                self.t_transposes(tl, i + 1)
        P.barrier()

    def mix_in_phase(self, x1, zT, rnnT, vtok, io):
        P = self.P
        S, NB, NS = self.S, self.NB, self.NS
        self.arena_off = self.arena_mark
        self.common_tile_bufs()
        Win = self.alloc([128, 8, INC], BF16)
        WaBD = self.alloc([128, 4, 128], BF16)
        WxBD = self.alloc([128, 4, 128], BF16)
        cw = self.alloc([128, 4, 4], F32)
        cvec = self.alloc([128, 4, 4], F32)
        lamc = self.alloc([128, 4], F32)
        onec = self.alloc([128, 1], F32)
        zst = [self.alloc([128, 12, TT], BF16), self.alloc([128, 12, TT], BF16)]
        ubuf = self.alloc([128, 4, TT + 3], F32)
        ubs = self.alloc([128, 4, NB, 11], F32)
        h0s = self.alloc([128, 4, NB], F32)
        hprev = self.alloc([128, 4], F32)
        hfin = self.alloc([128, 4, NB], F32)
        gs = self.alloc([128, 4, TT], F32)
        gt = self.alloc([128, 4, TT], F32)
        B = [self.alloc([128, 4, TT], F32) for _ in range(8)]
        xcb4 = self.alloc([128, 4, TT], BF16)
        rst = [self.alloc([128, 4, TT], BF16), self.alloc([128, 4, TT], BF16)]
        tok = [self.alloc([128, 512], F32), self.alloc([128, 512], F32)]
        tokb = self.alloc([128, 512], BF16)
        psZ = [self.pb[0], self.pb[1]]
        psA4 = [self.pb[2][:, 0:256], self.pb[2][:, 256:512], self.pb[3][:, 0:256], self.pb[3][:, 256:512]]
        psX4 = [self.pb[4][:, 0:256], self.pb[4][:, 256:512], self.pb[5][:, 0:256], self.pb[5][:, 256:512]]
        self.psT = [self.pb[6].bitcast(BF16), self.pb[7].bitcast(BF16)]
        NZ = len(psZ)

        k = 0
        for kc in range(8):
            for hh in range(2):
                self.load_cast(Win[:, kc, hh * 1280:(hh + 1) * 1280],
                               io["w_in"][kc * 128:(kc + 1) * 128, hh * 1280:(hh + 1) * 1280],
                               1280, self.gains[:, 8 + kc:8 + kc + 1], "Win", k)
                k += 1
        self.load_cast(WaBD, io["wabd"], 512, 1.0, "WaBD", 0)
        self.load_cast(WxBD, io["wxbd"], 512, 1.0, "WxBD", 1)
        P.add("sp", lambda e: e.dma_start(out=cw, in_=io["cw"]), writes=["cw"], stream="c_cw")
        P.add("sp", lambda e: e.dma_start(out=cvec, in_=io["cvec"]), writes=["cvec"], stream="c_cvec")
        P.add("sp", lambda e: e.dma_start(out=ubs[:, :, :, 0:3], in_=io["convst"]), writes=["ubs"], stream="c_ubs")
        P.add("sp", lambda e: e.dma_start(out=h0s, in_=io["h0"]), writes=["h0s"], stream="c_h0s")
        P.add("dve", lambda e: e.memset(onec, 1.0), writes=["onec"])
        P.add("dve", lambda e: e.memset(hprev, 0.0), writes=["hprev"])
        P.add("dve", lambda e: e.memset(ubuf[:, :, 0:3], 0.0), writes=[("ub", c) for c in range(4)])
        P.add("act", lambda e: e.activation(out=lamc, in_=cvec[:, 3, :], func=AF.Exp, scale=-1.0),
              reads=["cvec"], writes=["lamc"])
        P.add("act", lambda e: e.activation(out=lamc, in_=lamc, func=AF.Ln, bias=onec[:, 0:1], scale=1.0),
              reads=["lamc", "onec"], writes=["lamc"])
        P.add("dve", lambda e: e.tensor_scalar_mul(lamc, lamc, -8.0), reads=["lamc"], writes=["lamc"])

        tl = self.tiles()
        nt = len(tl)
        nprompt = nt - 1 if NS else nt

        def zproj(i, pending):
            t0, n = tl[i]
            is_s = (t0 >= S)
            zs = zst[i % 2]
            j = 0
            pending = list(pending)
            for c in range(20):
                if c >= 1 and pending:
                    pending.pop(0)()
                pz = psZ[j % NZ]
                pk = ("psZ", j % NZ)
                j += 1
                for kc in range(8):
                    P.add("pe", lambda e, c=c, kc=kc, pz=pz: e.matmul(
                        pz[:, 0:n], lhsT=Win[:, kc, c * 128:(c + 1) * 128], rhs=self.xnT[:, kc, 0:n],
                        start=(kc == 0), stop=(kc == 7)),
                        reads=["Win", "xnT"], writes=[pk])
                if c < 4:
                    P.add("act", lambda e, c=c, pz=pz: e.mul(zs[:, c, 0:n], pz[:, 0:n], 0.125),
                          reads=[pk], writes=[("zst", i % 2)])
                elif c < 12:
                    if c % 2 == 0:
                        P.add("act", lambda e, c=c, pz=pz: e.copy(zs[:, c, 0:n], pz[:, 0:n]),
                              reads=[pk], writes=[("zst", i % 2)])
                    else:
                        P.add("dve", lambda e, c=c, pz=pz: e.tensor_copy(zs[:, c, 0:n], pz[:, 0:n]),
                              reads=[pk], writes=[("zst", i % 2)])
                elif c < 16:
                    cc = c - 12
                    if is_s:
                        P.add("act", lambda e, cc=cc, pz=pz: e.copy(
                            ubs[:, cc, :, 3:11], pz[:, 0:n].rearrange("p (b t) -> p b t", t=8)),
                            reads=[pk], writes=[("ubs", cc)])
                    else:
                        P.add("act", lambda e, cc=cc, pz=pz: e.copy(ubuf[:, cc, 3:3 + n], pz[:, 0:n]),
                              reads=[pk], writes=[("ub", cc)])
                else:
                    cc = c - 16
                    P.add("act", lambda e, cc=cc, pz=pz: e.copy(gs[:, cc, 0:n], pz[:, 0:n]),
                          reads=[pk], writes=[("gs", cc)])
            dst = zT[:, :, t0:t0 + n].rearrange("c p t -> p c t")
            P.add("sp", lambda e, d=dst, o=zs[:, :, 0:n]: e.dma_start(out=d, in_=o),
                  reads=[("zst", i % 2)], stream="zo%d" % (i % 2))
            for f in pending:
                f()

        def gelu_gate(i):
            t0, n = tl[i]
            g = gs[:, :, 0:n]
            t = B[0][:, :, 0:n]
            P.add("dve", lambda e: e.tensor_tensor(out=t, in0=g, in1=g, op=ALU.mult),
                  reads=[("gs", c) for c in range(4)], writes=["t0"])
            P.add("dve", lambda e: e.tensor_scalar(out=t, in0=t, scalar1=0.044715, scalar2=1.0,
                                                   op0=ALU.mult, op1=ALU.add),
                  reads=["t0"], writes=["t0"])
            P.add("dve", lambda e: e.tensor_tensor(out=t, in0=t, in1=g, op=ALU.mult),
                  reads=["t0"] + [("gs", c) for c in range(4)], writes=["t0"])
            P.add("act", lambda e: e.activation(out=t, in_=t, func=AF.Sigmoid, scale=GELU_C),
                  reads=["t0"], writes=["t0"])
            P.add("dve", lambda e: e.tensor_tensor(out=gt[:, :, 0:n], in0=t, in1=g, op=ALU.mult),
                  reads=["t0"] + [("gs", c) for c in range(4)], writes=["gt"])

        def lru_prompt_steps(i):
            t0, n = tl[i]
            rs = rst[i % 2]
            XC, RG, IG, A, OM, BT, G0 = B[1], B[2], B[3], B[4], B[6], B[7], B[0]
            allxc = [("xc", c) for c in range(4)]
            allub = [("ub", c) for c in range(4)]
            allgs = [("gs", c) for c in range(4)]
            allrg = [("rg", c) for c in range(4)]
            g = gs[:, :, 0:n]
            t = G0[:, :, 0:n]
            om = OM[:, :, 0:n]
            bt = BT[:, :, 0:n]
            steps = []

            def conv0():
                for cc in range(4):
                    P.add("dve", lambda e, cc=cc: e.tensor_scalar(
                        out=XC[:, cc, 0:n], in0=ubuf[:, cc, 0:n], scalar1=cw[:, cc, 0:1],
                        scalar2=cvec[:, 0, cc:cc + 1], op0=ALU.mult, op1=ALU.add),
                        reads=[("ub", cc), "cw", "cvec"], writes=[("xc", cc)])
            steps.append(conv0)

            def convj(j):
                def f():
                    for cc in range(4):
                        P.add("dve", lambda e, cc=cc: e.scalar_tensor_tensor(
                            out=XC[:, cc, 0:n], in0=ubuf[:, cc, j:j + n], scalar=cw[:, cc, j:j + 1],
                            in1=XC[:, cc, 0:n], op0=ALU.mult, op1=ALU.add),
                            reads=[("ub", cc), "cw", ("xc", cc)], writes=[("xc", cc)])
                return f
            for j in range(1, 4):
                steps.append(convj(j))

            def cast():
                P.add("pool", lambda e: e.tensor_copy(ubuf[:, :, 0:3], ubuf[:, :, n:n + 3]), reads=allub,
                      writes=allub)
                P.add("act", lambda e: e.copy(xcb4[:, :, 0:n], XC[:, :, 0:n]), reads=allxc, writes=["xcb"])
            steps.append(cast)

            def gel1():
                P.add("dve", lambda e: e.tensor_tensor(out=t, in0=g, in1=g, op=ALU.mult), reads=allgs, writes=["t0"])
                P.add("dve", lambda e: e.tensor_scalar(out=t, in0=t, scalar1=0.044715, scalar2=1.0,
                                                       op0=ALU.mult, op1=ALU.add), reads=["t0"], writes=["t0"])
                P.add("dve", lambda e: e.tensor_tensor(out=t, in0=t, in1=g, op=ALU.mult), reads=["t0"] + allgs,
                      writes=["t0"])
            steps.append(gel1)

            def gates():
                for cc in range(4):
                    P.add("pe", lambda e, cc=cc: e.matmul(psA4[cc][:, 0:n], lhsT=WaBD[:, cc, :],
                                                          rhs=xcb4[:, cc, 0:n], start=True, stop=True),
                          reads=["WaBD", "xcb"], writes=[("psA", cc // 2)])
                    P.add("pe", lambda e, cc=cc: e.matmul(psX4[cc][:, 0:n], lhsT=WxBD[:, cc, :],
                                                          rhs=xcb4[:, cc, 0:n], start=True, stop=True),
                          reads=["WxBD", "xcb"], writes=[("psX", cc // 2)])
            steps.append(gates)

            def sig1():
                P.add("act", lambda e: e.activation(out=t, in_=t, func=AF.Sigmoid, scale=GELU_C),
                      reads=["t0"], writes=["t0"])
                for cc in range(4):
                    P.add("act", lambda e, cc=cc: e.activation(
                        out=RG[:, cc, 0:n], in_=psA4[cc][:, 0:n], func=AF.Sigmoid,
                        bias=cvec[:, 1, cc:cc + 1], scale=1.0),
                        reads=[("psA", cc // 2), "cvec"], writes=[("rg", cc)])
            steps.append(sig1)

            def sig2():
                for cc in range(4):
                    P.add("act", lambda e, cc=cc: e.activation(
                        out=IG[:, cc, 0:n], in_=psX4[cc][:, 0:n], func=AF.Sigmoid,
                        bias=cvec[:, 2, cc:cc + 1], scale=1.0),
                        reads=[("psX", cc // 2), "cvec"], writes=[("ig", cc)])
                P.add("dve", lambda e: e.tensor_tensor(out=gt[:, :, 0:n], in0=t, in1=g, op=ALU.mult),
                      reads=["t0"] + allgs, writes=["gt"])
            steps.append(sig2)

            def expa():
                for cc in range(4):
                    P.add("act", lambda e, cc=cc: e.activation(out=A[:, cc, 0:n], in_=RG[:, cc, 0:n], func=AF.Exp,
                                                               scale=lamc[:, cc:cc + 1]),
                          reads=[("rg", cc), "lamc"], writes=[("a", cc)])
            steps.append(expa)

            def om1():
                alla = [("a", c) for c in range(4)]
                P.add("dve", lambda e: e.tensor_tensor(out=om, in0=A[:, :, 0:n], in1=A[:, :, 0:n], op=ALU.mult),
                      reads=alla, writes=["om"])
                P.add("dve", lambda e: e.tensor_scalar(out=om, in0=om, scalar1=-1.0, scalar2=1.0,
                                                       op0=ALU.mult, op1=ALU.add), reads=["om"], writes=["om"])
                P.add("act", lambda e: e.activation(out=om, in_=om, func=AF.Sqrt), reads=["om"], writes=["om"])
                P.add("dve", lambda e: e.tensor_tensor(out=bt, in0=IG[:, :, 0:n], in1=XC[:, :, 0:n], op=ALU.mult),
                      reads=[("ig", c) for c in range(4)] + allxc, writes=["bt"])
            steps.append(om1)

            def bt2():
                P.add("dve", lambda e: e.tensor_tensor(out=bt, in0=bt, in1=om, op=ALU.mult),
                      reads=["bt", "om"], writes=["bt"])
            steps.append(bt2)

            def scans():
                for cc in range(4):
                    P.add("dve", lambda e, cc=cc: e.tensor_tensor_scan(
                        out=RG[:, cc, 0:n], data0=A[:, cc, 0:n], data1=BT[:, cc, 0:n],
                        initial=hprev[:, cc:cc + 1], op0=ALU.mult, op1=ALU.add),
                        reads=[("a", cc), "bt", "hprev"], writes=[("rg", cc)])
            steps.append(scans)

            def fin():
                P.add("dve", lambda e: e.tensor_copy(hprev[:, :], RG[:, :, n - 1]), reads=allrg, writes=["hprev"])
                P.add("dve", lambda e: e.tensor_tensor(out=rs[:, :, 0:n], in0=RG[:, :, 0:n], in1=gt[:, :, 0:n],
                                                       op=ALU.mult),
                      reads=allrg + ["gt"], writes=[("rst", i % 2)])
                dst = rnnT[:, :, t0:t0 + n].rearrange("c p t -> p c t")
                P.add("sp", lambda e, d=dst, o=rs[:, :, 0:n]: e.dma_start(out=d, in_=o),
                      reads=[("rst", i % 2)], stream="ro%d" % (i % 2))
                if i == nprompt - 1:
                    P.add("sp", lambda e: e.dma_start(out=io["lru_h_p"], in_=hprev), reads=["hprev"],
                          stream="o_hp")
            steps.append(fin)
            return steps

        def lru_sample(i):
            t0, n = tl[i]
            rs = rst[i % 2]
            tmp = [B[k][:, 0, :] for k in range(8)]
            psA, psX = psA4[0], psX4[0]
            xcb = xcb4[:, 0, :]
            for cc in range(4):
                xc = tmp[1][:, 0:n]
                xc3 = xc.rearrange("p (b t) -> p b t", t=8)
                uv = [ubs[:, cc, :, j:j + 8] for j in range(4)]
                ukey = ("ubs", cc)
                P.add("dve", lambda e, cc=cc, xc3=xc3, uv=uv: e.tensor_scalar(
                    out=xc3, in0=uv[0], scalar1=cw[:, cc, 0:1], scalar2=cvec[:, 0, cc:cc + 1],
                    op0=ALU.mult, op1=ALU.add),
                    reads=[ukey, "cw", "cvec"], writes=["xc"])
                for j in range(1, 4):
                    P.add("dve", lambda e, cc=cc, j=j, xc3=xc3, uv=uv: e.scalar_tensor_tensor(
                        out=xc3, in0=uv[j], scalar=cw[:, cc, j:j + 1], in1=xc3, op0=ALU.mult, op1=ALU.add),
                        reads=[ukey, "cw", "xc"], writes=["xc"])
                P.add("act", lambda e, xc=xc: e.copy(xcb[:, 0:n], xc), reads=["xc"], writes=["xcb"])
                P.add("pe", lambda e, cc=cc: e.matmul(psA[:, 0:n], lhsT=WaBD[:, cc, :], rhs=xcb[:, 0:n],
                                                      start=True, stop=True),
                      reads=["WaBD", "xcb"], writes=[("psA", 0)])
                P.add("pe", lambda e, cc=cc: e.matmul(psX[:, 0:n], lhsT=WxBD[:, cc, :], rhs=xcb[:, 0:n],
                                                      start=True, stop=True),
                      reads=["WxBD", "xcb"], writes=[("psX", 0)])
                rg = tmp[2][:, 0:n]
                ig = tmp[3][:, 0:n]
                P.add("act", lambda e, cc=cc, rg=rg: e.activation(out=rg, in_=psA[:, 0:n], func=AF.Sigmoid,
                                                                  bias=cvec[:, 1, cc:cc + 1], scale=1.0),
                      reads=[("psA", 0), "cvec"], writes=["rg"])
                P.add("act", lambda e, cc=cc, ig=ig: e.activation(out=ig, in_=psX[:, 0:n], func=AF.Sigmoid,
                                                                  bias=cvec[:, 2, cc:cc + 1], scale=1.0),
                      reads=[("psX", 0), "cvec"], writes=["ig"])
                a = tmp[4][:, 0:n]
                th = tmp[5][:, 0:n]
                P.add("act", lambda e, cc=cc, a=a, rg=rg: e.activation(out=a, in_=rg, func=AF.Exp,
                                                                       scale=lamc[:, cc:cc + 1]),
                      reads=["rg", "lamc"], writes=["a"])
                P.add("act", lambda e, cc=cc, th=th, rg=rg: e.activation(out=th, in_=rg, func=AF.Tanh,
                                                                         scale=lamc[:, cc:cc + 1]),
                      reads=["rg", "lamc"], writes=["th"])
                om = tmp[6][:, 0:n]
                P.add("dve", lambda e, th=th, om=om: e.tensor_scalar(out=om, in0=th, scalar1=-1.0, scalar2=1.0,
                                                                     op0=ALU.mult, op1=ALU.add),
                      reads=["th"], writes=["om"])
                P.add("dve", lambda e, om=om: e.reciprocal(om, om), reads=["om"], writes=["om"])
                P.add("dve", lambda e, th=th, om=om: e.scalar_tensor_tensor(
                    out=om, in0=th, scalar=-2.0, in1=om, op0=ALU.mult, op1=ALU.mult),
                    reads=["th", "om"], writes=["om"])
                P.add("act", lambda e, om=om: e.activation(out=om, in_=om, func=AF.Sqrt), reads=["om"],
                      writes=["om"])
                bt = tmp[7][:, 0:n]
                P.add("dve", lambda e, om=om, ig=ig, bt=bt: e.tensor_tensor(out=bt, in0=om, in1=ig, op=ALU.mult),
                      reads=["om", "ig"], writes=["bt"])
                P.add("dve", lambda e, xc=xc, bt=bt: e.tensor_tensor(out=bt, in0=bt, in1=xc, op=ALU.mult),
                      reads=["bt", "xc"], writes=["bt"])
                hs = tmp[2][:, 0:n]
                a3 = a.rearrange("p (b t) -> p b t", t=8)
                bt3 = bt.rearrange("p (b t) -> p b t", t=8)
                t3 = tmp[3][:, 0:NB]
                P.add("dve", lambda e, cc=cc, a3=a3, t3=t3: e.tensor_tensor(
                    out=t3, in0=a3[:, :, 0], in1=h0s[:, cc, :], op=ALU.mult),
                    reads=["a", "h0s", "bt"], writes=["ig"])
                P.add("dve", lambda e, bt3=bt3, t3=t3: e.tensor_tensor(
                    out=bt3[:, :, 0], in0=bt3[:, :, 0], in1=t3, op=ALU.add),
                    reads=["ig", "bt"], writes=["bt"])
                P.add("dve", lambda e, a3=a3: e.memset(a3[:, :, 0], 0.0), reads=["ig"], writes=["a"])
                P.add("dve", lambda e, a=a, bt=bt, hs=hs: e.tensor_tensor_scan(
                    out=hs, data0=a, data1=bt, initial=0.0, op0=ALU.mult, op1=ALU.add),
                    reads=["a", "bt", "th"], writes=["rg"])
                hs3 = hs.rearrange("p (b t) -> p b t", t=8)
                P.add("dve", lambda e, cc=cc, hs3=hs3: e.tensor_copy(hfin[:, cc, :], hs3[:, :, 7]),
                      reads=["rg"], writes=["hfin"])
                P.add("dve", lambda e, cc=cc, hs=hs: e.tensor_tensor(out=rs[:, cc, 0:n], in0=hs,
                                                                     in1=gt[:, cc, 0:n], op=ALU.mult),
                      reads=["rg", "gt"], writes=[("rst", i % 2)])
            dst = rnnT[:, :, t0:t0 + n].rearrange("c p t -> p c t")
            P.add("sp", lambda e, d=dst, o=rs[:, :, 0:n]: e.dma_start(out=d, in_=o),
                  reads=[("rst", i % 2)], stream="ro%d" % (i % 2))
            P.add("sp", lambda e: e.dma_start(out=io["lru_h_s"], in_=hfin), reads=["hfin"], stream="o_hs")

        tk = [0]

        def tok_outputs(i):
            t0, n = tl[i]
            is_s = (t0 >= S)
            if not is_s and t0 < S - WBUF:
                return
            ns = n // 128
            for s in range(ns):
                tt = t0 + s * 128
                last = (not is_s) and (tt + 128 == S)
                sects = [0, 1] + ([2] if (is_s or last) else [])
                for sec in sects:
                    j = tk[0]
                    tk[0] += 1
                    pz = psZ[j % NZ]
                    pk = ("psZ", j % NZ)
                    c0 = 512 * (1 + sec)
                    for kc in range(8):
                        P.add("pe", lambda e, kc=kc, pz=pz, s=s, c0=c0: e.matmul(
                            pz[:, :], lhsT=self.xnT[:, kc, s * 128:(s + 1) * 128], rhs=Win[:, kc, c0:c0 + 512],
                            start=(kc == 0), stop=(kc == 7)),
                            reads=["Win", "xnT"], writes=[pk])
                    tb = tok[j % 2]
                    tkey = ("tok", j % 2)
                    P.add("act", lambda e, tb=tb, pz=pz: e.copy(tb, pz[:, :]), reads=[pk], writes=[tkey])
                    if is_s:
                        if sec < 2:
                            dst = (io["knew"], io["vnew"])[sec]
                            P.add("sp", lambda e, d=dst, tb=tb: e.dma_start(out=d, in_=tb), reads=[tkey],
                                  stream="tk%d" % (j % 2))
                            if sec == 1:
                                P.add("dve", lambda e, tb=tb: e.tensor_copy(tokb, tb), reads=[tkey],
                                      writes=["tokb"])
                                P.add("sp", lambda e: e.dma_start(out=vtok, in_=tokb), reads=["tokb"],
                                      stream="tkb")
                        else:
                            for b in range(NB):
                                P.add("sp", lambda e, b=b, tb=tb: e.dma_start(
                                    out=io["conv_s"][b], in_=tb[b * 8 + 5:b * 8 + 8, :]),
                                    reads=[tkey], stream="tk%d" % (j % 2))
                    else:
                        if sec < 2:
                            r0 = tt - (S - WBUF)
                            dst = (io["kwin"], io["vwin"])[sec][r0:r0 + 128, :]
                            P.add("sp", lambda e, d=dst, tb=tb: e.dma_start(out=d, in_=tb), reads=[tkey],
                                  stream="tk%d" % (j % 2))
                        else:
                            P.add("sp", lambda e, tb=tb: e.dma_start(out=io["conv_p"], in_=tb[125:128, :]),
                                  reads=[tkey], stream="tk%d" % (j % 2))

        self.t_load(x1, tl, 0)
        if nt > 1:
            self.t_load(x1, tl, 1)
        self.t_prep(tl, 0)
        self.t_transposes(tl, 0)
        pending = []
        for i in range(nt):
            zproj(i, pending)
            pending = []
            tok_outputs(i)
            if i + 2 < nt:
                self.t_load(x1, tl, i + 2)
            if i + 1 < nt:
                self.t_prep(tl, i + 1)
                self.t_transposes(tl, i + 1)
            if tl[i][0] >= S:
                gelu_gate(i)
                lru_sample(i)
            else:
                pending = lru_prompt_steps(i)
        for f in pending:
            f()
        P.barrier()

    def attn_prompt_phase(self, zT, attnT, io):
        P = self.P
        S = self.S
        self.arena_off = self.arena_mark
        qT = self.alloc([128, S], BF16)
        kT = self.alloc([128, S], BF16)
        vT = self.alloc([128, S], BF16)
        acc = self.alloc([128, 2, S], F32)
        mask = self.alloc([128, 2, 256], BF16)
        onesf = self.alloc([128, 64], F32)
        pT = [self.alloc([128, 2, 256], BF16), self.alloc([128, 2, 256], BF16)]
        vblk = [self.alloc([128, 2, 65], BF16), self.alloc([128, 2, 65], BF16)]
        ast = [self.alloc([128, 512], BF16), self.alloc([128, 512], BF16)]
        psS2 = [self.pbig[:, 0:1024], self.pbig[:, 1024:2048]]
        psV = [self.pb[4].bitcast(BF16)[:, 0:128], self.pb[5].bitcast(BF16)[:, 0:128]]
        psO = [self.pb[6], self.pb[7]]
        psB = [self.pb[4], self.pb[4]]
        P.add("sp", lambda e: e.dma_start(out=mask, in_=io["maskp"]), writes=["mask"], stream="c_mask")
        P.add("dve", lambda e: e.memset(onesf, 1.0), writes=["onesf"])
        for v in range(2):
            P.add("dve", lambda e, v=v: e.memset(vblk[v][:, :, 64:65], 1.0), writes=[("vblk", v)])
        it = [0]
        for c in range(4):
            for nm, buf, ch in (("qT", qT, c), ("kT", kT, 4 + c), ("vT", vT, 8 + c)):
                for h0 in range(0, S, 4096):
                    h1 = min(S, h0 + 4096)
                    P.add("sp", lambda e, buf=buf, ch=ch, h0=h0, h1=h1: e.dma_start(
                        out=buf[:, h0:h1], in_=zT[ch, :, h0:h1]), writes=[nm], stream="ld_" + nm)
            P.add("pool", lambda e: e.memset(acc[0:65, :, :], 0.0), writes=["acc"])
            iters = []
            for d in (1, 4, 16):
                span = 128 * d
                nsb = S // span
                for nb in range(nsb):
                    nq = 256 if nb + 1 < nsb else 128
                    for r in range(d):
                        base = nb * span + r
                        iters.append((base, d, nq))

            def stage1(base, d, nq, pb):
                ks = slice(base, base + d * 127 + 1, d)
                qs = slice(base, base + d * (nq - 1) + 1, d)
                for e2 in range(2):
                    rows = slice(e2 * 64, (e2 + 1) * 64)
                    P.add("pe", lambda e, e2=e2, rows=rows: e.matmul(
                        psS2[pb][:, e2 * 512:e2 * 512 + nq], lhsT=kT[rows, ks], rhs=qT[rows, qs],
                        start=True, stop=True),
                        reads=["kT", "qT"], writes=[("psS", pb)])
                P.add("pe", lambda e: e.transpose(out=psV[pb], in_=vT[:, ks], identity=self.ident[:, :]),
                      reads=["vT", "ident"], writes=[("psV", pb)])
                sv = psS2[pb].rearrange("p (e q) -> p e q", e=2)[:, :, 0:nq]
                P.add("act", lambda e: e.activation(out=pT[pb][:, :, 0:nq], in_=sv, func=AF.Exp),
                      reads=[("psS", pb)], writes=[("pT", pb)])
                P.add("pool", lambda e: e.tensor_tensor(
                    out=pT[pb][:, :, 0:nq], in0=pT[pb][:, :, 0:nq], in1=mask[:, :, 0:nq], op=ALU.mult),
                    reads=[("pT", pb), "mask"], writes=[("pT", pb)])
                P.add("dve", lambda e: e.tensor_copy(
                    vblk[pb][:, :, 0:64], psV[pb].rearrange("p (e c) -> p e c", e=2)),
                    reads=[("psV", pb)], writes=[("vblk", pb)])

            def stage2(base, d, nq, pb):
                qs = slice(base, base + d * (nq - 1) + 1, d)
                for e2 in range(2):
                    P.add("pe", lambda e, e2=e2: e.matmul(
                        psO[pb][0:65, e2 * 256:e2 * 256 + nq], lhsT=vblk[pb][:, e2, :],
                        rhs=pT[pb][:, e2, 0:nq], start=True, stop=True),
                        reads=[("vblk", pb), ("pT", pb)], writes=[("psO", pb)])
                av = acc[0:65, :, qs]
                ov = psO[pb][0:65, :].rearrange("p (e q) -> p e q", e=2)[:, :, 0:nq]
                P.add("dve", lambda e: e.tensor_tensor(out=av, in0=ov, in1=av, op=ALU.add),
                      reads=[("psO", pb), "acc"], writes=["acc"])

            for idx, (base, d, nq) in enumerate(iters):
                stage1(base, d, nq, idx % 2)
                if idx > 0:
                    pbase, pd, pnq = iters[idx - 1]
                    stage2(pbase, pd, pnq, (idx - 1) % 2)
            pbase, pd, pnq = iters[-1]
            stage2(pbase, pd, pnq, (len(iters) - 1) % 2)
            for e2 in range(2):
                P.add("dve", lambda e, e2=e2: e.reciprocal(acc[64:65, e2, :], acc[64:65, e2, :]),
                      reads=["acc"], writes=["acc"])
            k = 0
            for e2 in range(2):
                for p0 in range(0, S, 512):
                    pbb = k % 2
                    k += 1
                    P.add("pe", lambda e, e2=e2, p0=p0, pbb=pbb: e.matmul(
                        psB[pbb][0:64, :], lhsT=onesf[64:65, 0:64], rhs=acc[64:65, e2, p0:p0 + 512],
                        start=True, stop=True),
                        reads=["onesf", "acc"], writes=[("psV", 0)])
                    P.add("dve", lambda e, e2=e2, p0=p0, pbb=pbb: e.tensor_tensor(
                        out=ast[pbb][0:64, :], in0=acc[0:64, e2, p0:p0 + 512], in1=psB[pbb][0:64, :],
                        op=ALU.mult),
                        reads=["acc", ("psV", 0)], writes=[("ast", pbb)])
                    P.add("sp", lambda e, e2=e2, p0=p0, pbb=pbb, c=c: e.dma_start(
                        out=attnT[2 * c + e2, :, p0:p0 + 512], in_=ast[pbb][0:64, :]),
                        reads=[("ast", pbb)], stream="ao%d" % pbb)
        P.barrier()

    def attn_sample_phase(self, zT, vtok, attnT, io):
        P = self.P
        S, NB, NS = self.S, self.NB, self.NS
        self.arena_off = self.arena_mark
        NBLK = WBUF // 128
        stgk = [self.alloc([128, 8, 512], F32) for _ in range(4)]
        kb = self.alloc([128, NBLK, 512], BF16)
        kTs = self.alloc([128, 4, WBUF + 8], BF16)
        vb = [self.alloc([128, NBLK + 1, 8, 65], BF16), self.alloc([128, NBLK + 1, 8, 65], BF16)]
        qTs = self.alloc([128, 4, NS], BF16)
        kTn = self.alloc([128, 4, NS], BF16)
        vnb = self.alloc([128, 512], BF16)
        masks = self.alloc([128, NBLK + 1, 8, 8], BF16)
        pTs = [self.alloc([128, NBLK + 1, 8, 8], BF16), self.alloc([128, NBLK + 1, 8, 8], BF16)]
        accs = self.alloc([128, 8, NS], F32)
        onesf = self.alloc([128, 64], F32)
        ast = [self.alloc([128, NS], BF16), self.alloc([128, NS], BF16)]
        psK = [self.pb[0].bitcast(BF16), self.pb[1].bitcast(BF16)]
        psMain = [self.pb[2], self.pb[3]]
        psNew = [self.pb[4], self.pb[5]]
        psO = [self.pb[6][:, 0:64], self.pb[7][:, 0:64]]
        psB = self.pb[6][:, 128:256]
        P.add("sp", lambda e: e.dma_start(out=masks, in_=io["masks"]), writes=["masks"], stream="c_masks")
        P.add("sp", lambda e: e.dma_start(out=qTs, in_=zT[0:4, :, S:S + NS].rearrange("c p t -> p c t")),
              writes=["qTs"], stream="c_qTs")
        P.add("sp", lambda e: e.dma_start(out=kTn, in_=zT[4:8, :, S:S + NS].rearrange("c p t -> p c t")),
              writes=["kTn"], stream="c_kTn")
        P.add("dve", lambda e: e.memset(onesf, 1.0), writes=["onesf"])
        for v in range(2):
            P.add("pool", lambda e, v=v: e.memset(vb[v][:, :, :, 64:65], 1.0), writes=[("vb", v)])
        sk = [0]

        def load_half(src, b, half):
            j = sk[0]
            sk[0] += 1
            slot = j % 4
            sv = src[b, half * 1024:(half + 1) * 1024, :].rearrange("(k p) d -> p k d", p=128)
            P.add("sp", lambda e, slot=slot, sv=sv: e.dma_start(out=stgk[slot], in_=sv),
                  writes=[("stgk", slot)], stream="sk%d" % slot)
            return slot

        def sview(blk, h):
            par, hh = h % 2, h // 2
            if blk < NBLK:
                off = blk * 32 + hh * 8
                return psMain[par][:, off:off + 8]
            return psNew[par][:, hh * 8:hh * 8 + 8]

        for b in range(NB):
            vv = vb[b % 2]
            vkey = ("vb", b % 2)
            for half in range(2):
                slot = load_half(io["kcache"], b, half)
                eng = "dve" if half == 0 else "pool"
                P.add(eng, lambda e, slot=slot, half=half: e.tensor_copy(kb[:, half * 8:(half + 1) * 8, :],
                                                                         stgk[slot]),
                      reads=[("stgk", slot)], writes=[("kb", half)])
            for half in range(2):
                slot = load_half(io["vcache"], b, half)
                dstv = vv[:, half * 8:(half + 1) * 8, :, 0:64]
                srcv = stgk[slot].rearrange("p k (h c) -> p k h c", h=8)
                if half == 0:
                    P.add("act", lambda e, dstv=dstv, srcv=srcv: e.copy(dstv, srcv),
                          reads=[("stgk", slot)], writes=[vkey])
                else:
                    P.add("dve", lambda e, dstv=dstv, srcv=srcv: e.tensor_copy(dstv, srcv),
                          reads=[("stgk", slot)], writes=[vkey])
            P.add("sp", lambda e, b=b: e.dma_start(out=vnb[0:8, :], in_=vtok[b * 8:(b + 1) * 8, :]),
                  writes=["vnb"], stream="vn")
            P.add("dve", lambda e, vv=vv: e.tensor_copy(vv[0:8, NBLK, :, 0:64],
                                                        vnb[0:8, :].rearrange("p (h c) -> p h c", h=8)),
                  reads=["vnb"], writes=[vkey])
            P.add("pool", lambda e, b=b: e.tensor_copy(kTs[:, :, WBUF:WBUF + 8], kTn[:, :, b * 8:(b + 1) * 8]),
                  reads=["kTn"], writes=["kTs"])
            tj = 0
            for blk in range(NBLK):
                pk = tj % 2
                tj += 1
                for c in range(4):
                    P.add("pe", lambda e, blk=blk, c=c, pk=pk: e.transpose(
                        out=psK[pk][:, c * 128:(c + 1) * 128], in_=kb[:, blk, c * 128:(c + 1) * 128],
                        identity=self.ident[:, :]),
                        reads=[("kb", blk // 8), "ident"], writes=[("psK", pk)])
                src = psK[pk][:, 0:512].rearrange("p (c k) -> p c k", c=4)
                dst = kTs[:, :, blk * 128:(blk + 1) * 128]
                if blk % 2 == 0:
                    P.add("act", lambda e, d=dst, s=src: e.copy(d, s), reads=[("psK", pk)], writes=["kTs"])
                else:
                    P.add("dve", lambda e, d=dst, s=src: e.tensor_copy(d, s), reads=[("psK", pk)],
                          writes=["kTs"])
            pp = pTs[b % 2]
            pkey = ("pTs", b % 2)
            for blk in range(NBLK + 1):
                nk = 128 if blk < NBLK else 8
                for h in range(8):
                    rows = slice((h % 2) * 64, (h % 2) * 64 + 64)
                    P.add("pe", lambda e, blk=blk, h=h, rows=rows, nk=nk, b=b: e.matmul(
                        sview(blk, h)[0:nk, :], lhsT=kTs[rows, h // 2, blk * 128:blk * 128 + nk],
                        rhs=qTs[rows, h // 2, b * 8:(b + 1) * 8], start=True, stop=True),
                        reads=["kTs", "qTs"], writes=["psSs"])
            for par in range(2):
                P.add("act", lambda e, par=par, pp=pp: e.activation(
                    out=pp[:, 0:NBLK, par::2, :],
                    in_=psMain[par].rearrange("p (k h t) -> p k h t", k=NBLK, h=4), func=AF.Exp),
                    reads=["psSs"], writes=[pkey])
                P.add("act", lambda e, par=par, pp=pp: e.activation(
                    out=pp[0:8, NBLK, par::2, :],
                    in_=psNew[par][0:8, 0:32].rearrange("p (h t) -> p h t", h=4), func=AF.Exp),
                    reads=["psSs"], writes=[pkey])
            P.add("pool", lambda e, pp=pp: e.tensor_tensor(out=pp[:, 0:NBLK, :, :], in0=pp[:, 0:NBLK, :, :],
                                                           in1=masks[:, 0:NBLK, :, :], op=ALU.mult),
                  reads=[pkey, "masks"], writes=[pkey])
            P.add("dve", lambda e, pp=pp: e.tensor_tensor(out=pp[0:8, NBLK, :, :], in0=pp[0:8, NBLK, :, :],
                                                          in1=masks[0:8, NBLK, :, :], op=ALU.mult),
                  reads=[pkey, "masks"], writes=[pkey])
            po = psO[b % 2]
            for h in range(8):
                for blk in range(NBLK + 1):
                    nk = 128 if blk < NBLK else 8
                    P.add("pe", lambda e, blk=blk, h=h, nk=nk, po=po, vv=vv, pp=pp: e.matmul(
                        po[0:65, h * 8:(h + 1) * 8], lhsT=vv[0:nk, blk, h, :], rhs=pp[0:nk, blk, h, :],
                        start=(blk == 0), stop=(blk == NBLK)),
                        reads=[vkey, pkey], writes=[("psOs", b % 2)])
            P.add("dve", lambda e, b=b, po=po: e.tensor_copy(
                accs[0:65, :, b * 8:(b + 1) * 8], po[0:65, :].rearrange("p (h t) -> p h t", h=8)),
                reads=[("psOs", b % 2)], writes=["accs"])
        for h in range(8):
            P.add("dve", lambda e, h=h: e.reciprocal(accs[64:65, h, :], accs[64:65, h, :]),
                  reads=["accs"], writes=["accs"])
            P.add("pe", lambda e, h=h: e.matmul(psB[0:64, 0:NS], lhsT=onesf[64:65, 0:64], rhs=accs[64:65, h, :],
                                                start=True, stop=True),
                  reads=["onesf", "accs"], writes=[("psOs", 0)])
            P.add("dve", lambda e, h=h: e.tensor_tensor(out=ast[h % 2][0:64, :], in0=accs[0:64, h, :],
                                                        in1=psB[0:64, 0:NS], op=ALU.mult),
                  reads=["accs", ("psOs", 0)], writes=[("ast", h % 2)])
            P.add("sp", lambda e, h=h: e.dma_start(out=attnT[h, :, S:S + NS], in_=ast[h % 2][0:64, :]),
                  reads=[("ast", h % 2)], stream="aso%d" % (h % 2))
        P.barrier()

    def mix_out_phase(self, x1, attnT, rnnT, x2, io):
        P = self.P
        self.arena_off = self.arena_mark
        self.common_tile_bufs()
        WoA = self.alloc([128, 8, D], BF16)
        WoR = self.alloc([128, 4, D], BF16)
        at = [self.alloc([128, 8, TT], BF16), self.alloc([128, 8, TT], BF16)]
        rt = [self.alloc([128, 4, TT], BF16), self.alloc([128, 4, TT], BF16)]
        psD = [self.pb[4], self.pb[5]]
        wo = io["w_out"]
        for h in range(8):
            self.load_cast(WoA[0:64, h, :], wo[h * 64:(h + 1) * 64, :], D, 1.0, "WoA", h)
        for cc in range(4):
            self.load_cast(WoR[:, cc, :], wo[512 + cc * 128:512 + (cc + 1) * 128, :], D, 1.0, "WoR", cc)
        tl = self.tiles()
        nt = len(tl)

        def load(i):
            t0, n = tl[i]
            slot = i % 2
            self.t_load(x1, tl, i)
            P.add("sp", lambda e: e.dma_start(out=at[slot][0:64, :, 0:n],
                                              in_=attnT[:, :, t0:t0 + n].rearrange("h c t -> c h t")),
                  writes=[("at", slot)], stream="at%d" % slot)
            P.add("sp", lambda e: e.dma_start(out=rt[slot][:, :, 0:n],
                                              in_=rnnT[:, :, t0:t0 + n].rearrange("c p t -> p c t")),
                  writes=[("rt", slot)], stream="rt%d" % slot)

        load(0)
        if nt > 1:
            load(1)
        j = 0
        for i in range(nt):
            t0, n = tl[i]
            ns = n // 128
            slot = i % 2
            xs = self.xbuf[slot]
            for s in range(ns):
                for hf in range(2):
                    pd = psD[j % 2]
                    pk = ("psD", j % 2)
                    j += 1
                    for h in range(8):
                        P.add("pe", lambda e, h=h, s=s, hf=hf, pd=pd, slot=slot: e.matmul(
                            pd[:, :], lhsT=at[slot][0:64, h, s * 128:(s + 1) * 128],
                            rhs=WoA[0:64, h, hf * 512:(hf + 1) * 512], start=(h == 0), stop=False),
                            reads=["WoA", ("at", slot)], writes=[pk])
                    for cc in range(4):
                        P.add("pe", lambda e, cc=cc, s=s, hf=hf, pd=pd, slot=slot: e.matmul(
                            pd[:, :], lhsT=rt[slot][:, cc, s * 128:(s + 1) * 128],
                            rhs=WoR[:, cc, hf * 512:(hf + 1) * 512], start=False, stop=(cc == 3)),
                            reads=["WoR", ("rt", slot)], writes=[pk])
                    P.add("dve", lambda e, s=s, hf=hf, pd=pd, xs=xs: e.tensor_tensor(
                        out=xs[:, s, hf * 512:(hf + 1) * 512], in0=pd[:, :],
                        in1=xs[:, s, hf * 512:(hf + 1) * 512], op=ALU.add),
                        reads=[pk, ("x", slot)], writes=[("x", slot)])
            dst = x2[t0:t0 + n, :].rearrange("(s p) d -> p s d", p=128)
            P.add("sp", lambda e, d=dst, o=xs[:, 0:ns, :]: e.dma_start(out=d, in_=o),
                  reads=[("x", slot)], stream="xo%d" % slot)
            if i + 2 < nt:
                load(i + 2)
        P.barrier()

    def build(self):
        nc = self.nc
        P = self.P
        NT, S, NB, NS = self.NT, self.S, self.NB, self.NS
        xin = self.dram_in("xin", [NT, D])
        w1g = self.dram_in("w1g", [D, DFF]); w1u = self.dram_in("w1u", [D, DFF]); w1d = self.dram_in("w1d", [DFF, D])
        w2g = self.dram_in("w2g", [D, DFF]); w2u = self.dram_in("w2u", [D, DFF]); w2d = self.dram_in("w2d", [DFF, D])
        gains_d = self.dram_in("gains", [128, 24])
        lnf_d = self.dram_in("lnf", [D])
        ident_d = self.dram_in("ident", [128, 128], BF16)
        io = {}
        io["w_in"] = self.dram_in("w_in", [D, INC])
        io["w_out"] = self.dram_in("w_out", [D, D])
        io["wabd"] = self.dram_in("wabd", [128, 4, 128])
        io["wxbd"] = self.dram_in("wxbd", [128, 4, 128])
        io["cw"] = self.dram_in("cw", [128, 4, 4])
        io["cvec"] = self.dram_in("cvec", [128, 4, 4])
        io["convst"] = self.dram_in("convst", [128, 4, NB, 3])
        io["h0"] = self.dram_in("h0", [128, 4, NB])
        io["kcache"] = self.dram_in("kcache", [NB, WBUF, 512])
        io["vcache"] = self.dram_in("vcache", [NB, WBUF, 512])
        io["maskp"] = self.dram_in("maskp", [128, 2, 256], BF16)
        io["masks"] = self.dram_in("masks", [128, 17, 8, 8], BF16)
        y = self.dram_out("y", [NT, D])
        io["kwin"] = self.dram_out("kwin", [WBUF, 512])
        io["vwin"] = self.dram_out("vwin", [WBUF, 512])
        io["lru_h_p"] = self.dram_out("lru_h_p", [128, 4])
        io["conv_p"] = self.dram_out("conv_p", [3, 512])
        io["knew"] = self.dram_out("knew", [NS, 512])
        io["vnew"] = self.dram_out("vnew", [NS, 512])
        io["lru_h_s"] = self.dram_out("lru_h_s", [128, 4, NB])
        io["conv_s"] = self.dram_out("conv_s", [NB, 3, 512])
        x1 = self.dram_tmp("x1", [NT, D], F32)
        x2 = self.dram_tmp("x2", [NT, D], F32)
        zT = self.dram_tmp("zT", [12, 128, NT], BF16)
        rnnT = self.dram_tmp("rnnT", [4, 128, NT], BF16)
        attnT = self.dram_tmp("attnT", [8, 64, NT], BF16)
        vtok = self.dram_tmp("vtok", [NS, 512], BF16)
        with ExitStack() as stack:
            self.stack = stack
            self.arena = stack.enter_context(nc.sbuf_tensor("arena", [128, ARENA_WORDS], F32))
            self.pbig = stack.enter_context(nc.psum_tensor("pbig", [128, 4096], F32))
            self.pb = [self.pbig[:, i * 512:(i + 1) * 512] for i in range(8)]
            self.arena_off = 0
            self.gains = self.alloc([128, 24], F32)
            self.lnf = self.alloc([128, D], F32)
            self.ident = self.alloc([128, 128], BF16)
            self.epsc = self.alloc([128, 1], F32)
            self.arena_mark = self.arena_off

            P.add("dve", lambda e: e.memset(self.epsc, EPS), writes=["epsc"])
            P.add("sp", lambda e: e.dma_start(out=self.gains, in_=gains_d), writes=["gains"], stream="c_gains")
            P.add("sp", lambda e: e.dma_start(out=self.lnf, in_=_bcast_rows(lnf_d, 128)), writes=["lnf"],
                  stream="c_lnf")
            P.add("sp", lambda e: e.dma_start(out=self.ident, in_=ident_d), writes=["ident"], stream="c_ident")
            P.barrier()

            src = xin
            if "ffn1" in self.phases:
                self.ffn_phase("f1", src, x1, w1g, w1u, w1d, 0, final_norm=False)
                src = x1
            ph = self.phases
            if "mix" in ph or "mix_in" in ph:
                self.mix_in_phase(src, zT, rnnT, vtok, io)
            if "mix" in ph or "attn_p" in ph:
                self.attn_prompt_phase(zT, attnT, io)
            if "mix" in ph or "attn_s" in ph:
                self.attn_sample_phase(zT, vtok, attnT, io)
            if "mix" in ph or "mix_out" in ph:
                self.mix_out_phase(src, attnT, rnnT, x2, io)
                src = x2
            if "ffn2" in self.phases:
                self.ffn_phase("f2", src, y, w2g, w2u, w2d, 2, final_norm=True)

            P.finalize_outputs()
            P.emit(nc, lambda name: stack.enter_context(nc.semaphore(name)))
        return nc


def _fm(v):
    return np.ascontiguousarray(np.asarray(v, np.float32).reshape(-1, 128).T)


def _blockdiag(w):
    out = np.zeros((128, 4, 128), np.float32)
    for c in range(4):
        out[0:64, c, 0:64] = w[2 * c]
        out[64:128, c, 64:128] = w[2 * c + 1]
    return out


def _mask_prompt():
    j = np.arange(128)[:, None]
    i = np.arange(128)[None, :]
    m = np.concatenate([(j <= i), (j >= i)], axis=1).astype(np.float32)
    return np.ascontiguousarray(np.broadcast_to(m[:, None, :], (128, 2, 256))).astype(ml_dtypes.bfloat16)


def _mask_sample():
    m = np.zeros((128, 17, 8, 8), np.float32)
    p = np.arange(128)
    for blk in range(17):
        for t in range(8):
            s = blk * 128 + p
            dd = WBUF + t - s
            mult = ((dd >= 0) & (dd <= 128)).astype(np.float32)
            mult += ((dd >= 0) & (dd % 4 == 0) & (dd <= 512))
            mult += ((dd >= 0) & (dd % 16 == 0) & (dd <= 2048))
            if blk == 16:
                mult = np.where(p < 8, mult, 0.0)
            m[:, blk, :, t] = mult[:, None]
    return m.astype(ml_dtypes.bfloat16)


def make_in_maps(inputs, n_cores, S, NB):
    f = lambda a: np.asarray(a, np.float32)
    xp, xs = f(inputs["x_prompt"]), f(inputs["x_sample"])
    shared = dict(
        w1g=f(inputs["w_ffn1_gate"])[0], w1u=f(inputs["w_ffn1_up"])[0], w1d=f(inputs["w_ffn1_down"])[0],
        w2g=f(inputs["w_ffn2_gate"])[0], w2u=f(inputs["w_ffn2_up"])[0], w2d=f(inputs["w_ffn2_down"])[0],
        gains=np.ascontiguousarray(np.concatenate(
            [_fm(inputs["ln_ffn1"][0]), _fm(inputs["ln_mix"][0]), _fm(inputs["ln_ffn2"][0])], axis=1)),
        lnf=f(inputs["ln_final"]),
        ident=np.eye(128, dtype=np.float32).astype(ml_dtypes.bfloat16),
        w_in=f(inputs["w_in"])[0], w_out=f(inputs["w_out"])[0],
        wabd=_blockdiag(f(inputs["w_gate_a"])[0]), wxbd=_blockdiag(f(inputs["w_gate_x"])[0]),
        cw=np.ascontiguousarray(f(inputs["conv_w"])[0].reshape(4, 4, 128).transpose(2, 1, 0)),
        cvec=np.ascontiguousarray(np.stack(
            [_fm(inputs["conv_b"][0]), _fm(f(inputs["b_gate_a"])[0].reshape(-1)),
             _fm(f(inputs["b_gate_x"])[0].reshape(-1)), _fm(inputs["lru_lambda"][0])], axis=1)),
        maskp=_mask_prompt(), masks=_mask_sample(),
    )
    maps = []
    for c in range(n_cores):
        bs = slice(c * NB, (c + 1) * NB)
        m = dict(shared)
        m["xin"] = np.ascontiguousarray(np.concatenate([xp[c, :S], xs[bs].reshape(NB * 8, D)], axis=0))
        cs = f(inputs["state_lru_conv"])[0, bs]
        m["convst"] = np.ascontiguousarray(cs.reshape(NB, 3, 4, 128).transpose(3, 2, 0, 1))
        hh = f(inputs["state_lru_h"])[0, bs]
        m["h0"] = np.ascontiguousarray(hh.reshape(NB, 4, 128).transpose(2, 1, 0))
        m["kcache"] = np.ascontiguousarray(f(inputs["cache_k_win"])[0, bs].reshape(NB, WBUF, 512))
        m["vcache"] = np.ascontiguousarray(f(inputs["cache_v_win"])[0, bs].reshape(NB, WBUF, 512))
        maps.append(m)
    return maps


def assemble(results, n_cores, S, NB):
    ys = [r["y"] for r in results]
    y_prompt = np.stack([y[:S] for y in ys])
    y_sample = np.concatenate([y[S:].reshape(NB, 8, D) for y in ys])
    kwin = np.stack([r["kwin"].reshape(WBUF, NH, HD) for r in results])[None]
    vwin = np.stack([r["vwin"].reshape(WBUF, NH, HD) for r in results])[None]
    hp = np.stack([r["lru_h_p"].T.reshape(DR) for r in results])[None]
    cp = np.stack([r["conv_p"] for r in results])[None]
    knew = np.concatenate([r["knew"].reshape(NB, 8, NH, HD) for r in results])[None]
    vnew = np.concatenate([r["vnew"].reshape(NB, 8, NH, HD) for r in results])[None]
    hs = np.concatenate([r["lru_h_s"].transpose(2, 1, 0).reshape(NB, DR) for r in results])[None]
    cs = np.concatenate([r["conv_s"] for r in results])[None]
    outs = (y_prompt, y_sample, kwin, vwin, hp, cp, knew, vnew, hs, cs)
    return tuple(np.ascontiguousarray(o, dtype=np.float32) for o in outs)


_NC_CACHE = {}


def kernel(**inputs):
    n_cores = 8
    S = inputs["x_prompt"].shape[1]
    NB = inputs["x_sample"].shape[0] // n_cores
    key = (S, NB)
    if key not in _NC_CACHE:
        _NC_CACHE[key] = Builder(S, NB).build()
    nc = _NC_CACHE[key]
    maps = make_in_maps(inputs, n_cores, S, NB)
    res = run_bass_kernel_spmd(nc, maps, core_ids=list(range(n_cores)))
    return assemble(res.results, n_cores, S, NB)
```

```python
from contextlib import ExitStack
import numpy as np
import ml_dtypes
import concourse.bass as bass
import concourse.mybir as mybir
from concourse.bass_utils import run_bass_kernel_spmd

F32 = mybir.dt.float32
BF16 = mybir.dt.bfloat16
AF = mybir.ActivationFunctionType
ALU = mybir.AluOpType

D = 1024
DFF = 2816
NFC = DFF // 128
NH = 8
HD = 64
DR = 512
INC = 2560
EPS = 1e-6
SEM_CAP = 30000
TT = 256
NSUB = TT // 128
WBUF = 2048
ARENA_WORDS = 51 * 1024
GELU_C = 1.5957691216057308


class _Op:
    __slots__ = ("eng", "fn", "reads", "writes", "stream", "is_dma", "deps", "signal", "sig")

    def __init__(self, eng, fn, reads, writes, stream):
        self.eng = eng
        self.fn = fn
        self.reads = tuple(reads)
        self.writes = tuple(writes)
        self.stream = stream
        self.is_dma = stream is not None
        self.deps = []
        self.signal = False
        self.sig = None


class Prog:
    ENGS = ("pe", "act", "dve", "pool", "sp")

    def __init__(self):
        self.ops = []
        self.res_w = {}
        self.res_r = {}
        self.bar_deps = []
        self.bar_pending = set()

    def barrier(self):
        last = {}
        for i, op in enumerate(self.ops):
            last[("dma", op.stream) if op.is_dma else ("eng", op.eng)] = i
        self.bar_deps = sorted(last.values())
        self.bar_pending = set(self.ENGS)
        self.res_w = {}
        self.res_r = {}

    def add(self, eng, fn, reads=(), writes=(), stream=None):
        op = _Op(eng, fn, reads, writes, stream)
        idx = len(self.ops)
        raw = set()
        other = set()
        for r in op.reads:
            raw.update(self.res_w.get(r, ()))
        for w in op.writes:
            other.update(self.res_w.get(w, ()))
            other.update(self.res_r.get(w, ()))
        deps = set()
        for d in raw | other:
            dop = self.ops[d]
            if dop.is_dma or op.is_dma or dop.eng != op.eng:
                deps.add(d)
            elif d in raw:
                deps.add(d)
        if eng in self.bar_pending:
            self.bar_pending.discard(eng)
            for d in self.bar_deps:
                dop = self.ops[d]
                if dop.is_dma or dop.eng != eng:
                    deps.add(d)
        op.deps = sorted(deps)
        for d in op.deps:
            self.ops[d].signal = True
        for r in op.reads:
            self.res_r.setdefault(r, []).append(idx)
        for w in op.writes:
            if self.res_r.get(w):
                self.res_w[w] = [idx]
                self.res_r[w] = []
            else:
                self.res_w.setdefault(w, []).append(idx)
        self.ops.append(op)
        return idx

    def finalize_outputs(self):
        last = {}
        for i, op in enumerate(self.ops):
            if op.is_dma:
                last[op.stream] = i
        for op in self.ops:
            if op.is_dma:
                op.signal = True
        self.final_dma = sorted(last.values())

    def emit(self, nc, sem_ctx):
        cnt = {}
        for op in self.ops:
            if not op.signal:
                continue
            key = ("dma", op.stream) if op.is_dma else ("eng", op.eng)
            c = cnt.get(key, 0)
            epoch, within = divmod(c, SEM_CAP if not op.is_dma else SEM_CAP // 16)
            cnt[key] = c + 1
            semname = "%s_%s_%d" % (key[0], key[1], epoch)
            val = (within + 1) * (16 if op.is_dma else 1)
            op.sig = (semname, val)
        per_eng = {e: [] for e in self.ENGS}
        for i, op in enumerate(self.ops):
            per_eng[op.eng].append(i)
        sems = {}

        def sem(name):
            if name not in sems:
                sems[name] = sem_ctx(name)
            return sems[name]

        for op in self.ops:
            if op.sig is not None:
                sem(op.sig[0])
        final_dma = self.final_dma

        with nc.Block() as block:
            def body(ename):
                def _run(engine):
                    seen = {}

                    def wait(i):
                        sname, val = self.ops[i].sig
                        if seen.get(sname, 0) >= val:
                            return
                        seen[sname] = val
                        engine.wait_ge(sem(sname), val)

                    for i in per_eng[ename]:
                        op = self.ops[i]
                        for d in op.deps:
                            wait(d)
                        ins = op.fn(engine)
                        if op.sig is not None:
                            ins.then_inc(sem(op.sig[0]), 16 if op.is_dma else 1)
                    if ename == "sp":
                        for i in final_dma:
                            wait(i)
                return _run

            block.tensor(body("pe"))
            block.scalar(body("act"))
            block.vector(body("dve"))
            block.gpsimd(body("pool"))
            block.sync(body("sp"))


def _bcast_rows(ap1d, nparts):
    n = ap1d.shape[-1]
    return bass.AP(ap1d.tensor, ap1d.offset, [[0, nparts], [1, n]])


class Builder:
    def __init__(self, S, NB, phases=("ffn1", "mix", "ffn2")):
        assert S % 2048 == 0 and (NB * 8) % 128 == 0
        self.S = S
        self.NB = NB
        self.NS = NB * 8
        self.NT = S + self.NS
        self.phases = phases
        self.nc = bass.Bass("TRN2", target_bir_lowering=False)
        self.P = Prog()
        self.stack = None
        self._stg_i = 0

    def dram_in(self, name, shape, dt=F32):
        return self.nc.dram_tensor(name, list(shape), dt, kind="ExternalInput").ap()

    def dram_out(self, name, shape, dt=F32):
        return self.nc.dram_tensor(name, list(shape), dt, kind="ExternalOutput").ap()

    def dram_tmp(self, name, shape, dt):
        return self.nc.dram_tensor(name, list(shape), dt, kind="Internal").ap()

    def alloc(self, shape, dt):
        esz = 4 if dt == F32 else 2
        n = 1
        for s in shape[1:]:
            n *= s
        nbytes = (n * esz + 31) // 32 * 32
        w0 = self.arena_off
        nw = nbytes // 4
        assert w0 + nw <= ARENA_WORDS, "SBUF arena overflow: %d" % (w0 + nw)
        self.arena_off = w0 + nw
        v = self.arena[:, w0:w0 + nw]
        if dt != F32:
            v = v.bitcast(dt)
        v = v[:, 0:n]
        if len(shape) == 3:
            v = v.rearrange("p (a b) -> p a b", a=shape[1])
        elif len(shape) == 4:
            v = v.rearrange("p (a b c) -> p a b c", a=shape[1], b=shape[2])
        return v

    def tiles(self):
        out = []
        t = 0
        while t < self.S:
            n = min(TT, self.S - t)
            out.append((t, n))
            t += n
        if self.NS:
            out.append((self.S, self.NS))
        return out

    def load_cast(self, dst_ap, src_ap, width, scale, key, idx):
        P = self.P
        slot = self._stg_i % 2
        self._stg_i += 1
        np_ = dst_ap.shape[0]
        view = self.stg[slot][0:np_, 0:width]
        if len(dst_ap.shape) == 3:
            view = view.rearrange("p (a b) -> p a b", a=dst_ap.shape[1])
        P.add("sp", lambda e, v=view, s=src_ap: e.dma_start(out=v, in_=s),
              writes=[("stg", slot)], stream="stg%d" % slot)
        rd = [("stg", slot)] + ([] if isinstance(scale, float) else ["gains"])
        if idx % 2 == 1:
            P.add("act", lambda e, o=dst_ap, v=view, sc=scale: e.mul(o, v, sc), reads=rd, writes=[key])
        else:
            P.add("dve", lambda e, o=dst_ap, v=view, sc=scale: e.tensor_scalar_mul(o, v, sc), reads=rd, writes=[key])

    def t_load(self, x_src, tl, i):
        t0, n = tl[i]
        ns = n // 128
        slot = i % 2
        src = x_src[t0:t0 + n, :].rearrange("(s p) d -> p s d", p=128)
        self.P.add("sp", lambda e, o=self.xbuf[slot][:, 0:ns, :], s=src: e.dma_start(out=o, in_=s),
                   writes=[("x", slot)], stream="x%d" % slot)

    def t_prep(self, tl, i):
        P = self.P
        t0, n = tl[i]
        ns = n // 128
        slot = i % 2
        xs = self.xbuf[slot]
        for s in range(ns):
            P.add("act", lambda e, s=s, xs=xs: e.activation(
                out=self.junk[:, :], in_=xs[:, s, :], func=AF.Square, scale=1.0 / 32.0,
                accum_out=self.ss[:, s:s + 1]),
                reads=[("x", slot)], writes=["junk", "ss"])
        P.add("act", lambda e: e.activation(out=self.rstd[:, 0:ns], in_=self.ss[:, 0:ns], func=AF.Sqrt,
                                            bias=self.epsc[:, 0:1], scale=1.0),
              reads=["ss", "epsc"], writes=["rstd"])
        P.add("dve", lambda e: e.reciprocal(self.rstd[:, 0:ns], self.rstd[:, 0:ns]),
              reads=["rstd"], writes=["rstd"])
        for s in range(ns):
            if s % 2 == 0:
                P.add("dve", lambda e, s=s, xs=xs: e.tensor_scalar_mul(self.xn[:, s, :], xs[:, s, :],
                                                                       self.rstd[:, s:s + 1]),
                      reads=[("x", slot), "rstd"], writes=[("xn", s)])
            else:
                P.add("act", lambda e, s=s, xs=xs: e.mul(self.xn[:, s, :], xs[:, s, :], self.rstd[:, s:s + 1]),
                      reads=[("x", slot), "rstd"], writes=[("xn", s)])

    def t_transposes(self, tl, i):
        P = self.P
        t0, n = tl[i]
        ns = n // 128
        for s in range(ns):
            bank = self.psT[s % 2]
            for kc in range(8):
                P.add("pe", lambda e, s=s, kc=kc, bank=bank: e.transpose(
                    out=bank[:, kc * 128:(kc + 1) * 128], in_=self.xn[:, s, kc * 128:(kc + 1) * 128],
                    identity=self.ident[:, :]),
                    reads=[("xn", s), "ident"], writes=[("psT", s % 2)])
            src = bank[:, :].rearrange("p (k t) -> p k t", k=8)
            dst = self.xnT[:, :, s * 128:(s + 1) * 128]
            if s % 2 == 0:
                P.add("act", lambda e, d=dst, sr=src: e.copy(d, sr), reads=[("psT", s % 2)], writes=["xnT"])
            else:
                P.add("dve", lambda e, d=dst, sr=src: e.tensor_copy(d, sr), reads=[("psT", s % 2)],
                      writes=["xnT"])

    def common_tile_bufs(self):
        self.xbuf = [self.alloc([128, NSUB, D], F32), self.alloc([128, NSUB, D], F32)]
        self.xn = self.alloc([128, NSUB, D], BF16)
        self.xnT = self.alloc([128, 8, TT], BF16)
        self.junk = self.alloc([128, D], BF16)
        self.ss = self.alloc([128, 4], F32)
        self.rstd = self.alloc([128, 4], F32)
        self.ss2 = self.alloc([128, 4], F32)
        self.rstd2 = self.alloc([128, 4], F32)
        self.stg = [self.alloc([128, DFF // 2], F32), self.alloc([128, DFF // 2], F32)]

    def ffn_phase(self, tag, x_src, x_dst, wg, wu, wd, gcol, final_norm):
        P = self.P
        self.arena_off = self.arena_mark
        self.common_tile_bufs()
        self.wA = self.alloc([128, 8, DFF], BF16)
        self.wB = self.alloc([128, 8, DFF], BF16)
        self.wC = self.alloc([128, NFC, D], BF16)
        self.aT = self.alloc([128, NFC, TT], BF16)
        self.sg = [self.alloc([128, TT], F32), self.alloc([128, TT], F32)]
        psG = [self.pb[0], self.pb[1]]
        psU = [self.pb[2], self.pb[3]]
        psD = [self.pb[4], self.pb[5]]
        self.psT = [self.pb[6].bitcast(BF16), self.pb[7].bitcast(BF16)]
        HW = DFF // 2
        k = 0
        for kc in range(8):
            for hh in range(2):
                self.load_cast(self.wA[:, kc, hh * HW:(hh + 1) * HW],
                               wg[kc * 128:(kc + 1) * 128, hh * HW:(hh + 1) * HW],
                               HW, self.gains[:, gcol * 8 + kc:gcol * 8 + kc + 1], "wA", k)
                k += 1
        for kc in range(8):
            for hh in range(2):
                self.load_cast(self.wB[:, kc, hh * HW:(hh + 1) * HW],
                               wu[kc * 128:(kc + 1) * 128, hh * HW:(hh + 1) * HW],
                               HW, self.gains[:, gcol * 8 + kc:gcol * 8 + kc + 1], "wB", k)
                k += 1
        for fc in range(NFC):
            self.load_cast(self.wC[:, fc, :], wd[fc * 128:(fc + 1) * 128, :], D, 0.5, "wC", k)
            k += 1
        tl = self.tiles()
        xbuf = self.xbuf

        def gate_up(i):
            t0, n = tl[i]
            for fc in range(NFC):
                pb = fc % 2
                pg, pu = psG[pb], psU[pb]
                for kc in range(8):
                    P.add("pe", lambda e, fc=fc, kc=kc, pg=pg: e.matmul(
                        pg[:, 0:n], lhsT=self.wA[:, kc, fc * 128:(fc + 1) * 128], rhs=self.xnT[:, kc, 0:n],
                        start=(kc == 0), stop=(kc == 7)),
                        reads=["wA", "xnT"], writes=[("psG", pb)])
                for kc in range(8):
                    P.add("pe", lambda e, fc=fc, kc=kc, pu=pu: e.matmul(
                        pu[:, 0:n], lhsT=self.wB[:, kc, fc * 128:(fc + 1) * 128], rhs=self.xnT[:, kc, 0:n],
                        start=(kc == 0), stop=(kc == 7)),
                        reads=["wB", "xnT"], writes=[("psU", pb)])
                sg = self.sg[pb]
                P.add("act", lambda e, pg=pg, sg=sg: e.activation(out=sg[:, 0:n], in_=pg[:, 0:n], func=AF.Silu),
                      reads=[("psG", pb)], writes=[("sg", pb)])
                P.add("dve", lambda e, fc=fc, pu=pu, sg=sg: e.tensor_tensor(
                    out=self.aT[:, fc, 0:n], in0=sg[:, 0:n], in1=pu[:, 0:n], op=ALU.mult),
                    reads=[("psU", pb), ("sg", pb)], writes=[("aT", fc)])

        def down(i):
            t0, n = tl[i]
            ns = n // 128
            slot = i % 2
            xs = xbuf[slot]
            j = 0
            for s in range(ns):
                for h in range(2):
                    pb = j % 2
                    j += 1
                    pd = psD[pb]
                    for fc in range(NFC):
                        P.add("pe", lambda e, fc=fc, s=s, h=h, pd=pd: e.matmul(
                            pd[:, :], lhsT=self.aT[:, fc, s * 128:(s + 1) * 128],
                            rhs=self.wC[:, fc, h * 512:(h + 1) * 512], start=(fc == 0), stop=(fc == NFC - 1)),
                            reads=["wC", ("aT", fc)], writes=[("psD", pb)])
                    P.add("dve", lambda e, s=s, h=h, pd=pd, xs=xs: e.tensor_tensor(
                        out=xs[:, s, h * 512:(h + 1) * 512], in0=pd[:, :], in1=xs[:, s, h * 512:(h + 1) * 512],
                        op=ALU.add),
                        reads=[("psD", pb), ("x", slot)], writes=[("x", slot)])
            if final_norm:
                for s in range(ns):
                    P.add("act", lambda e, s=s, xs=xs: e.activation(
                        out=self.junk[:, :], in_=xs[:, s, :], func=AF.Square, scale=1.0 / 32.0,
                        accum_out=self.ss2[:, s:s + 1]),
                        reads=[("x", slot)], writes=["junk", "ss2"])
                P.add("act", lambda e: e.activation(out=self.rstd2[:, 0:ns], in_=self.ss2[:, 0:ns], func=AF.Sqrt,
                                                    bias=self.epsc[:, 0:1], scale=1.0),
                      reads=["ss2", "epsc"], writes=["rstd2"])
                P.add("dve", lambda e: e.reciprocal(self.rstd2[:, 0:ns], self.rstd2[:, 0:ns]),
                      reads=["rstd2"], writes=["rstd2"])
                for s in range(ns):
                    P.add("dve", lambda e, s=s, xs=xs: e.scalar_tensor_tensor(
                        out=xs[:, s, :], in0=xs[:, s, :], scalar=self.rstd2[:, s:s + 1], in1=self.lnf[:, :],
                        op0=ALU.mult, op1=ALU.mult),
                        reads=[("x", slot), "rstd2", "lnf"], writes=[("x", slot)])
            dst = x_dst[t0:t0 + n, :].rearrange("(s p) d -> p s d", p=128)
            P.add("sp", lambda e, d=dst, o=xbuf[slot][:, 0:ns, :]: e.dma_start(out=d, in_=o),
                  reads=[("x", slot)], stream="xo%d" % slot)

        nt = len(tl)
        self.t_load(x_src, tl, 0)
        if nt > 1:
            self.t_load(x_src, tl, 1)
        self.t_prep(tl, 0)
        self.t_transposes(tl, 0)
        for i in range(nt):
            gate_up(i)
            if i + 1 < nt:
                self.t_prep(tl, i + 1)
            down(i)
            if i + 2 < nt:
                self.t_load(x_src, tl, i + 2)
            if i + 1 < nt:
                self.t_transposes(tl, i + 1)
        P.barrier()

    def mix_in_phase(self, x1, zT, rnnT, vtok, io):
        P = self.P
        S, NB, NS = self.S, self.NB, self.NS
        self.arena_off = self.arena_mark
        self.common_tile_bufs()
        Win = self.alloc([128, 8, INC], BF16)
        WaBD = self.alloc([128, 4, 128], BF16)
        WxBD = self.alloc([128, 4, 128], BF16)
        cw = self.alloc([128, 4, 4], F32)
        cvec = self.alloc([128, 4, 4], F32)
        lamc = self.alloc([128, 4], F32)
        onec = self.alloc([128, 1], F32)
        zst = [self.alloc([128, 12, TT], BF16), self.alloc([128, 12, TT], BF16)]
        ubuf = self.alloc([128, 4, TT + 3], F32)
        ubs = self.alloc([128, 4, NB, 11], F32)
        h0s = self.alloc([128, 4, NB], F32)
        hprev = self.alloc([128, 4], F32)
        hfin = self.alloc([128, 4, NB], F32)
        gs = self.alloc([128, 4, TT], F32)
        gt = self.alloc([128, 4, TT], F32)
        B = [self.alloc([128, 4, TT], F32) for _ in range(8)]
        xcb4 = self.alloc([128, 4, TT], BF16)
        rst = [self.alloc([128, 4, TT], BF16), self.alloc([128, 4, TT], BF16)]
        tok = [self.alloc([128, 512], F32), self.alloc([128, 512], F32)]
        tokb = self.alloc([128, 512], BF16)
        psZ = [self.pb[0], self.pb[1]]
        psA4 = [self.pb[2][:, 0:256], self.pb[2][:, 256:512], self.pb[3][:, 0:256], self.pb[3][:, 256:512]]
        psX4 = [self.pb[4][:, 0:256], self.pb[4][:, 256:512], self.pb[5][:, 0:256], self.pb[5][:, 256:512]]
        self.psT = [self.pb[6].bitcast(BF16), self.pb[7].bitcast(BF16)]
        NZ = len(psZ)

        k = 0
        for kc in range(8):
            for hh in range(2):
                self.load_cast(Win[:, kc, hh * 1280:(hh + 1) * 1280],
                               io["w_in"][kc * 128:(kc + 1) * 128, hh * 1280:(hh + 1) * 1280],
                               1280, self.gains[:, 8 + kc:8 + kc + 1], "Win", k)
                k += 1
        self.load_cast(WaBD, io["wabd"], 512, 1.0, "WaBD", 0)
        self.load_cast(WxBD, io["wxbd"], 512, 1.0, "WxBD", 1)
        P.add("sp", lambda e: e.dma_start(out=cw, in_=io["cw"]), writes=["cw"], stream="c_cw")
        P.add("sp", lambda e: e.dma_start(out=cvec, in_=io["cvec"]), writes=["cvec"], stream="c_cvec")
        P.add("sp", lambda e: e.dma_start(out=ubs[:, :, :, 0:3], in_=io["convst"]), writes=["ubs"], stream="c_ubs")
        P.add("sp", lambda e: e.dma_start(out=h0s, in_=io["h0"]), writes=["h0s"], stream="c_h0s")
        P.add("dve", lambda e: e.memset(onec, 1.0), writes=["onec"])
        P.add("dve", lambda e: e.memset(hprev, 0.0), writes=["hprev"])
        P.add("dve", lambda e: e.memset(ubuf[:, :, 0:3], 0.0), writes=[("ub", c) for c in range(4)])
        P.add("act", lambda e: e.activation(out=lamc, in_=cvec[:, 3, :], func=AF.Exp, scale=-1.0),
              reads=["cvec"], writes=["lamc"])
        P.add("act", lambda e: e.activation(out=lamc, in_=lamc, func=AF.Ln, bias=onec[:, 0:1], scale=1.0),
              reads=["lamc", "onec"], writes=["lamc"])
        P.add("dve", lambda e: e.tensor_scalar_mul(lamc, lamc, -8.0), reads=["lamc"], writes=["lamc"])

        tl = self.tiles()
        nt = len(tl)
        nprompt = nt - 1 if NS else nt

        def zproj(i, pending):
            t0, n = tl[i]
            is_s = (t0 >= S)
            zs = zst[i % 2]
            j = 0
            pending = list(pending)
            for c in range(20):
                if c >= 1 and pending:
                    pending.pop(0)()
                pz = psZ[j % NZ]
                pk = ("psZ", j % NZ)
                j += 1
                for kc in range(8):
                    P.add("pe", lambda e, c=c, kc=kc, pz=pz: e.matmul(
                        pz[:, 0:n], lhsT=Win[:, kc, c * 128:(c + 1) * 128], rhs=self.xnT[:, kc, 0:n],
                        start=(kc == 0), stop=(kc == 7)),
                        reads=["Win", "xnT"], writes=[pk])
                if c < 4:
                    P.add("act", lambda e, c=c, pz=pz: e.mul(zs[:, c, 0:n], pz[:, 0:n], 0.125),
                          reads=[pk], writes=[("zst", i % 2)])
                elif c < 12:
                    if c % 2 == 0:
                        P.add("act", lambda e, c=c, pz=pz: e.copy(zs[:, c, 0:n], pz[:, 0:n]),
                              reads=[pk], writes=[("zst", i % 2)])
                    else:
                        P.add("dve", lambda e, c=c, pz=pz: e.tensor_copy(zs[:, c, 0:n], pz[:, 0:n]),
                              reads=[pk], writes=[("zst", i % 2)])
                elif c < 16:
                    cc = c - 12
                    if is_s:
                        P.add("act", lambda e, cc=cc, pz=pz: e.copy(
                            ubs[:, cc, :, 3:11], pz[:, 0:n].rearrange("p (b t) -> p b t", t=8)),
                            reads=[pk], writes=[("ubs", cc)])
                    else:
                        P.add("act", lambda e, cc=cc, pz=pz: e.copy(ubuf[:, cc, 3:3 + n], pz[:, 0:n]),
                              reads=[pk], writes=[("ub", cc)])
                else:
                    cc = c - 16
                    P.add("act", lambda e, cc=cc, pz=pz: e.copy(gs[:, cc, 0:n], pz[:, 0:n]),
                          reads=[pk], writes=[("gs", cc)])
            dst = zT[:, :, t0:t0 + n].rearrange("c p t -> p c t")
            P.add("sp", lambda e, d=dst, o=zs[:, :, 0:n]: e.dma_start(out=d, in_=o),
                  reads=[("zst", i % 2)], stream="zo%d" % (i % 2))
            for f in pending:
                f()

        def gelu_gate(i):
            t0, n = tl[i]
            g = gs[:, :, 0:n]
            t = B[0][:, :, 0:n]
            P.add("dve", lambda e: e.tensor_tensor(out=t, in0=g, in1=g, op=ALU.mult),
                  reads=[("gs", c) for c in range(4)], writes=["t0"])
            P.add("dve", lambda e: e.tensor_scalar(out=t, in0=t, scalar1=0.044715, scalar2=1.0,
                                                   op0=ALU.mult, op1=ALU.add),
                  reads=["t0"], writes=["t0"])
            P.add("dve", lambda e: e.tensor_tensor(out=t, in0=t, in1=g, op=ALU.mult),
                  reads=["t0"] + [("gs", c) for c in range(4)], writes=["t0"])
            P.add("act", lambda e: e.activation(out=t, in_=t, func=AF.Sigmoid, scale=GELU_C),
                  reads=["t0"], writes=["t0"])
            P.add("dve", lambda e: e.tensor_tensor(out=gt[:, :, 0:n], in0=t, in1=g, op=ALU.mult),
                  reads=["t0"] + [("gs", c) for c in range(4)], writes=["gt"])

        def lru_prompt_steps(i):
            t0, n = tl[i]
            rs = rst[i % 2]
            XC, RG, IG, A, OM, BT, G0 = B[1], B[2], B[3], B[4], B[6], B[7], B[0]
            allxc = [("xc", c) for c in range(4)]
            allub = [("ub", c) for c in range(4)]
            allgs = [("gs", c) for c in range(4)]
            allrg = [("rg", c) for c in range(4)]
            g = gs[:, :, 0:n]
            t = G0[:, :, 0:n]
            om = OM[:, :, 0:n]
            bt = BT[:, :, 0:n]
            steps = []

            def conv0():
                for cc in range(4):
                    P.add("dve", lambda e, cc=cc: e.tensor_scalar(
                        out=XC[:, cc, 0:n], in0=ubuf[:, cc, 0:n], scalar1=cw[:, cc, 0:1],
                        scalar2=cvec[:, 0, cc:cc + 1], op0=ALU.mult, op1=ALU.add),
                        reads=[("ub", cc), "cw", "cvec"], writes=[("xc", cc)])
            steps.append(conv0)

            def convj(j):
                def f():
                    for cc in range(4):
                        P.add("dve", lambda e, cc=cc: e.scalar_tensor_tensor(
                            out=XC[:, cc, 0:n], in0=ubuf[:, cc, j:j + n], scalar=cw[:, cc, j:j + 1],
                            in1=XC[:, cc, 0:n], op0=ALU.mult, op1=ALU.add),
                            reads=[("ub", cc), "cw", ("xc", cc)], writes=[("xc", cc)])
                return f
            for j in range(1, 4):
                steps.append(convj(j))

            def cast():
                P.add("pool", lambda e: e.tensor_copy(ubuf[:, :, 0:3], ubuf[:, :, n:n + 3]), reads=allub,
                      writes=allub)
                P.add("act", lambda e: e.copy(xcb4[:, :, 0:n], XC[:, :, 0:n]), reads=allxc, writes=["xcb"])
            steps.append(cast)

            def gel1():
                P.add("dve", lambda e: e.tensor_tensor(out=t, in0=g, in1=g, op=ALU.mult), reads=allgs, writes=["t0"])
                P.add("dve", lambda e: e.tensor_scalar(out=t, in0=t, scalar1=0.044715, scalar2=1.0,
                                                       op0=ALU.mult, op1=ALU.add), reads=["t0"], writes=["t0"])
                P.add("dve", lambda e: e.tensor_tensor(out=t, in0=t, in1=g, op=ALU.mult), reads=["t0"] + allgs,
                      writes=["t0"])
            steps.append(gel1)

            def gates():
                for cc in range(4):
                    P.add("pe", lambda e, cc=cc: e.matmul(psA4[cc][:, 0:n], lhsT=WaBD[:, cc, :],
                                                          rhs=xcb4[:, cc, 0:n], start=True, stop=True),
                          reads=["WaBD", "xcb"], writes=[("psA", cc // 2)])
                    P.add("pe", lambda e, cc=cc: e.matmul(psX4[cc][:, 0:n], lhsT=WxBD[:, cc, :],
                                                          rhs=xcb4[:, cc, 0:n], start=True, stop=True),
                          reads=["WxBD", "xcb"], writes=[("psX", cc // 2)])
            steps.append(gates)

            def sig1():
                P.add("act", lambda e: e.activation(out=t, in_=t, func=AF.Sigmoid, scale=GELU_C),
                      reads=["t0"], writes=["t0"])
                for cc in range(4):
                    P.add("act", lambda e, cc=cc: e.activation(
                        out=RG[:, cc, 0:n], in_=psA4[cc][:, 0:n], func=AF.Sigmoid,
                        bias=cvec[:, 1, cc:cc + 1], scale=1.0),
                        reads=[("psA", cc // 2), "cvec"], writes=[("rg", cc)])
            steps.append(sig1)

            def sig2():
                for cc in range(4):
                    P.add("act", lambda e, cc=cc: e.activation(
                        out=IG[:, cc, 0:n], in_=psX4[cc][:, 0:n], func=AF.Sigmoid,
                        bias=cvec[:, 2, cc:cc + 1], scale=1.0),
                        reads=[("psX", cc // 2), "cvec"], writes=[("ig", cc)])
                P.add("dve", lambda e: e.tensor_tensor(out=gt[:, :, 0:n], in0=t, in1=g, op=ALU.mult),
                      reads=["t0"] + allgs, writes=["gt"])
            steps.append(sig2)

            def expa():
                for cc in range(4):
                    P.add("act", lambda e, cc=cc: e.activation(out=A[:, cc, 0:n], in_=RG[:, cc, 0:n], func=AF.Exp,
                                                               scale=lamc[:, cc:cc + 1]),
                          reads=[("rg", cc), "lamc"], writes=[("a", cc)])
            steps.append(expa)

            def om1():
                alla = [("a", c) for c in range(4)]
                P.add("dve", lambda e: e.tensor_tensor(out=om, in0=A[:, :, 0:n], in1=A[:, :, 0:n], op=ALU.mult),
                      reads=alla, writes=["om"])
                P.add("dve", lambda e: e.tensor_scalar(out=om, in0=om, scalar1=-1.0, scalar2=1.0,
                                                       op0=ALU.mult, op1=ALU.add), reads=["om"], writes=["om"])
                P.add("act", lambda e: e.activation(out=om, in_=om, func=AF.Sqrt), reads=["om"], writes=["om"])
                P.add("dve", lambda e: e.tensor_tensor(out=bt, in0=IG[:, :, 0:n], in1=XC[:, :, 0:n], op=ALU.mult),
                      reads=[("ig", c) for c in range(4)] + allxc, writes=["bt"])
            steps.append(om1)

            def bt2():
                P.add("dve", lambda e: e.tensor_tensor(out=bt, in0=bt, in1=om, op=ALU.mult),
                      reads=["bt", "om"], writes=["bt"])
            steps.append(bt2)

            def scans():
                for cc in range(4):
                    P.add("dve", lambda e, cc=cc: e.tensor_tensor_scan(
                        out=RG[:, cc, 0:n], data0=A[:, cc, 0:n], data1=BT[:, cc, 0:n],
                        initial=hprev[:, cc:cc + 1], op0=ALU.mult, op1=ALU.add),
                        reads=[("a", cc), "bt", "hprev"], writes=[("rg", cc)])
            steps.append(scans)

            def fin():
                P.add("dve", lambda e: e.tensor_copy(hprev[:, :], RG[:, :, n - 1]), reads=allrg, writes=["hprev"])
                P.add("dve", lambda e: e.tensor_tensor(out=rs[:, :, 0:n], in0=RG[:, :, 0:n], in1=gt[:, :, 0:n],
                                                       op=ALU.mult),
                      reads=allrg + ["gt"], writes=[("rst", i % 2)])
                dst = rnnT[:, :, t0:t0 + n].rearrange("c p t -> p c t")
                P.add("sp", lambda e, d=dst, o=rs[:, :, 0:n]: e.dma_start(out=d, in_=o),
                      reads=[("rst", i % 2)], stream="ro%d" % (i % 2))
                if i == nprompt - 1:
                    P.add("sp", lambda e: e.dma_start(out=io["lru_h_p"], in_=hprev), reads=["hprev"],
                          stream="o_hp")
            steps.append(fin)
            return steps

        def lru_sample(i):
            t0, n = tl[i]
            rs = rst[i % 2]
            tmp = [B[k][:, 0, :] for k in range(8)]
            psA, psX = psA4[0], psX4[0]
            xcb = xcb4[:, 0, :]
            for cc in range(4):
                xc = tmp[1][:, 0:n]
                xc3 = xc.rearrange("p (b t) -> p b t", t=8)
                uv = [ubs[:, cc, :, j:j + 8] for j in range(4)]
                ukey = ("ubs", cc)
                P.add("dve", lambda e, cc=cc, xc3=xc3, uv=uv: e.tensor_scalar(
                    out=xc3, in0=uv[0], scalar1=cw[:, cc, 0:1], scalar2=cvec[:, 0, cc:cc + 1],
                    op0=ALU.mult, op1=ALU.add),
                    reads=[ukey, "cw", "cvec"], writes=["xc"])
                for j in range(1, 4):
                    P.add("dve", lambda e, cc=cc, j=j, xc3=xc3, uv=uv: e.scalar_tensor_tensor(
                        out=xc3, in0=uv[j], scalar=cw[:, cc, j:j + 1], in1=xc3, op0=ALU.mult, op1=ALU.add),
                        reads=[ukey, "cw", "xc"], writes=["xc"])
                P.add("act", lambda e, xc=xc: e.copy(xcb[:, 0:n], xc), reads=["xc"], writes=["xcb"])
                P.add("pe", lambda e, cc=cc: e.matmul(psA[:, 0:n], lhsT=WaBD[:, cc, :], rhs=xcb[:, 0:n],
                                                      start=True, stop=True),
                      reads=["WaBD", "xcb"], writes=[("psA", 0)])
                P.add("pe", lambda e, cc=cc: e.matmul(psX[:, 0:n], lhsT=WxBD[:, cc, :], rhs=xcb[:, 0:n],
                                                      start=True, stop=True),
                      reads=["WxBD", "xcb"], writes=[("psX", 0)])
                rg = tmp[2][:, 0:n]
                ig = tmp[3][:, 0:n]
                P.add("act", lambda e, cc=cc, rg=rg: e.activation(out=rg, in_=psA[:, 0:n], func=AF.Sigmoid,
                                                                  bias=cvec[:, 1, cc:cc + 1], scale=1.0),
                      reads=[("psA", 0), "cvec"], writes=["rg"])
                P.add("act", lambda e, cc=cc, ig=ig: e.activation(out=ig, in_=psX[:, 0:n], func=AF.Sigmoid,
                                                                  bias=cvec[:, 2, cc:cc + 1], scale=1.0),
                      reads=[("psX", 0), "cvec"], writes=["ig"])
                a = tmp[4][:, 0:n]
                th = tmp[5][:, 0:n]
                P.add("act", lambda e, cc=cc, a=a, rg=rg: e.activation(out=a, in_=rg, func=AF.Exp,
                                                                       scale=lamc[:, cc:cc + 1]),
                      reads=["rg", "lamc"], writes=["a"])
                P.add("act", lambda e, cc=cc, th=th, rg=rg: e.activation(out=th, in_=rg, func=AF.Tanh,
                                                                         scale=lamc[:, cc:cc + 1]),
                      reads=["rg", "lamc"], writes=["th"])
                om = tmp[6][:, 0:n]
                P.add("dve", lambda e, th=th, om=om: e.tensor_scalar(out=om, in0=th, scalar1=-1.0, scalar2=1.0,
                                                                     op0=ALU.mult, op1=ALU.add),
                      reads=["th"], writes=["om"])
                P.add("dve", lambda e, om=om: e.reciprocal(om, om), reads=["om"], writes=["om"])
                P.add("dve", lambda e, th=th, om=om: e.scalar_tensor_tensor(
                    out=om, in0=th, scalar=-2.0, in1=om, op0=ALU.mult, op1=ALU.mult),
                    reads=["th", "om"], writes=["om"])
                P.add("act", lambda e, om=om: e.activation(out=om, in_=om, func=AF.Sqrt), reads=["om"],
                      writes=["om"])
                bt = tmp[7][:, 0:n]
                P.add("dve", lambda e, om=om, ig=ig, bt=bt: e.tensor_tensor(out=bt, in0=om, in1=ig, op=ALU.mult),
                      reads=["om", "ig"], writes=["bt"])
                P.add("dve", lambda e, xc=xc, bt=bt: e.tensor_tensor(out=bt, in0=bt, in1=xc, op=ALU.mult),
                      reads=["bt", "xc"], writes=["bt"])
                hs = tmp[2][:, 0:n]
                a3 = a.rearrange("p (b t) -> p b t", t=8)
                bt3 = bt.rearrange("p (b t) -> p b t", t=8)
                t3 = tmp[3][:, 0:NB]
                P.add("dve", lambda e, cc=cc, a3=a3, t3=t3: e.tensor_tensor(
                    out=t3, in0=a3[:, :, 0], in1=h0s[:, cc, :], op=ALU.mult),
                    reads=["a", "h0s", "bt"], writes=["ig"])
                P.add("dve", lambda e, bt3=bt3, t3=t3: e.tensor_tensor(
                    out=bt3[:, :, 0], in0=bt3[:, :, 0], in1=t3, op=ALU.add),
                    reads=["ig", "bt"], writes=["bt"])
                P.add("dve", lambda e, a3=a3: e.memset(a3[:, :, 0], 0.0), reads=["ig"], writes=["a"])
                P.add("dve", lambda e, a=a, bt=bt, hs=hs: e.tensor_tensor_scan(
                    out=hs, data0=a, data1=bt, initial=0.0, op0=ALU.mult, op1=ALU.add),
                    reads=["a", "bt", "th"], writes=["rg"])
                hs3 = hs.rearrange("p (b t) -> p b t", t=8)
                P.add("dve", lambda e, cc=cc, hs3=hs3: e.tensor_copy(hfin[:, cc, :], hs3[:, :, 7]),
                      reads=["rg"], writes=["hfin"])
                P.add("dve", lambda e, cc=cc, hs=hs: e.tensor_tensor(out=rs[:, cc, 0:n], in0=hs,
                                                                     in1=gt[:, cc, 0:n], op=ALU.mult),
                      reads=["rg", "gt"], writes=[("rst", i % 2)])
            dst = rnnT[:, :, t0:t0 + n].rearrange("c p t -> p c t")
            P.add("sp", lambda e, d=dst, o=rs[:, :, 0:n]: e.dma_start(out=d, in_=o),
                  reads=[("rst", i % 2)], stream="ro%d" % (i % 2))
            P.add("sp", lambda e: e.dma_start(out=io["lru_h_s"], in_=hfin), reads=["hfin"], stream="o_hs")

        tk = [0]

        def tok_outputs(i):
            t0, n = tl[i]
            is_s = (t0 >= S)
            if not is_s and t0 < S - WBUF:
                return
            ns = n // 128
            for s in range(ns):
                tt = t0 + s * 128
                last = (not is_s) and (tt + 128 == S)
                sects = [0, 1] + ([2] if (is_s or last) else [])
                for sec in sects:
                    j = tk[0]
                    tk[0] += 1
                    pz = psZ[j % NZ]
                    pk = ("psZ", j % NZ)
                    c0 = 512 * (1 + sec)
                    for kc in range(8):
                        P.add("pe", lambda e, kc=kc, pz=pz, s=s, c0=c0: e.matmul(
                            pz[:, :], lhsT=self.xnT[:, kc, s * 128:(s + 1) * 128], rhs=Win[:, kc, c0:c0 + 512],
                            start=(kc == 0), stop=(kc == 7)),
                            reads=["Win", "xnT"], writes=[pk])
                    tb = tok[j % 2]
                    tkey = ("tok", j % 2)
                    P.add("act", lambda e, tb=tb, pz=pz: e.copy(tb, pz[:, :]), reads=[pk], writes=[tkey])
                    if is_s:
                        if sec < 2:
                            dst = (io["knew"], io["vnew"])[sec]
                            P.add("sp", lambda e, d=dst, tb=tb: e.dma_start(out=d, in_=tb), reads=[tkey],
                                  stream="tk%d" % (j % 2))
                            if sec == 1:
                                P.add("dve", lambda e, tb=tb: e.tensor_copy(tokb, tb), reads=[tkey],
                                      writes=["tokb"])
                                P.add("sp", lambda e: e.dma_start(out=vtok, in_=tokb), reads=["tokb"],
                                      stream="tkb")
                        else:
                            for b in range(NB):
                                P.add("sp", lambda e, b=b, tb=tb: e.dma_start(
                                    out=io["conv_s"][b], in_=tb[b * 8 + 5:b * 8 + 8, :]),
                                    reads=[tkey], stream="tk%d" % (j % 2))
                    else:
                        if sec < 2:
                            r0 = tt - (S - WBUF)
                            dst = (io["kwin"], io["vwin"])[sec][r0:r0 + 128, :]
                            P.add("sp", lambda e, d=dst, tb=tb: e.dma_start(out=d, in_=tb), reads=[tkey],
                                  stream="tk%d" % (j % 2))
                        else:
                            P.add("sp", lambda e, tb=tb: e.dma_start(out=io["conv_p"], in_=tb[125:128, :]),
                                  reads=[tkey], stream="tk%d" % (j % 2))

        self.t_load(x1, tl, 0)
        if nt > 1:
            self.t_load(x1, tl, 1)
        self.t_prep(tl, 0)
        self.t_transposes(tl, 0)
        pending = []
        for i in range(nt):
            zproj(i, pending)
            pending = []
            tok_outputs(i)
            if i + 2 < nt:
                self.t_load(x1, tl, i + 2)
            if i + 1 < nt:
                self.t_prep(tl, i + 1)
                self.t_transposes(tl, i + 1)
            if tl[i][0] >= S:
                gelu_gate(i)
                lru_sample(i)
            else:
                pending = lru_prompt_steps(i)
        for f in pending:
            f()
        P.barrier()

    def attn_prompt_phase(self, zT, attnT, io):
        P = self.P
        S = self.S
        self.arena_off = self.arena_mark
        qT = self.alloc([128, S], BF16)
        kT = self.alloc([128, S], BF16)
        vT = self.alloc([128, S], BF16)
        acc = self.alloc([128, 2, S], F32)
        mask = self.alloc([128, 2, 256], BF16)
        onesf = self.alloc([128, 64], F32)
        pT = [self.alloc([128, 2, 256], BF16) for _ in range(3)]
        vblk = [self.alloc([128, 2, 65], BF16) for _ in range(3)]
        lnr = self.alloc([128, S], F32)
        ast = [self.alloc([128, 512], BF16), self.alloc([128, 512], BF16)]
        psS = [self.pb[0], self.pb[1], self.pb[2], self.pb[3]]
        psV = [self.pb[4].bitcast(BF16)[:, 0:128], self.pb[5].bitcast(BF16)[:, 0:128]]
        psO = [self.pb[6], self.pb[7]]
        psB = [self.pb[4], self.pb[4]]
        P.add("sp", lambda e: e.dma_start(out=mask, in_=io["maskp"]), writes=["mask"], stream="c_mask")
        P.add("dve", lambda e: e.memset(onesf, 1.0), writes=["onesf"])
        for v in range(3):
            P.add("dve", lambda e, v=v: e.memset(vblk[v][:, :, 64:65], 1.0), writes=[("vblk", v)])
        it = [0]
        for c in range(4):
            for nm, buf, ch in (("qT", qT, c), ("kT", kT, 4 + c), ("vT", vT, 8 + c)):
                for h0 in range(0, S, 4096):
                    h1 = min(S, h0 + 4096)
                    P.add("sp", lambda e, buf=buf, ch=ch, h0=h0, h1=h1: e.dma_start(
                        out=buf[:, h0:h1], in_=zT[ch, :, h0:h1]), writes=[nm], stream="ld_" + nm)
            P.add("pool", lambda e: e.memset(acc[0:65, :, :], 0.0), writes=["acc"])
            iters = []
            for d in (1, 4, 16):
                span = 128 * d
                nsb = S // span
                for nb in range(nsb):
                    nq = 256 if nb + 1 < nsb else 128
                    for r in range(d):
                        base = nb * span + r
                        iters.append((base, d, nq))

            def stage1(base, d, nq, j):
                ks = slice(base, base + d * 127 + 1, d)
                qs = slice(base, base + d * (nq - 1) + 1, d)
                s4, s3, s2 = j % 4, j % 3, j % 2
                P.add("pe", lambda e: e.matmul(psS[s4][:, 0:nq], lhsT=kT[0:64, ks], rhs=qT[0:64, qs],
                                               start=True, stop=True),
                      reads=["kT", "qT"], writes=[("psS", s4)])
                P.add("pe", lambda e: e.transpose(out=psV[s2], in_=vT[:, ks], identity=self.ident[:, :]),
                      reads=["vT", "ident"], writes=[("psV", s2)])
                P.add("pe", lambda e: e.matmul(psS[s4][:, 256:256 + nq], lhsT=kT[64:128, ks], rhs=qT[64:128, qs],
                                               start=True, stop=True),
                      reads=["kT", "qT"], writes=[("psS", s4)])
                sv = psS[s4].rearrange("p (e q) -> p e q", e=2)[:, :, 0:nq]
                P.add("act", lambda e: e.activation(out=pT[s3][:, :, 0:nq], in_=sv, func=AF.Exp),
                      reads=[("psS", s4)], writes=[("pT", s3, 0), ("pT", s3, 1)])
                P.add("pool", lambda e: e.tensor_tensor(
                    out=pT[s3][:, 0, 0:nq], in0=pT[s3][:, 0, 0:nq], in1=mask[:, 0, 0:nq], op=ALU.mult),
                    reads=[("pT", s3, 0), "mask"], writes=[("pT", s3, 0)])
                P.add("dve", lambda e: e.tensor_tensor(
                    out=pT[s3][:, 1, 0:nq], in0=pT[s3][:, 1, 0:nq], in1=mask[:, 1, 0:nq], op=ALU.mult),
                    reads=[("pT", s3, 1), "mask"], writes=[("pT", s3, 1)])
                P.add("act", lambda e: e.copy(vblk[s3][:, :, 0:64], psV[s2].rearrange("p (e c) -> p e c", e=2)),
                      reads=[("psV", s2)], writes=[("vblk", s3)])

            def stage2(base, d, nq, j):
                qs = slice(base, base + d * (nq - 1) + 1, d)
                s3, s2 = j % 3, j % 2
                for e2 in range(2):
                    P.add("pe", lambda e, e2=e2: e.matmul(
                        psO[s2][0:65, e2 * 256:e2 * 256 + nq], lhsT=vblk[s3][:, e2, :],
                        rhs=pT[s3][:, e2, 0:nq], start=True, stop=True),
                        reads=[("vblk", s3), ("pT", s3, e2)], writes=[("psO", s2)])
                av = acc[0:65, :, qs]
                ov = psO[s2][0:65, :].rearrange("p (e q) -> p e q", e=2)[:, :, 0:nq]
                P.add("dve", lambda e: e.tensor_tensor(out=av, in0=ov, in1=av, op=ALU.add),
                      reads=[("psO", s2), "acc"], writes=["acc"])

            SK = 2
            for idx, (base, d, nq) in enumerate(iters):
                stage1(base, d, nq, idx)
                if idx >= SK:
                    pbase, pd, pnq = iters[idx - SK]
                    stage2(pbase, pd, pnq, idx - SK)
            for idx in range(max(0, len(iters) - SK), len(iters)):
                pbase, pd, pnq = iters[idx]
                stage2(pbase, pd, pnq, idx)
            for e2 in range(2):
                P.add("act", lambda e, e2=e2: e.activation(out=lnr[64:65, :], in_=acc[64:65, e2, :], func=AF.Ln),
                      reads=["acc"], writes=["lnr"])
                P.add("act", lambda e, e2=e2: e.activation(out=acc[64:65, e2, :], in_=lnr[64:65, :], func=AF.Exp,
                                                           scale=-1.0),
                      reads=["lnr"], writes=["acc"])
            k = 0
            for e2 in range(2):
                for p0 in range(0, S, 512):
                    pbb = k % 2
                    k += 1
                    P.add("pe", lambda e, e2=e2, p0=p0, pbb=pbb: e.matmul(
                        psB[pbb][0:64, :], lhsT=onesf[64:65, 0:64], rhs=acc[64:65, e2, p0:p0 + 512],
                        start=True, stop=True),
                        reads=["onesf", "acc"], writes=[("psV", 0)])
                    P.add("dve", lambda e, e2=e2, p0=p0, pbb=pbb: e.tensor_tensor(
                        out=ast[pbb][0:64, :], in0=acc[0:64, e2, p0:p0 + 512], in1=psB[pbb][0:64, :],
                        op=ALU.mult),
                        reads=["acc", ("psV", 0)], writes=[("ast", pbb)])
                    P.add("sp", lambda e, e2=e2, p0=p0, pbb=pbb, c=c: e.dma_start(
                        out=attnT[c, e2 * 64:(e2 + 1) * 64, p0:p0 + 512], in_=ast[pbb][0:64, :]),
                        reads=[("ast", pbb)], stream="ao%d" % pbb)
        P.barrier()

    def attn_sample_phase(self, zT, vtok, attnT, io):
        P = self.P
        S, NB, NS = self.S, self.NB, self.NS
        self.arena_off = self.arena_mark
        NBLK = WBUF // 128
        stgk = [self.alloc([128, 8, 512], F32) for _ in range(4)]
        kb = self.alloc([128, NBLK, 512], BF16)
        kTs2 = [self.alloc([128, 4, WBUF + 8], BF16), self.alloc([128, 4, WBUF + 8], BF16)]
        vb = [self.alloc([128, NBLK + 1, 8, 65], BF16), self.alloc([128, NBLK + 1, 8, 65], BF16)]
        qTs = self.alloc([128, 4, NS], BF16)
        kTn = self.alloc([128, 4, NS], BF16)
        vnb = self.alloc([128, 512], BF16)
        masks = self.alloc([128, NBLK + 1, 8, 8], BF16)
        pTs = [self.alloc([128, NBLK + 1, 8, 8], BF16), self.alloc([128, NBLK + 1, 8, 8], BF16)]
        accs = self.alloc([128, 8, NS], F32)
        onesf = self.alloc([128, 64], F32)
        ast = [self.alloc([128, NS], BF16), self.alloc([128, NS], BF16)]
        psK = [self.pb[0].bitcast(BF16), self.pb[1].bitcast(BF16)]
        psMain = [self.pb[2], self.pb[3]]
        psNew = [self.pb[4], self.pb[5]]
        psO = [self.pb[6][:, 0:64], self.pb[7][:, 0:64]]
        psB = self.pb[6][:, 128:256]
        P.add("sp", lambda e: e.dma_start(out=masks, in_=io["masks"]), writes=["masks"], stream="c_masks")
        P.add("sp", lambda e: e.dma_start(out=qTs, in_=zT[0:4, :, S:S + NS].rearrange("c p t -> p c t")),
              writes=["qTs"], stream="c_qTs")
        P.add("sp", lambda e: e.dma_start(out=kTn, in_=zT[4:8, :, S:S + NS].rearrange("c p t -> p c t")),
              writes=["kTn"], stream="c_kTn")
        P.add("dve", lambda e: e.memset(onesf, 1.0), writes=["onesf"])
        for v in range(2):
            P.add("pool", lambda e, v=v: e.memset(vb[v][:, :, :, 64:65], 1.0), writes=[("vb", v)])
        sk = [0]

        def load_half(src, b, half):
            j = sk[0]
            sk[0] += 1
            slot = j % 4
            sv = src[b, half * 1024:(half + 1) * 1024, :].rearrange("(k p) d -> p k d", p=128)
            P.add("sp", lambda e, slot=slot, sv=sv: e.dma_start(out=stgk[slot], in_=sv),
                  writes=[("stgk", slot)], stream="sk%d" % slot)
            return slot

        def sview(blk, h):
            par, hh = h % 2, h // 2
            if blk < NBLK:
                off = blk * 32 + hh * 8
                return psMain[par][:, off:off + 8]
            return psNew[par][:, hh * 8:hh * 8 + 8]

        def prep(b):
            vv = vb[b % 2]
            vkey = ("vb", b % 2)
            kT_b = kTs2[b % 2]
            kkey = ("kTs", b % 2)
            for half in range(2):
                slot = load_half(io["kcache"], b, half)
                if half == 0:
                    P.add("dve", lambda e, slot=slot, half=half: e.tensor_copy(
                        kb[:, half * 8:(half + 1) * 8, :], stgk[slot]),
                        reads=[("stgk", slot)], writes=[("kb", half)])
                else:
                    P.add("act", lambda e, slot=slot, half=half: e.copy(
                        kb[:, half * 8:(half + 1) * 8, :], stgk[slot]),
                        reads=[("stgk", slot)], writes=[("kb", half)])
            for half in range(2):
                slot = load_half(io["vcache"], b, half)
                dstv = vv[:, half * 8:(half + 1) * 8, :, 0:64]
                srcv = stgk[slot].rearrange("p k (h c) -> p k h c", h=8)
                if half == 0:
                    P.add("act", lambda e, dstv=dstv, srcv=srcv: e.copy(dstv, srcv),
                          reads=[("stgk", slot)], writes=[vkey])
                else:
                    P.add("dve", lambda e, dstv=dstv, srcv=srcv: e.tensor_copy(dstv, srcv),
                          reads=[("stgk", slot)], writes=[vkey])
            P.add("sp", lambda e, b=b: e.dma_start(out=vnb[0:8, :], in_=vtok[b * 8:(b + 1) * 8, :]),
                  writes=["vnb"], stream="vn")
            P.add("dve", lambda e, vv=vv: e.tensor_copy(vv[0:8, NBLK, :, 0:64],
                                                        vnb[0:8, :].rearrange("p (h c) -> p h c", h=8)),
                  reads=["vnb"], writes=[vkey])
            P.add("pool", lambda e, b=b: e.tensor_copy(kT_b[:, :, WBUF:WBUF + 8], kTn[:, :, b * 8:(b + 1) * 8]),
                  reads=["kTn"], writes=[kkey])
            for blk in range(NBLK):
                pk = blk % 2
                for c in range(4):
                    P.add("pe", lambda e, blk=blk, c=c, pk=pk: e.transpose(
                        out=psK[pk][:, c * 128:(c + 1) * 128], in_=kb[:, blk, c * 128:(c + 1) * 128],
                        identity=self.ident[:, :]),
                        reads=[("kb", blk // 8), "ident"], writes=[("psK", pk)])
                src = psK[pk][:, 0:512].rearrange("p (c k) -> p c k", c=4)
                dst = kT_b[:, :, blk * 128:(blk + 1) * 128]
                if blk % 2 == 0:
                    P.add("act", lambda e, d=dst, s=src: e.copy(d, s), reads=[("psK", pk)], writes=[kkey])
                else:
                    P.add("dve", lambda e, d=dst, s=src: e.tensor_copy(d, s), reads=[("psK", pk)],
                          writes=[kkey])

        def scores(b):
            kT_b = kTs2[b % 2]
            kkey = ("kTs", b % 2)
            for blk in range(NBLK + 1):
                nk = 128 if blk < NBLK else 8
                for hh in range(4):
                    for par in range(2):
                        h = 2 * hh + par
                        rows = slice(par * 64, par * 64 + 64)
                        P.add("pe", lambda e, blk=blk, h=h, rows=rows, nk=nk: e.matmul(
                            sview(blk, h)[0:nk, :], lhsT=kT_b[rows, h // 2, blk * 128:blk * 128 + nk],
                            rhs=qTs[rows, h // 2, b * 8:(b + 1) * 8], start=True, stop=True),
                            reads=[kkey, "qTs"], writes=["psSs"])

        def softmax_pv(b):
            vv = vb[b % 2]
            vkey = ("vb", b % 2)
            pp = pTs[b % 2]
            pkey = ("pTs", b % 2)
            for par in range(2):
                P.add("act", lambda e, par=par, pp=pp: e.activation(
                    out=pp[:, 0:NBLK, par::2, :],
                    in_=psMain[par].rearrange("p (k h t) -> p k h t", k=NBLK, h=4), func=AF.Exp),
                    reads=["psSs"], writes=[pkey])
                P.add("act", lambda e, par=par, pp=pp: e.activation(
                    out=pp[0:8, NBLK, par::2, :],
                    in_=psNew[par][0:8, 0:32].rearrange("p (h t) -> p h t", h=4), func=AF.Exp),
                    reads=["psSs"], writes=[pkey])
            P.add("dve", lambda e, pp=pp: e.tensor_tensor(out=pp[:, 0:NBLK, :, :], in0=pp[:, 0:NBLK, :, :],
                                                          in1=masks[:, 0:NBLK, :, :], op=ALU.mult),
                  reads=[pkey, "masks"], writes=[pkey])
            P.add("dve", lambda e, pp=pp: e.tensor_tensor(out=pp[0:8, NBLK, :, :], in0=pp[0:8, NBLK, :, :],
                                                          in1=masks[0:8, NBLK, :, :], op=ALU.mult),
                  reads=[pkey, "masks"], writes=[pkey])
            po = psO[b % 2]
            for h in range(8):
                for blk in range(NBLK + 1):
                    nk = 128 if blk < NBLK else 8
                    P.add("pe", lambda e, blk=blk, h=h, nk=nk, po=po, vv=vv, pp=pp: e.matmul(
                        po[0:65, h * 8:(h + 1) * 8], lhsT=vv[0:nk, blk, h, :], rhs=pp[0:nk, blk, h, :],
                        start=(blk == 0), stop=(blk == NBLK)),
                        reads=[vkey, pkey], writes=[("psOs", b % 2)])
            P.add("dve", lambda e, b=b, po=po: e.tensor_copy(
                accs[0:65, :, b * 8:(b + 1) * 8], po[0:65, :].rearrange("p (h t) -> p h t", h=8)),
                reads=[("psOs", b % 2)], writes=["accs"])

        prep(0)
        for b in range(NB):
            scores(b)
            if b + 1 < NB:
                prep(b + 1)
            softmax_pv(b)
        for h in range(8):
            P.add("dve", lambda e, h=h: e.reciprocal(accs[64:65, h, :], accs[64:65, h, :]),
                  reads=["accs"], writes=["accs"])
            P.add("pe", lambda e, h=h: e.matmul(psB[0:64, 0:NS], lhsT=onesf[64:65, 0:64], rhs=accs[64:65, h, :],
                                                start=True, stop=True),
                  reads=["onesf", "accs"], writes=[("psOs", 0)])
            P.add("dve", lambda e, h=h: e.tensor_tensor(out=ast[h % 2][0:64, :], in0=accs[0:64, h, :],
                                                        in1=psB[0:64, 0:NS], op=ALU.mult),
                  reads=["accs", ("psOs", 0)], writes=[("ast", h % 2)])
            P.add("sp", lambda e, h=h: e.dma_start(out=attnT[h // 2, (h % 2) * 64:(h % 2) * 64 + 64, S:S + NS],
                                                   in_=ast[h % 2][0:64, :]),
                  reads=[("ast", h % 2)], stream="aso%d" % (h % 2))
        P.barrier()

    def mix_out_phase(self, x1, attnT, rnnT, x2, io):
        P = self.P
        self.arena_off = self.arena_mark
        self.common_tile_bufs()
        WoA = self.alloc([128, 4, D], BF16)
        WoR = self.alloc([128, 4, D], BF16)
        at = [self.alloc([128, 4, TT], BF16), self.alloc([128, 4, TT], BF16)]
        rt = [self.alloc([128, 4, TT], BF16), self.alloc([128, 4, TT], BF16)]
        psD = [self.pb[4], self.pb[5]]
        wo = io["w_out"]
        for cc in range(4):
            self.load_cast(WoA[:, cc, :], wo[cc * 128:(cc + 1) * 128, :], D, 1.0, "WoA", cc)
        for cc in range(4):
            self.load_cast(WoR[:, cc, :], wo[512 + cc * 128:512 + (cc + 1) * 128, :], D, 1.0, "WoR", cc)
        tl = self.tiles()
        nt = len(tl)

        def load(i):
            t0, n = tl[i]
            slot = i % 2
            self.t_load(x1, tl, i)
            P.add("sp", lambda e: e.dma_start(out=at[slot][:, :, 0:n],
                                              in_=attnT[:, :, t0:t0 + n].rearrange("c p t -> p c t")),
                  writes=[("at", slot)], stream="at%d" % slot)
            P.add("sp", lambda e: e.dma_start(out=rt[slot][:, :, 0:n],
                                              in_=rnnT[:, :, t0:t0 + n].rearrange("c p t -> p c t")),
                  writes=[("rt", slot)], stream="rt%d" % slot)

        load(0)
        if nt > 1:
            load(1)
        j = 0
        for i in range(nt):
            t0, n = tl[i]
            ns = n // 128
            slot = i % 2
            xs = self.xbuf[slot]
            for s in range(ns):
                for hf in range(2):
                    pd = psD[j % 2]
                    pk = ("psD", j % 2)
                    j += 1
                    for cc in range(4):
                        P.add("pe", lambda e, cc=cc, s=s, hf=hf, pd=pd, slot=slot: e.matmul(
                            pd[:, :], lhsT=at[slot][:, cc, s * 128:(s + 1) * 128],
                            rhs=WoA[:, cc, hf * 512:(hf + 1) * 512], start=(cc == 0), stop=False),
                            reads=["WoA", ("at", slot)], writes=[pk])
                    for cc in range(4):
                        P.add("pe", lambda e, cc=cc, s=s, hf=hf, pd=pd, slot=slot: e.matmul(
                            pd[:, :], lhsT=rt[slot][:, cc, s * 128:(s + 1) * 128],
                            rhs=WoR[:, cc, hf * 512:(hf + 1) * 512], start=False, stop=(cc == 3)),
                            reads=["WoR", ("rt", slot)], writes=[pk])
                    P.add("dve", lambda e, s=s, hf=hf, pd=pd, xs=xs: e.tensor_tensor(
                        out=xs[:, s, hf * 512:(hf + 1) * 512], in0=pd[:, :],
                        in1=xs[:, s, hf * 512:(hf + 1) * 512], op=ALU.add),
                        reads=[pk, ("x", slot)], writes=[("x", slot)])
            dst = x2[t0:t0 + n, :].rearrange("(s p) d -> p s d", p=128)
            P.add("sp", lambda e, d=dst, o=xs[:, 0:ns, :]: e.dma_start(out=d, in_=o),
                  reads=[("x", slot)], stream="xo%d" % slot)
            if i + 2 < nt:
                load(i + 2)
        P.barrier()

    def build(self):
        nc = self.nc
        P = self.P
        NT, S, NB, NS = self.NT, self.S, self.NB, self.NS
        xin = self.dram_in("xin", [NT, D])
        w1g = self.dram_in("w1g", [D, DFF]); w1u = self.dram_in("w1u", [D, DFF]); w1d = self.dram_in("w1d", [DFF, D])
        w2g = self.dram_in("w2g", [D, DFF]); w2u = self.dram_in("w2u", [D, DFF]); w2d = self.dram_in("w2d", [DFF, D])
        gains_d = self.dram_in("gains", [128, 24])
        lnf_d = self.dram_in("lnf", [D])
        ident_d = self.dram_in("ident", [128, 128], BF16)
        io = {}
        io["w_in"] = self.dram_in("w_in", [D, INC])
        io["w_out"] = self.dram_in("w_out", [D, D])
        io["wabd"] = self.dram_in("wabd", [128, 4, 128])
        io["wxbd"] = self.dram_in("wxbd", [128, 4, 128])
        io["cw"] = self.dram_in("cw", [128, 4, 4])
        io["cvec"] = self.dram_in("cvec", [128, 4, 4])
        io["convst"] = self.dram_in("convst", [128, 4, NB, 3])
        io["h0"] = self.dram_in("h0", [128, 4, NB])
        io["kcache"] = self.dram_in("kcache", [NB, WBUF, 512])
        io["vcache"] = self.dram_in("vcache", [NB, WBUF, 512])
        io["maskp"] = self.dram_in("maskp", [128, 2, 256], BF16)
        io["masks"] = self.dram_in("masks", [128, 17, 8, 8], BF16)
        y = self.dram_out("y", [NT, D])
        io["kwin"] = self.dram_out("kwin", [WBUF, 512])
        io["vwin"] = self.dram_out("vwin", [WBUF, 512])
        io["lru_h_p"] = self.dram_out("lru_h_p", [128, 4])
        io["conv_p"] = self.dram_out("conv_p", [3, 512])
        io["knew"] = self.dram_out("knew", [NS, 512])
        io["vnew"] = self.dram_out("vnew", [NS, 512])
        io["lru_h_s"] = self.dram_out("lru_h_s", [128, 4, NB])
        io["conv_s"] = self.dram_out("conv_s", [NB, 3, 512])
        x1 = self.dram_tmp("x1", [NT, D], F32)
        x2 = self.dram_tmp("x2", [NT, D], F32)
        zT = self.dram_tmp("zT", [12, 128, NT], BF16)
        rnnT = self.dram_tmp("rnnT", [4, 128, NT], BF16)
        attnT = self.dram_tmp("attnT", [4, 128, NT], BF16)
        vtok = self.dram_tmp("vtok", [NS, 512], BF16)
        with ExitStack() as stack:
            self.stack = stack
            self.arena = stack.enter_context(nc.sbuf_tensor("arena", [128, ARENA_WORDS], F32))
            self.pbig = stack.enter_context(nc.psum_tensor("pbig", [128, 4096], F32))
            self.pb = [self.pbig[:, i * 512:(i + 1) * 512] for i in range(8)]
            self.arena_off = 0
            self.gains = self.alloc([128, 24], F32)
            self.lnf = self.alloc([128, D], F32)
            self.ident = self.alloc([128, 128], BF16)
            self.epsc = self.alloc([128, 1], F32)
            self.arena_mark = self.arena_off

            P.add("dve", lambda e: e.memset(self.epsc, EPS), writes=["epsc"])
            P.add("sp", lambda e: e.dma_start(out=self.gains, in_=gains_d), writes=["gains"], stream="c_gains")
            P.add("sp", lambda e: e.dma_start(out=self.lnf, in_=_bcast_rows(lnf_d, 128)), writes=["lnf"],
                  stream="c_lnf")
            P.add("sp", lambda e: e.dma_start(out=self.ident, in_=ident_d), writes=["ident"], stream="c_ident")
            P.barrier()

            src = xin
            if "ffn1" in self.phases:
                self.ffn_phase("f1", src, x1, w1g, w1u, w1d, 0, final_norm=False)
                src = x1
            ph = self.phases
            if "mix" in ph or "mix_in" in ph:
                self.mix_in_phase(src, zT, rnnT, vtok, io)
            if "mix" in ph or "attn_p" in ph:
                self.attn_prompt_phase(zT, attnT, io)
            if "mix" in ph or "attn_s" in ph:
                self.attn_sample_phase(zT, vtok, attnT, io)
            if "mix" in ph or "mix_out" in ph:
                self.mix_out_phase(src, attnT, rnnT, x2, io)
                src = x2
            if "ffn2" in self.phases:
                self.ffn_phase("f2", src, y, w2g, w2u, w2d, 2, final_norm=True)

            P.finalize_outputs()
            P.emit(nc, lambda name: stack.enter_context(nc.semaphore(name)))
        return nc


def _fm(v):
    return np.ascontiguousarray(np.asarray(v, np.float32).reshape(-1, 128).T)


def _blockdiag(w):
    out = np.zeros((128, 4, 128), np.float32)
    for c in range(4):
        out[0:64, c, 0:64] = w[2 * c]
        out[64:128, c, 64:128] = w[2 * c + 1]
    return out


def _mask_prompt():
    j = np.arange(128)[:, None]
    i = np.arange(128)[None, :]
    m = np.concatenate([(j <= i), (j >= i)], axis=1).astype(np.float32)
    return np.ascontiguousarray(np.broadcast_to(m[:, None, :], (128, 2, 256))).astype(ml_dtypes.bfloat16)


def _mask_sample():
    m = np.zeros((128, 17, 8, 8), np.float32)
    p = np.arange(128)
    for blk in range(17):
        for t in range(8):
            s = blk * 128 + p
            dd = WBUF + t - s
            mult = ((dd >= 0) & (dd <= 128)).astype(np.float32)
            mult += ((dd >= 0) & (dd % 4 == 0) & (dd <= 512))
            mult += ((dd >= 0) & (dd % 16 == 0) & (dd <= 2048))
            if blk == 16:
                mult = np.where(p < 8, mult, 0.0)
            m[:, blk, :, t] = mult[:, None]
    return m.astype(ml_dtypes.bfloat16)


def make_in_maps(inputs, n_cores, S, NB):
    f = lambda a: np.asarray(a, np.float32)
    xp, xs = f(inputs["x_prompt"]), f(inputs["x_sample"])
    shared = dict(
        w1g=f(inputs["w_ffn1_gate"])[0], w1u=f(inputs["w_ffn1_up"])[0], w1d=f(inputs["w_ffn1_down"])[0],
        w2g=f(inputs["w_ffn2_gate"])[0], w2u=f(inputs["w_ffn2_up"])[0], w2d=f(inputs["w_ffn2_down"])[0],
        gains=np.ascontiguousarray(np.concatenate(
            [_fm(inputs["ln_ffn1"][0]), _fm(inputs["ln_mix"][0]), _fm(inputs["ln_ffn2"][0])], axis=1)),
        lnf=f(inputs["ln_final"]),
        ident=np.eye(128, dtype=np.float32).astype(ml_dtypes.bfloat16),
        w_in=f(inputs["w_in"])[0], w_out=f(inputs["w_out"])[0],
        wabd=_blockdiag(f(inputs["w_gate_a"])[0]), wxbd=_blockdiag(f(inputs["w_gate_x"])[0]),
        cw=np.ascontiguousarray(f(inputs["conv_w"])[0].reshape(4, 4, 128).transpose(2, 1, 0)),
        cvec=np.ascontiguousarray(np.stack(
            [_fm(inputs["conv_b"][0]), _fm(f(inputs["b_gate_a"])[0].reshape(-1)),
             _fm(f(inputs["b_gate_x"])[0].reshape(-1)), _fm(inputs["lru_lambda"][0])], axis=1)),
        maskp=_mask_prompt(), masks=_mask_sample(),
    )
    maps = []
    for c in range(n_cores):
        bs = slice(c * NB, (c + 1) * NB)
        m = dict(shared)
        m["xin"] = np.ascontiguousarray(np.concatenate([xp[c, :S], xs[bs].reshape(NB * 8, D)], axis=0))
        cs = f(inputs["state_lru_conv"])[0, bs]
        m["convst"] = np.ascontiguousarray(cs.reshape(NB, 3, 4, 128).transpose(3, 2, 0, 1))
        hh = f(inputs["state_lru_h"])[0, bs]
        m["h0"] = np.ascontiguousarray(hh.reshape(NB, 4, 128).transpose(2, 1, 0))
        m["kcache"] = np.ascontiguousarray(f(inputs["cache_k_win"])[0, bs].reshape(NB, WBUF, 512))
        m["vcache"] = np.ascontiguousarray(f(inputs["cache_v_win"])[0, bs].reshape(NB, WBUF, 512))
        maps.append(m)
    return maps


def assemble(results, n_cores, S, NB):
    ys = [r["y"] for r in results]
    y_prompt = np.stack([y[:S] for y in ys])
    y_sample = np.concatenate([y[S:].reshape(NB, 8, D) for y in ys])
    kwin = np.stack([r["kwin"].reshape(WBUF, NH, HD) for r in results])[None]
    vwin = np.stack([r["vwin"].reshape(WBUF, NH, HD) for r in results])[None]
    hp = np.stack([r["lru_h_p"].T.reshape(DR) for r in results])[None]
    cp = np.stack([r["conv_p"] for r in results])[None]
    knew = np.concatenate([r["knew"].reshape(NB, 8, NH, HD) for r in results])[None]
    vnew = np.concatenate([r["vnew"].reshape(NB, 8, NH, HD) for r in results])[None]
    hs = np.concatenate([r["lru_h_s"].transpose(2, 1, 0).reshape(NB, DR) for r in results])[None]
    cs = np.concatenate([r["conv_s"] for r in results])[None]
    outs = (y_prompt, y_sample, kwin, vwin, hp, cp, knew, vnew, hs, cs)
    return tuple(np.ascontiguousarray(o, dtype=np.float32) for o in outs)


_NC_CACHE = {}


def kernel(**inputs):
    n_cores = 8
    S = inputs["x_prompt"].shape[1]
    NB = inputs["x_sample"].shape[0] // n_cores
    key = (S, NB)
    if key not in _NC_CACHE:
        _NC_CACHE[key] = Builder(S, NB).build()
    nc = _NC_CACHE[key]
    maps = make_in_maps(inputs, n_cores, S, NB)
    res = run_bass_kernel_spmd(nc, maps, core_ids=list(range(n_cores)))
    return assemble(res.results, n_cores, S, NB)
```

```python
from contextlib import ExitStack
import numpy as np
import ml_dtypes
import concourse.bass as bass
import concourse.mybir as mybir
from concourse.bass_utils import run_bass_kernel_spmd

F32 = mybir.dt.float32
BF16 = mybir.dt.bfloat16
AF = mybir.ActivationFunctionType
ALU = mybir.AluOpType

D = 1024
DFF = 2816
NFC = DFF // 128
NH = 8
HD = 64
DR = 512
INC = 2560
EPS = 1e-6
SEM_CAP = 30000
TT = 256
NSUB = TT // 128
WBUF = 2048
ARENA_WORDS = 51 * 1024
GELU_C = 1.5957691216057308


class _Op:
    __slots__ = ("eng", "fn", "reads", "writes", "stream", "is_dma", "deps", "signal", "sig")

    def __init__(self, eng, fn, reads, writes, stream):
        self.eng = eng
        self.fn = fn
        self.reads = tuple(reads)
        self.writes = tuple(writes)
        self.stream = stream
        self.is_dma = stream is not None
        self.deps = []
        self.signal = False
        self.sig = None


class Prog:
    ENGS = ("pe", "act", "dve", "pool", "sp")

    def __init__(self):
        self.ops = []
        self.res_w = {}
        self.res_r = {}
        self.bar_deps = []
        self.bar_pending = set()

    def barrier(self):
        last = {}
        for i, op in enumerate(self.ops):
            last[("dma", op.stream) if op.is_dma else ("eng", op.eng)] = i
        self.bar_deps = sorted(last.values())
        self.bar_pending = set(self.ENGS)
        self.res_w = {}
        self.res_r = {}

    def add(self, eng, fn, reads=(), writes=(), stream=None):
        op = _Op(eng, fn, reads, writes, stream)
        idx = len(self.ops)
        raw = set()
        other = set()
        for r in op.reads:
            raw.update(self.res_w.get(r, ()))
        for w in op.writes:
            other.update(self.res_w.get(w, ()))
            other.update(self.res_r.get(w, ()))
        deps = set()
        for d in raw | other:
            dop = self.ops[d]
            if dop.is_dma or op.is_dma or dop.eng != op.eng:
                deps.add(d)
            elif d in raw:
                deps.add(d)
        if eng in self.bar_pending:
            self.bar_pending.discard(eng)
            for d in self.bar_deps:
                dop = self.ops[d]
                if dop.is_dma or dop.eng != eng:
                    deps.add(d)
        op.deps = sorted(deps)
        for d in op.deps:
            self.ops[d].signal = True
        for r in op.reads:
            self.res_r.setdefault(r, []).append(idx)
        for w in op.writes:
            if self.res_r.get(w):
                self.res_w[w] = [idx]
                self.res_r[w] = []
            else:
                self.res_w.setdefault(w, []).append(idx)
        self.ops.append(op)
        return idx

    def finalize_outputs(self):
        last = {}
        for i, op in enumerate(self.ops):
            if op.is_dma:
                last[op.stream] = i
        for op in self.ops:
            if op.is_dma:
                op.signal = True
        self.final_dma = sorted(last.values())

    def emit(self, nc, sem_ctx):
        cnt = {}
        for op in self.ops:
            if not op.signal:
                continue
            key = ("dma", op.stream) if op.is_dma else ("eng", op.eng)
            c = cnt.get(key, 0)
            epoch, within = divmod(c, SEM_CAP if not op.is_dma else SEM_CAP // 16)
            cnt[key] = c + 1
            semname = "%s_%s_%d" % (key[0], key[1], epoch)
            val = (within + 1) * (16 if op.is_dma else 1)
            op.sig = (semname, val)
        per_eng = {e: [] for e in self.ENGS}
        for i, op in enumerate(self.ops):
            per_eng[op.eng].append(i)
        sems = {}

        def sem(name):
            if name not in sems:
                sems[name] = sem_ctx(name)
            return sems[name]

        for op in self.ops:
            if op.sig is not None:
                sem(op.sig[0])
        final_dma = self.final_dma

        with nc.Block() as block:
            def body(ename):
                def _run(engine):
                    seen = {}

                    def wait(i):
                        sname, val = self.ops[i].sig
                        if seen.get(sname, 0) >= val:
                            return
                        seen[sname] = val
                        engine.wait_ge(sem(sname), val)

                    for i in per_eng[ename]:
                        op = self.ops[i]
                        for d in op.deps:
                            wait(d)
                        ins = op.fn(engine)
                        if op.sig is not None:
                            ins.then_inc(sem(op.sig[0]), 16 if op.is_dma else 1)
                    if ename == "sp":
                        for i in final_dma:
                            wait(i)
                return _run

            block.tensor(body("pe"))
            block.scalar(body("act"))
            block.vector(body("dve"))
            block.gpsimd(body("pool"))
            block.sync(body("sp"))


def _bcast_rows(ap1d, nparts):
    n = ap1d.shape[-1]
    return bass.AP(ap1d.tensor, ap1d.offset, [[0, nparts], [1, n]])


class Builder:
    def __init__(self, S, NB, phases=("ffn1", "mix", "ffn2")):
        assert S % 2048 == 0 and (NB * 8) % 128 == 0
        self.S = S
        self.NB = NB
        self.NS = NB * 8
        self.NT = S + self.NS
        self.phases = phases
        self.nc = bass.Bass("TRN2", target_bir_lowering=False)
        self.P = Prog()
        self.stack = None
        self._stg_i = 0

    def dram_in(self, name, shape, dt=F32):
        return self.nc.dram_tensor(name, list(shape), dt, kind="ExternalInput").ap()

    def dram_out(self, name, shape, dt=F32):
        return self.nc.dram_tensor(name, list(shape), dt, kind="ExternalOutput").ap()

    def dram_tmp(self, name, shape, dt):
        return self.nc.dram_tensor(name, list(shape), dt, kind="Internal").ap()

    def alloc(self, shape, dt):
        esz = 4 if dt == F32 else 2
        n = 1
        for s in shape[1:]:
            n *= s
        nbytes = (n * esz + 31) // 32 * 32
        w0 = self.arena_off
        nw = nbytes // 4
        assert w0 + nw <= ARENA_WORDS, "SBUF arena overflow: %d" % (w0 + nw)
        self.arena_off = w0 + nw
        v = self.arena[:, w0:w0 + nw]
        if dt != F32:
            v = v.bitcast(dt)
        v = v[:, 0:n]
        if len(shape) == 3:
            v = v.rearrange("p (a b) -> p a b", a=shape[1])
        elif len(shape) == 4:
            v = v.rearrange("p (a b c) -> p a b c", a=shape[1], b=shape[2])
        return v

    def tiles(self):
        out = []
        t = 0
        while t < self.S:
            n = min(TT, self.S - t)
            out.append((t, n))
            t += n
        if self.NS:
            out.append((self.S, self.NS))
        return out

    def load_cast(self, dst_ap, src_ap, width, scale, key, idx):
        P = self.P
        slot = self._stg_i % 2
        self._stg_i += 1
        np_ = dst_ap.shape[0]
        view = self.stg[slot][0:np_, 0:width]
        if len(dst_ap.shape) == 3:
            view = view.rearrange("p (a b) -> p a b", a=dst_ap.shape[1])
        P.add("sp", lambda e, v=view, s=src_ap: e.dma_start(out=v, in_=s),
              writes=[("stg", slot)], stream="stg%d" % slot)
        rd = [("stg", slot)] + ([] if isinstance(scale, float) else ["gains"])
        if idx % 2 == 1:
            P.add("act", lambda e, o=dst_ap, v=view, sc=scale: e.mul(o, v, sc), reads=rd, writes=[key])
        else:
            P.add("dve", lambda e, o=dst_ap, v=view, sc=scale: e.tensor_scalar_mul(o, v, sc), reads=rd, writes=[key])

    def t_load(self, x_src, tl, i):
        t0, n = tl[i]
        ns = n // 128
        slot = i % 2
        src = x_src[t0:t0 + n, :].rearrange("(s p) d -> p s d", p=128)
        self.P.add("sp", lambda e, o=self.xbuf[slot][:, 0:ns, :], s=src: e.dma_start(out=o, in_=s),
                   writes=[("x", slot)], stream="x%d" % slot)

    def t_prep(self, tl, i):
        P = self.P
        t0, n = tl[i]
        ns = n // 128
        slot = i % 2
        xs = self.xbuf[slot]
        for s in range(ns):
            P.add("act", lambda e, s=s, xs=xs: e.activation(
                out=self.junk[:, :], in_=xs[:, s, :], func=AF.Square, scale=1.0 / 32.0,
                accum_out=self.ss[:, s:s + 1]),
                reads=[("x", slot)], writes=["junk", "ss"])
        P.add("act", lambda e: e.activation(out=self.rstd[:, 0:ns], in_=self.ss[:, 0:ns], func=AF.Sqrt,
                                            bias=self.epsc[:, 0:1], scale=1.0),
              reads=["ss", "epsc"], writes=["rstd"])
        P.add("dve", lambda e: e.reciprocal(self.rstd[:, 0:ns], self.rstd[:, 0:ns]),
              reads=["rstd"], writes=["rstd"])
        for s in range(ns):
            if s % 2 == 0:
                P.add("dve", lambda e, s=s, xs=xs: e.tensor_scalar_mul(self.xn[:, s, :], xs[:, s, :],
                                                                       self.rstd[:, s:s + 1]),
                      reads=[("x", slot), "rstd"], writes=[("xn", s)])
            else:
                P.add("act", lambda e, s=s, xs=xs: e.mul(self.xn[:, s, :], xs[:, s, :], self.rstd[:, s:s + 1]),
                      reads=[("x", slot), "rstd"], writes=[("xn", s)])

    def t_transposes(self, tl, i):
        P = self.P
        t0, n = tl[i]
        ns = n // 128
        for s in range(ns):
            bank = self.psT[s % 2]
            for kc in range(8):
                P.add("pe", lambda e, s=s, kc=kc, bank=bank: e.transpose(
                    out=bank[:, kc * 128:(kc + 1) * 128], in_=self.xn[:, s, kc * 128:(kc + 1) * 128],
                    identity=self.ident[:, :]),
                    reads=[("xn", s), "ident"], writes=[("psT", s % 2)])
            src = bank[:, :].rearrange("p (k t) -> p k t", k=8)
            dst = self.xnT[:, :, s * 128:(s + 1) * 128]
            if s % 2 == 0:
                P.add("act", lambda e, d=dst, sr=src: e.copy(d, sr), reads=[("psT", s % 2)], writes=["xnT"])
            else:
                P.add("dve", lambda e, d=dst, sr=src: e.tensor_copy(d, sr), reads=[("psT", s % 2)],
                      writes=["xnT"])

    def common_tile_bufs(self):
        self.xbuf = [self.alloc([128, NSUB, D], F32), self.alloc([128, NSUB, D], F32)]
        self.xn = self.alloc([128, NSUB, D], BF16)
        self.xnT = self.alloc([128, 8, TT], BF16)
        self.junk = self.alloc([128, D], BF16)
        self.ss = self.alloc([128, 4], F32)
        self.rstd = self.alloc([128, 4], F32)
        self.ss2 = self.alloc([128, 4], F32)
        self.rstd2 = self.alloc([128, 4], F32)
        self.stg = [self.alloc([128, DFF // 2], F32), self.alloc([128, DFF // 2], F32)]

    def ffn_phase(self, tag, x_src, x_dst, wg, wu, wd, gcol, final_norm):
        P = self.P
        self.arena_off = self.arena_mark
        self.common_tile_bufs()
        self.wA = self.alloc([128, 8, DFF], BF16)
        self.wB = self.alloc([128, 8, DFF], BF16)
        self.wC = self.alloc([128, NFC, D], BF16)
        self.aT = self.alloc([128, NFC, TT], BF16)
        self.sg = [self.alloc([128, TT], F32), self.alloc([128, TT], F32)]
        psG = [self.pb[0], self.pb[1]]
        psU = [self.pb[2], self.pb[3]]
        psD = [self.pb[4], self.pb[5]]
        self.psT = [self.pb[6].bitcast(BF16), self.pb[7].bitcast(BF16)]
        HW = DFF // 2
        k = 0
        for kc in range(8):
            for hh in range(2):
                self.load_cast(self.wA[:, kc, hh * HW:(hh + 1) * HW],
                               wg[kc * 128:(kc + 1) * 128, hh * HW:(hh + 1) * HW],
                               HW, self.gains[:, gcol * 8 + kc:gcol * 8 + kc + 1], "wA", k)
                k += 1
        for kc in range(8):
            for hh in range(2):
                self.load_cast(self.wB[:, kc, hh * HW:(hh + 1) * HW],
                               wu[kc * 128:(kc + 1) * 128, hh * HW:(hh + 1) * HW],
                               HW, self.gains[:, gcol * 8 + kc:gcol * 8 + kc + 1], "wB", k)
                k += 1
        for fc in range(NFC):
            self.load_cast(self.wC[:, fc, :], wd[fc * 128:(fc + 1) * 128, :], D, 0.5, "wC", k)
            k += 1
        tl = self.tiles()
        xbuf = self.xbuf

        def gate_up(i):
            t0, n = tl[i]
            for fc in range(NFC):
                pb = fc % 2
                pg, pu = psG[pb], psU[pb]
                for kc in range(8):
                    P.add("pe", lambda e, fc=fc, kc=kc, pg=pg: e.matmul(
                        pg[:, 0:n], lhsT=self.wA[:, kc, fc * 128:(fc + 1) * 128], rhs=self.xnT[:, kc, 0:n],
                        start=(kc == 0), stop=(kc == 7)),
                        reads=["wA", "xnT"], writes=[("psG", pb)])
                for kc in range(8):
                    P.add("pe", lambda e, fc=fc, kc=kc, pu=pu: e.matmul(
                        pu[:, 0:n], lhsT=self.wB[:, kc, fc * 128:(fc + 1) * 128], rhs=self.xnT[:, kc, 0:n],
                        start=(kc == 0), stop=(kc == 7)),
                        reads=["wB", "xnT"], writes=[("psU", pb)])
                sg = self.sg[pb]
                P.add("act", lambda e, pg=pg, sg=sg: e.activation(out=sg[:, 0:n], in_=pg[:, 0:n], func=AF.Silu),
                      reads=[("psG", pb)], writes=[("sg", pb)])
                P.add("dve", lambda e, fc=fc, pu=pu, sg=sg: e.tensor_tensor(
                    out=self.aT[:, fc, 0:n], in0=sg[:, 0:n], in1=pu[:, 0:n], op=ALU.mult),
                    reads=[("psU", pb), ("sg", pb)], writes=[("aT", fc)])

        def down(i):
            t0, n = tl[i]
            ns = n // 128
            slot = i % 2
            xs = xbuf[slot]
            j = 0
            for s in range(ns):
                for h in range(2):
                    pb = j % 2
                    j += 1
                    pd = psD[pb]
                    for fc in range(NFC):
                        P.add("pe", lambda e, fc=fc, s=s, h=h, pd=pd: e.matmul(
                            pd[:, :], lhsT=self.aT[:, fc, s * 128:(s + 1) * 128],
                            rhs=self.wC[:, fc, h * 512:(h + 1) * 512], start=(fc == 0), stop=(fc == NFC - 1)),
                            reads=["wC", ("aT", fc)], writes=[("psD", pb)])
                    P.add("dve", lambda e, s=s, h=h, pd=pd, xs=xs: e.tensor_tensor(
                        out=xs[:, s, h * 512:(h + 1) * 512], in0=pd[:, :], in1=xs[:, s, h * 512:(h + 1) * 512],
                        op=ALU.add),
                        reads=[("psD", pb), ("x", slot)], writes=[("x", slot)])
            if final_norm:
                for s in range(ns):
                    P.add("act", lambda e, s=s, xs=xs: e.activation(
                        out=self.junk[:, :], in_=xs[:, s, :], func=AF.Square, scale=1.0 / 32.0,
                        accum_out=self.ss2[:, s:s + 1]),
                        reads=[("x", slot)], writes=["junk", "ss2"])
                P.add("act", lambda e: e.activation(out=self.rstd2[:, 0:ns], in_=self.ss2[:, 0:ns], func=AF.Sqrt,
                                                    bias=self.epsc[:, 0:1], scale=1.0),
                      reads=["ss2", "epsc"], writes=["rstd2"])
                P.add("dve", lambda e: e.reciprocal(self.rstd2[:, 0:ns], self.rstd2[:, 0:ns]),
                      reads=["rstd2"], writes=["rstd2"])
                for s in range(ns):
                    P.add("dve", lambda e, s=s, xs=xs: e.scalar_tensor_tensor(
                        out=xs[:, s, :], in0=xs[:, s, :], scalar=self.rstd2[:, s:s + 1], in1=self.lnf[:, :],
                        op0=ALU.mult, op1=ALU.mult),
                        reads=[("x", slot), "rstd2", "lnf"], writes=[("x", slot)])
            dst = x_dst[t0:t0 + n, :].rearrange("(s p) d -> p s d", p=128)
            P.add("sp", lambda e, d=dst, o=xbuf[slot][:, 0:ns, :]: e.dma_start(out=d, in_=o),
                  reads=[("x", slot)], stream="xo%d" % slot)

        nt = len(tl)
        self.t_load(x_src, tl, 0)
        if nt > 1:
            self.t_load(x_src, tl, 1)
        self.t_prep(tl, 0)
        self.t_transposes(tl, 0)
        for i in range(nt):
            gate_up(i)
            if i + 1 < nt:
                self.t_prep(tl, i + 1)
            down(i)
            if i + 2 < nt:
                self.t_load(x_src, tl, i + 2)
            if i + 1 < nt:
                self.t_transposes(tl, i + 1)
        P.barrier()

    def mix_in_phase(self, x1, zT, rnnT, vtok, io):
        P = self.P
        S, NB, NS = self.S, self.NB, self.NS
        self.arena_off = self.arena_mark
        self.common_tile_bufs()
        Win = self.alloc([128, 8, INC], BF16)
        WaBD = self.alloc([128, 4, 128], BF16)
        WxBD = self.alloc([128, 4, 128], BF16)
        cw = self.alloc([128, 4, 4], F32)
        cvec = self.alloc([128, 4, 4], F32)
        lamc = self.alloc([128, 4], F32)
        onec = self.alloc([128, 1], F32)
        zst = [self.alloc([128, 12, TT], BF16), self.alloc([128, 12, TT], BF16)]
        ubuf = self.alloc([128, 4, TT + 3], F32)
        ubs = self.alloc([128, 4, NB, 11], F32)
        h0s = self.alloc([128, 4, NB], F32)
        hprev = self.alloc([128, 4], F32)
        hfin = self.alloc([128, 4, NB], F32)
        gs = self.alloc([128, 4, TT], F32)
        gt = self.alloc([128, 4, TT], F32)
        B = [self.alloc([128, 4, TT], F32) for _ in range(8)]
        xcb4 = self.alloc([128, 4, TT], BF16)
        rst = [self.alloc([128, 4, TT], BF16), self.alloc([128, 4, TT], BF16)]
        tok = [self.alloc([128, 512], F32), self.alloc([128, 512], F32)]
        tokb = self.alloc([128, 512], BF16)
        psZ = [self.pb[0], self.pb[1]]
        psA4 = [self.pb[2][:, 0:256], self.pb[2][:, 256:512], self.pb[3][:, 0:256], self.pb[3][:, 256:512]]
        psX4 = [self.pb[4][:, 0:256], self.pb[4][:, 256:512], self.pb[5][:, 0:256], self.pb[5][:, 256:512]]
        self.psT = [self.pb[6].bitcast(BF16), self.pb[7].bitcast(BF16)]
        NZ = len(psZ)

        k = 0
        for kc in range(8):
            for hh in range(2):
                self.load_cast(Win[:, kc, hh * 1280:(hh + 1) * 1280],
                               io["w_in"][kc * 128:(kc + 1) * 128, hh * 1280:(hh + 1) * 1280],
                               1280, self.gains[:, 8 + kc:8 + kc + 1], "Win", k)
                k += 1
        self.load_cast(WaBD, io["wabd"], 512, 1.0, "WaBD", 0)
        self.load_cast(WxBD, io["wxbd"], 512, 1.0, "WxBD", 1)
        P.add("sp", lambda e: e.dma_start(out=cw, in_=io["cw"]), writes=["cw"], stream="c_cw")
        P.add("sp", lambda e: e.dma_start(out=cvec, in_=io["cvec"]), writes=["cvec"], stream="c_cvec")
        P.add("sp", lambda e: e.dma_start(out=ubs[:, :, :, 0:3], in_=io["convst"]), writes=["ubs"], stream="c_ubs")
        P.add("sp", lambda e: e.dma_start(out=h0s, in_=io["h0"]), writes=["h0s"], stream="c_h0s")
        P.add("dve", lambda e: e.memset(onec, 1.0), writes=["onec"])
        P.add("dve", lambda e: e.memset(hprev, 0.0), writes=["hprev"])
        P.add("dve", lambda e: e.memset(ubuf[:, :, 0:3], 0.0), writes=[("ub", c) for c in range(4)])
        P.add("act", lambda e: e.activation(out=lamc, in_=cvec[:, 3, :], func=AF.Exp, scale=-1.0),
              reads=["cvec"], writes=["lamc"])
        P.add("act", lambda e: e.activation(out=lamc, in_=lamc, func=AF.Ln, bias=onec[:, 0:1], scale=1.0),
              reads=["lamc", "onec"], writes=["lamc"])
        P.add("dve", lambda e: e.tensor_scalar_mul(lamc, lamc, -8.0), reads=["lamc"], writes=["lamc"])

        tl = self.tiles()
        nt = len(tl)
        nprompt = nt - 1 if NS else nt

        def zproj(i, pending):
            t0, n = tl[i]
            is_s = (t0 >= S)
            zs = zst[i % 2]
            j = 0
            pending = list(pending)
            for c in range(20):
                if c >= 1 and pending:
                    pending.pop(0)()
                pz = psZ[j % NZ]
                pk = ("psZ", j % NZ)
                j += 1
                for kc in range(8):
                    P.add("pe", lambda e, c=c, kc=kc, pz=pz: e.matmul(
                        pz[:, 0:n], lhsT=Win[:, kc, c * 128:(c + 1) * 128], rhs=self.xnT[:, kc, 0:n],
                        start=(kc == 0), stop=(kc == 7)),
                        reads=["Win", "xnT"], writes=[pk])
                if c < 4:
                    P.add("act", lambda e, c=c, pz=pz: e.mul(zs[:, c, 0:n], pz[:, 0:n], 0.125),
                          reads=[pk], writes=[("zst", i % 2)])
                elif c < 12:
                    if c % 2 == 0:
                        P.add("act", lambda e, c=c, pz=pz: e.copy(zs[:, c, 0:n], pz[:, 0:n]),
                              reads=[pk], writes=[("zst", i % 2)])
                    else:
                        P.add("dve", lambda e, c=c, pz=pz: e.tensor_copy(zs[:, c, 0:n], pz[:, 0:n]),
                              reads=[pk], writes=[("zst", i % 2)])
                elif c < 16:
                    cc = c - 12
                    if is_s:
                        P.add("act", lambda e, cc=cc, pz=pz: e.copy(
                            ubs[:, cc, :, 3:11], pz[:, 0:n].rearrange("p (b t) -> p b t", t=8)),
                            reads=[pk], writes=[("ubs", cc)])
                    else:
                        P.add("act", lambda e, cc=cc, pz=pz: e.copy(ubuf[:, cc, 3:3 + n], pz[:, 0:n]),
                              reads=[pk], writes=[("ub", cc)])
                else:
                    cc = c - 16
                    P.add("act", lambda e, cc=cc, pz=pz: e.copy(gs[:, cc, 0:n], pz[:, 0:n]),
                          reads=[pk], writes=[("gs", cc)])
            dst = zT[:, :, t0:t0 + n].rearrange("c p t -> p c t")
            P.add("sp", lambda e, d=dst, o=zs[:, :, 0:n]: e.dma_start(out=d, in_=o),
                  reads=[("zst", i % 2)], stream="zo%d" % (i % 2))
            for f in pending:
                f()

        def gelu_gate(i):
            t0, n = tl[i]
            g = gs[:, :, 0:n]
            t = B[0][:, :, 0:n]
            P.add("dve", lambda e: e.tensor_tensor(out=t, in0=g, in1=g, op=ALU.mult),
                  reads=[("gs", c) for c in range(4)], writes=["t0"])
            P.add("dve", lambda e: e.tensor_scalar(out=t, in0=t, scalar1=0.044715, scalar2=1.0,
                                                   op0=ALU.mult, op1=ALU.add),
                  reads=["t0"], writes=["t0"])
            P.add("dve", lambda e: e.tensor_tensor(out=t, in0=t, in1=g, op=ALU.mult),
                  reads=["t0"] + [("gs", c) for c in range(4)], writes=["t0"])
            P.add("act", lambda e: e.activation(out=t, in_=t, func=AF.Sigmoid, scale=GELU_C),
                  reads=["t0"], writes=["t0"])
            P.add("dve", lambda e: e.tensor_tensor(out=gt[:, :, 0:n], in0=t, in1=g, op=ALU.mult),
                  reads=["t0"] + [("gs", c) for c in range(4)], writes=["gt"])

        def lru_prompt_steps(i):
            t0, n = tl[i]
            rs = rst[i % 2]
            XC, RG, IG, A, OM, BT, G0 = B[1], B[2], B[3], B[4], B[6], B[7], B[0]
            allxc = [("xc", c) for c in range(4)]
            allub = [("ub", c) for c in range(4)]
            allgs = [("gs", c) for c in range(4)]
            allrg = [("rg", c) for c in range(4)]
            g = gs[:, :, 0:n]
            t = G0[:, :, 0:n]
            om = OM[:, :, 0:n]
            bt = BT[:, :, 0:n]
            steps = []

            def conv0():
                for cc in range(4):
                    P.add("dve", lambda e, cc=cc: e.tensor_scalar(
                        out=XC[:, cc, 0:n], in0=ubuf[:, cc, 0:n], scalar1=cw[:, cc, 0:1],
                        scalar2=cvec[:, 0, cc:cc + 1], op0=ALU.mult, op1=ALU.add),
                        reads=[("ub", cc), "cw", "cvec"], writes=[("xc", cc)])
            steps.append(conv0)

            def convj(j):
                def f():
                    for cc in range(4):
                        P.add("dve", lambda e, cc=cc: e.scalar_tensor_tensor(
                            out=XC[:, cc, 0:n], in0=ubuf[:, cc, j:j + n], scalar=cw[:, cc, j:j + 1],
                            in1=XC[:, cc, 0:n], op0=ALU.mult, op1=ALU.add),
                            reads=[("ub", cc), "cw", ("xc", cc)], writes=[("xc", cc)])
                return f
            for j in range(1, 4):
                steps.append(convj(j))

            def cast():
                P.add("pool", lambda e: e.tensor_copy(ubuf[:, :, 0:3], ubuf[:, :, n:n + 3]), reads=allub,
                      writes=allub)
                P.add("act", lambda e: e.copy(xcb4[:, :, 0:n], XC[:, :, 0:n]), reads=allxc, writes=["xcb"])
            steps.append(cast)

            def gel1():
                P.add("dve", lambda e: e.tensor_tensor(out=t, in0=g, in1=g, op=ALU.mult), reads=allgs, writes=["t0"])
                P.add("dve", lambda e: e.tensor_scalar(out=t, in0=t, scalar1=0.044715, scalar2=1.0,
                                                       op0=ALU.mult, op1=ALU.add), reads=["t0"], writes=["t0"])
                P.add("dve", lambda e: e.tensor_tensor(out=t, in0=t, in1=g, op=ALU.mult), reads=["t0"] + allgs,
                      writes=["t0"])
            steps.append(gel1)

            def gates():
                for cc in range(4):
                    P.add("pe", lambda e, cc=cc: e.matmul(psA4[cc][:, 0:n], lhsT=WaBD[:, cc, :],
                                                          rhs=xcb4[:, cc, 0:n], start=True, stop=True),
                          reads=["WaBD", "xcb"], writes=[("psA", cc // 2)])
                    P.add("pe", lambda e, cc=cc: e.matmul(psX4[cc][:, 0:n], lhsT=WxBD[:, cc, :],
                                                          rhs=xcb4[:, cc, 0:n], start=True, stop=True),
                          reads=["WxBD", "xcb"], writes=[("psX", cc // 2)])
            steps.append(gates)

            def sig1():
                P.add("act", lambda e: e.activation(out=t, in_=t, func=AF.Sigmoid, scale=GELU_C),
                      reads=["t0"], writes=["t0"])
                for cc in range(4):
                    P.add("act", lambda e, cc=cc: e.activation(
                        out=RG[:, cc, 0:n], in_=psA4[cc][:, 0:n], func=AF.Sigmoid,
                        bias=cvec[:, 1, cc:cc + 1], scale=1.0),
                        reads=[("psA", cc // 2), "cvec"], writes=[("rg", cc)])
            steps.append(sig1)

            def sig2():
                for cc in range(4):
                    P.add("act", lambda e, cc=cc: e.activation(
                        out=IG[:, cc, 0:n], in_=psX4[cc][:, 0:n], func=AF.Sigmoid,
                        bias=cvec[:, 2, cc:cc + 1], scale=1.0),
                        reads=[("psX", cc // 2), "cvec"], writes=[("ig", cc)])
                P.add("dve", lambda e: e.tensor_tensor(out=gt[:, :, 0:n], in0=t, in1=g, op=ALU.mult),
                      reads=["t0"] + allgs, writes=["gt"])
            steps.append(sig2)

            def expa():
                for cc in range(4):
                    P.add("act", lambda e, cc=cc: e.activation(out=A[:, cc, 0:n], in_=RG[:, cc, 0:n], func=AF.Exp,
                                                               scale=lamc[:, cc:cc + 1]),
                          reads=[("rg", cc), "lamc"], writes=[("a", cc)])
            steps.append(expa)

            def om1():
                alla = [("a", c) for c in range(4)]
                P.add("dve", lambda e: e.tensor_tensor(out=om, in0=A[:, :, 0:n], in1=A[:, :, 0:n], op=ALU.mult),
                      reads=alla, writes=["om"])
                P.add("dve", lambda e: e.tensor_scalar(out=om, in0=om, scalar1=-1.0, scalar2=1.0,
                                                       op0=ALU.mult, op1=ALU.add), reads=["om"], writes=["om"])
                P.add("act", lambda e: e.activation(out=om, in_=om, func=AF.Sqrt), reads=["om"], writes=["om"])
                P.add("dve", lambda e: e.tensor_tensor(out=bt, in0=IG[:, :, 0:n], in1=XC[:, :, 0:n], op=ALU.mult),
                      reads=[("ig", c) for c in range(4)] + allxc, writes=["bt"])
            steps.append(om1)

            def bt2():
                P.add("dve", lambda e: e.tensor_tensor(out=bt, in0=bt, in1=om, op=ALU.mult),
                      reads=["bt", "om"], writes=["bt"])
            steps.append(bt2)

            def scans():
                for cc in range(4):
                    P.add("dve", lambda e, cc=cc: e.tensor_tensor_scan(
                        out=RG[:, cc, 0:n], data0=A[:, cc, 0:n], data1=BT[:, cc, 0:n],
                        initial=hprev[:, cc:cc + 1], op0=ALU.mult, op1=ALU.add),
                        reads=[("a", cc), "bt", "hprev"], writes=[("rg", cc)])
            steps.append(scans)

            def fin():
                P.add("dve", lambda e: e.tensor_copy(hprev[:, :], RG[:, :, n - 1]), reads=allrg, writes=["hprev"])
                P.add("dve", lambda e: e.tensor_tensor(out=rs[:, :, 0:n], in0=RG[:, :, 0:n], in1=gt[:, :, 0:n],
                                                       op=ALU.mult),
                      reads=allrg + ["gt"], writes=[("rst", i % 2)])
                dst = rnnT[:, :, t0:t0 + n].rearrange("c p t -> p c t")
                P.add("sp", lambda e, d=dst, o=rs[:, :, 0:n]: e.dma_start(out=d, in_=o),
                      reads=[("rst", i % 2)], stream="ro%d" % (i % 2))
                if i == nprompt - 1:
                    P.add("sp", lambda e: e.dma_start(out=io["lru_h_p"], in_=hprev), reads=["hprev"],
                          stream="o_hp")
            steps.append(fin)
            return steps

        def lru_sample(i):
            t0, n = tl[i]
            rs = rst[i % 2]
            tmp = [B[k][:, 0, :] for k in range(8)]
            psA, psX = psA4[0], psX4[0]
            xcb = xcb4[:, 0, :]
            for cc in range(4):
                xc = tmp[1][:, 0:n]
                xc3 = xc.rearrange("p (b t) -> p b t", t=8)
                uv = [ubs[:, cc, :, j:j + 8] for j in range(4)]
                ukey = ("ubs", cc)
                P.add("dve", lambda e, cc=cc, xc3=xc3, uv=uv: e.tensor_scalar(
                    out=xc3, in0=uv[0], scalar1=cw[:, cc, 0:1], scalar2=cvec[:, 0, cc:cc + 1],
                    op0=ALU.mult, op1=ALU.add),
                    reads=[ukey, "cw", "cvec"], writes=["xc"])
                for j in range(1, 4):
                    P.add("dve", lambda e, cc=cc, j=j, xc3=xc3, uv=uv: e.scalar_tensor_tensor(
                        out=xc3, in0=uv[j], scalar=cw[:, cc, j:j + 1], in1=xc3, op0=ALU.mult, op1=ALU.add),
                        reads=[ukey, "cw", "xc"], writes=["xc"])
                P.add("act", lambda e, xc=xc: e.copy(xcb[:, 0:n], xc), reads=["xc"], writes=["xcb"])
                P.add("pe", lambda e, cc=cc: e.matmul(psA[:, 0:n], lhsT=WaBD[:, cc, :], rhs=xcb[:, 0:n],
                                                      start=True, stop=True),
                      reads=["WaBD", "xcb"], writes=[("psA", 0)])
                P.add("pe", lambda e, cc=cc: e.matmul(psX[:, 0:n], lhsT=WxBD[:, cc, :], rhs=xcb[:, 0:n],
                                                      start=True, stop=True),
                      reads=["WxBD", "xcb"], writes=[("psX", 0)])
                rg = tmp[2][:, 0:n]
                ig = tmp[3][:, 0:n]
                P.add("act", lambda e, cc=cc, rg=rg: e.activation(out=rg, in_=psA[:, 0:n], func=AF.Sigmoid,
                                                                  bias=cvec[:, 1, cc:cc + 1], scale=1.0),
                      reads=[("psA", 0), "cvec"], writes=["rg"])
                P.add("act", lambda e, cc=cc, ig=ig: e.activation(out=ig, in_=psX[:, 0:n], func=AF.Sigmoid,
                                                                  bias=cvec[:, 2, cc:cc + 1], scale=1.0),
                      reads=[("psX", 0), "cvec"], writes=["ig"])
                a = tmp[4][:, 0:n]
                th = tmp[5][:, 0:n]
                P.add("act", lambda e, cc=cc, a=a, rg=rg: e.activation(out=a, in_=rg, func=AF.Exp,
                                                                       scale=lamc[:, cc:cc + 1]),
                      reads=["rg", "lamc"], writes=["a"])
                P.add("act", lambda e, cc=cc, th=th, rg=rg: e.activation(out=th, in_=rg, func=AF.Tanh,
                                                                         scale=lamc[:, cc:cc + 1]),
                      reads=["rg", "lamc"], writes=["th"])
                om = tmp[6][:, 0:n]
                P.add("dve", lambda e, th=th, om=om: e.tensor_scalar(out=om, in0=th, scalar1=-1.0, scalar2=1.0,
                                                                     op0=ALU.mult, op1=ALU.add),
                      reads=["th"], writes=["om"])
                P.add("dve", lambda e, om=om: e.reciprocal(om, om), reads=["om"], writes=["om"])
                P.add("dve", lambda e, th=th, om=om: e.scalar_tensor_tensor(
                    out=om, in0=th, scalar=-2.0, in1=om, op0=ALU.mult, op1=ALU.mult),
                    reads=["th", "om"], writes=["om"])
                P.add("act", lambda e, om=om: e.activation(out=om, in_=om, func=AF.Sqrt), reads=["om"],
                      writes=["om"])
                bt = tmp[7][:, 0:n]
                P.add("dve", lambda e, om=om, ig=ig, bt=bt: e.tensor_tensor(out=bt, in0=om, in1=ig, op=ALU.mult),
                      reads=["om", "ig"], writes=["bt"])
                P.add("dve", lambda e, xc=xc, bt=bt: e.tensor_tensor(out=bt, in0=bt, in1=xc, op=ALU.mult),
                      reads=["bt", "xc"], writes=["bt"])
                hs = tmp[2][:, 0:n]
                a3 = a.rearrange("p (b t) -> p b t", t=8)
                bt3 = bt.rearrange("p (b t) -> p b t", t=8)
                t3 = tmp[3][:, 0:NB]
                P.add("dve", lambda e, cc=cc, a3=a3, t3=t3: e.tensor_tensor(
                    out=t3, in0=a3[:, :, 0], in1=h0s[:, cc, :], op=ALU.mult),
                    reads=["a", "h0s", "bt"], writes=["ig"])
                P.add("dve", lambda e, bt3=bt3, t3=t3: e.tensor_tensor(
                    out=bt3[:, :, 0], in0=bt3[:, :, 0], in1=t3, op=ALU.add),
                    reads=["ig", "bt"], writes=["bt"])
                P.add("dve", lambda e, a3=a3: e.memset(a3[:, :, 0], 0.0), reads=["ig"], writes=["a"])
                P.add("dve", lambda e, a=a, bt=bt, hs=hs: e.tensor_tensor_scan(
                    out=hs, data0=a, data1=bt, initial=0.0, op0=ALU.mult, op1=ALU.add),
                    reads=["a", "bt", "th"], writes=["rg"])
                hs3 = hs.rearrange("p (b t) -> p b t", t=8)
                P.add("dve", lambda e, cc=cc, hs3=hs3: e.tensor_copy(hfin[:, cc, :], hs3[:, :, 7]),
                      reads=["rg"], writes=["hfin"])
                P.add("dve", lambda e, cc=cc, hs=hs: e.tensor_tensor(out=rs[:, cc, 0:n], in0=hs,
                                                                     in1=gt[:, cc, 0:n], op=ALU.mult),
                      reads=["rg", "gt"], writes=[("rst", i % 2)])
            dst = rnnT[:, :, t0:t0 + n].rearrange("c p t -> p c t")
            P.add("sp", lambda e, d=dst, o=rs[:, :, 0:n]: e.dma_start(out=d, in_=o),
                  reads=[("rst", i % 2)], stream="ro%d" % (i % 2))
            P.add("sp", lambda e: e.dma_start(out=io["lru_h_s"], in_=hfin), reads=["hfin"], stream="o_hs")

        tk = [0]

        def tok_outputs(i):
            t0, n = tl[i]
            is_s = (t0 >= S)
            if not is_s and t0 < S - WBUF:
                return
            ns = n // 128
            for s in range(ns):
                tt = t0 + s * 128
                last = (not is_s) and (tt + 128 == S)
                sects = [0, 1] + ([2] if (is_s or last) else [])
                for sec in sects:
                    j = tk[0]
                    tk[0] += 1
                    pz = psZ[j % NZ]
                    pk = ("psZ", j % NZ)
                    c0 = 512 * (1 + sec)
                    for kc in range(8):
                        P.add("pe", lambda e, kc=kc, pz=pz, s=s, c0=c0: e.matmul(
                            pz[:, :], lhsT=self.xnT[:, kc, s * 128:(s + 1) * 128], rhs=Win[:, kc, c0:c0 + 512],
                            start=(kc == 0), stop=(kc == 7)),
                            reads=["Win", "xnT"], writes=[pk])
                    tb = tok[j % 2]
                    tkey = ("tok", j % 2)
                    P.add("act", lambda e, tb=tb, pz=pz: e.copy(tb, pz[:, :]), reads=[pk], writes=[tkey])
                    if is_s:
                        if sec < 2:
                            dst = (io["knew"], io["vnew"])[sec]
                            P.add("sp", lambda e, d=dst, tb=tb: e.dma_start(out=d, in_=tb), reads=[tkey],
                                  stream="tk%d" % (j % 2))
                            if sec == 1:
                                P.add("dve", lambda e, tb=tb: e.tensor_copy(tokb, tb), reads=[tkey],
                                      writes=["tokb"])
                                P.add("sp", lambda e: e.dma_start(out=vtok, in_=tokb), reads=["tokb"],
                                      stream="tkb")
                        else:
                            for b in range(NB):
                                P.add("sp", lambda e, b=b, tb=tb: e.dma_start(
                                    out=io["conv_s"][b], in_=tb[b * 8 + 5:b * 8 + 8, :]),
                                    reads=[tkey], stream="tk%d" % (j % 2))
                    else:
                        if sec < 2:
                            r0 = tt - (S - WBUF)
                            dst = (io["kwin"], io["vwin"])[sec][r0:r0 + 128, :]
                            P.add("sp", lambda e, d=dst, tb=tb: e.dma_start(out=d, in_=tb), reads=[tkey],
                                  stream="tk%d" % (j % 2))
                        else:
                            P.add("sp", lambda e, tb=tb: e.dma_start(out=io["conv_p"], in_=tb[125:128, :]),
                                  reads=[tkey], stream="tk%d" % (j % 2))

        self.t_load(x1, tl, 0)
        if nt > 1:
            self.t_load(x1, tl, 1)
        self.t_prep(tl, 0)
        self.t_transposes(tl, 0)
        pending = []
        for i in range(nt):
            zproj(i, pending)
            pending = []
            tok_outputs(i)
            if i + 2 < nt:
                self.t_load(x1, tl, i + 2)
            if i + 1 < nt:
                self.t_prep(tl, i + 1)
                self.t_transposes(tl, i + 1)
            if tl[i][0] >= S:
                gelu_gate(i)
                lru_sample(i)
            else:
                pending = lru_prompt_steps(i)
        for f in pending:
            f()
        P.barrier()

    def attn_prompt_phase(self, zT, attnT, io):
        P = self.P
        S = self.S
        self.arena_off = self.arena_mark
        qZ = self.alloc([128, 2, S], BF16)
        kT = self.alloc([128, S], BF16)
        vT = self.alloc([128, S], BF16)
        acc = self.alloc([128, 2, S], F32)
        mask = self.alloc([128, 2, 256], BF16)
        onesf = self.alloc([128, 64], F32)
        pT = [self.alloc([128, 2, 256], BF16) for _ in range(3)]
        vblk = [self.alloc([128, 2, 65], BF16) for _ in range(3)]
        lnr = self.alloc([128, S], F32)
        ast = [self.alloc([128, 512], BF16), self.alloc([128, 512], BF16)]
        psS = [self.pb[0], self.pb[1], self.pb[2], self.pb[3]]
        psV = [self.pb[4].bitcast(BF16)[:, 0:128], self.pb[5].bitcast(BF16)[:, 0:128]]
        psO = [self.pb[6], self.pb[7]]
        psB = [self.pb[4], self.pb[4]]
        P.add("sp", lambda e: e.dma_start(out=mask, in_=io["maskp"]), writes=["mask"], stream="c_mask")
        P.add("dve", lambda e: e.memset(onesf, 1.0), writes=["onesf"])
        for v in range(3):
            P.add("dve", lambda e, v=v: e.memset(vblk[v][:, :, 64:65], 1.0), writes=[("vblk", v)])
        it = [0]
        P.add("pool", lambda e: e.memset(qZ[64:128, 0, :], 0.0), writes=["qT"])
        P.add("pool", lambda e: e.memset(qZ[0:64, 1, :], 0.0), writes=["qT"])
        for c in range(4):
            for h0 in range(0, S, 4096):
                h1 = min(S, h0 + 4096)
                P.add("sp", lambda e, c=c, h0=h0, h1=h1: e.dma_start(
                    out=qZ[0:64, 0, h0:h1], in_=zT[c, 0:64, h0:h1]), writes=["qT"], stream="ld_qT")
                P.add("sp", lambda e, c=c, h0=h0, h1=h1: e.dma_start(
                    out=qZ[64:128, 1, h0:h1], in_=zT[c, 64:128, h0:h1]), writes=["qT"], stream="ld_qT")
            for nm, buf, ch in (("kT", kT, 4 + c), ("vT", vT, 8 + c)):
                for h0 in range(0, S, 4096):
                    h1 = min(S, h0 + 4096)
                    P.add("sp", lambda e, buf=buf, ch=ch, h0=h0, h1=h1: e.dma_start(
                        out=buf[:, h0:h1], in_=zT[ch, :, h0:h1]), writes=[nm], stream="ld_" + nm)
            P.add("pool", lambda e: e.memset(acc[0:65, :, :], 0.0), writes=["acc"])
            iters = []
            for d in (1, 4, 16):
                span = 128 * d
                nsb = S // span
                for nb in range(nsb):
                    nq = 256 if nb + 1 < nsb else 128
                    for r in range(d):
                        base = nb * span + r
                        iters.append((base, d, nq))

            def stage1(base, d, nq, j):
                ks = slice(base, base + d * 127 + 1, d)
                qs = slice(base, base + d * (nq - 1) + 1, d)
                s4, s3, s2 = j % 4, j % 3, j % 2
                sv = psS[s4].rearrange("p (e q) -> p e q", e=2)[:, :, 0:nq]
                P.add("pe", lambda e: e.matmul(sv, lhsT=kT[:, ks], rhs=qZ[:, :, qs], start=True, stop=True),
                      reads=["kT", "qT"], writes=[("psS", s4)])
                P.add("pe", lambda e: e.transpose(out=psV[s2], in_=vT[:, ks], identity=self.ident[:, :]),
                      reads=["vT", "ident"], writes=[("psV", s2)])
                P.add("act", lambda e: e.activation(out=pT[s3][:, :, 0:nq], in_=sv, func=AF.Exp),
                      reads=[("psS", s4)], writes=[("pT", s3, 0), ("pT", s3, 1)])
                P.add("pool", lambda e: e.tensor_tensor(
                    out=pT[s3][:, 0, 0:nq], in0=pT[s3][:, 0, 0:nq], in1=mask[:, 0, 0:nq], op=ALU.mult),
                    reads=[("pT", s3, 0), "mask"], writes=[("pT", s3, 0)])
                P.add("dve", lambda e: e.tensor_tensor(
                    out=pT[s3][:, 1, 0:nq], in0=pT[s3][:, 1, 0:nq], in1=mask[:, 1, 0:nq], op=ALU.mult),
                    reads=[("pT", s3, 1), "mask"], writes=[("pT", s3, 1)])
                P.add("act", lambda e: e.copy(vblk[s3][:, :, 0:64], psV[s2].rearrange("p (e c) -> p e c", e=2)),
                      reads=[("psV", s2)], writes=[("vblk", s3)])

            def stage2(base, d, nq, j):
                qs = slice(base, base + d * (nq - 1) + 1, d)
                s3, s2 = j % 3, j % 2
                for e2 in range(2):
                    P.add("pe", lambda e, e2=e2: e.matmul(
                        psO[s2][0:65, e2 * 256:e2 * 256 + nq], lhsT=vblk[s3][:, e2, :],
                        rhs=pT[s3][:, e2, 0:nq], start=True, stop=True),
                        reads=[("vblk", s3), ("pT", s3, e2)], writes=[("psO", s2)])
                av = acc[0:65, :, qs]
                ov = psO[s2][0:65, :].rearrange("p (e q) -> p e q", e=2)[:, :, 0:nq]
                P.add("dve", lambda e: e.tensor_tensor(out=av, in0=ov, in1=av, op=ALU.add),
                      reads=[("psO", s2), "acc"], writes=["acc"])

            SK = 2
            for idx, (base, d, nq) in enumerate(iters):
                stage1(base, d, nq, idx)
                if idx >= SK:
                    pbase, pd, pnq = iters[idx - SK]
                    stage2(pbase, pd, pnq, idx - SK)
            for idx in range(max(0, len(iters) - SK), len(iters)):
                pbase, pd, pnq = iters[idx]
                stage2(pbase, pd, pnq, idx)
            for e2 in range(2):
                P.add("act", lambda e, e2=e2: e.activation(out=lnr[64:65, :], in_=acc[64:65, e2, :], func=AF.Ln),
                      reads=["acc"], writes=["lnr"])
                P.add("act", lambda e, e2=e2: e.activation(out=acc[64:65, e2, :], in_=lnr[64:65, :], func=AF.Exp,
                                                           scale=-1.0),
                      reads=["lnr"], writes=["acc"])
            k = 0
            for e2 in range(2):
                for p0 in range(0, S, 512):
                    pbb = k % 2
                    k += 1
                    P.add("pe", lambda e, e2=e2, p0=p0, pbb=pbb: e.matmul(
                        psB[pbb][0:64, :], lhsT=onesf[64:65, 0:64], rhs=acc[64:65, e2, p0:p0 + 512],
                        start=True, stop=True),
                        reads=["onesf", "acc"], writes=[("psV", 0)])
                    P.add("dve", lambda e, e2=e2, p0=p0, pbb=pbb: e.tensor_tensor(
                        out=ast[pbb][0:64, :], in0=acc[0:64, e2, p0:p0 + 512], in1=psB[pbb][0:64, :],
                        op=ALU.mult),
                        reads=["acc", ("psV", 0)], writes=[("ast", pbb)])
                    P.add("sp", lambda e, e2=e2, p0=p0, pbb=pbb, c=c: e.dma_start(
                        out=attnT[c, e2 * 64:(e2 + 1) * 64, p0:p0 + 512], in_=ast[pbb][0:64, :]),
                        reads=[("ast", pbb)], stream="ao%d" % pbb)
        P.barrier()

    def attn_sample_phase(self, zT, vtok, attnT, io):
        P = self.P
        S, NB, NS = self.S, self.NB, self.NS
        self.arena_off = self.arena_mark
        NBLK = WBUF // 128
        stgk = [self.alloc([128, 8, 512], F32) for _ in range(4)]
        kb = self.alloc([128, NBLK, 512], BF16)
        kTs2 = [self.alloc([128, 4, WBUF + 8], BF16), self.alloc([128, 4, WBUF + 8], BF16)]
        vb = [self.alloc([128, NBLK + 1, 8, 65], BF16), self.alloc([128, NBLK + 1, 8, 65], BF16)]
        qTs = self.alloc([128, 4, NS], BF16)
        kTn = self.alloc([128, 4, NS], BF16)
        vnb = self.alloc([128, 512], BF16)
        masks = self.alloc([128, NBLK + 1, 8, 8], BF16)
        pTs = [self.alloc([128, NBLK + 1, 8, 8], BF16), self.alloc([128, NBLK + 1, 8, 8], BF16)]
        accs = self.alloc([128, 8, NS], F32)
        onesf = self.alloc([128, 64], F32)
        ast = [self.alloc([128, NS], BF16), self.alloc([128, NS], BF16)]
        psK = [self.pb[0].bitcast(BF16), self.pb[1].bitcast(BF16)]
        psMain = [self.pb[2], self.pb[3]]
        psNew = [self.pb[4], self.pb[5]]
        psO = [self.pb[6][:, 0:64], self.pb[7][:, 0:64]]
        psB = self.pb[6][:, 128:256]
        P.add("sp", lambda e: e.dma_start(out=masks, in_=io["masks"]), writes=["masks"], stream="c_masks")
        P.add("sp", lambda e: e.dma_start(out=qTs, in_=zT[0:4, :, S:S + NS].rearrange("c p t -> p c t")),
              writes=["qTs"], stream="c_qTs")
        P.add("sp", lambda e: e.dma_start(out=kTn, in_=zT[4:8, :, S:S + NS].rearrange("c p t -> p c t")),
              writes=["kTn"], stream="c_kTn")
        P.add("dve", lambda e: e.memset(onesf, 1.0), writes=["onesf"])
        for v in range(2):
            P.add("pool", lambda e, v=v: e.memset(vb[v][:, :, :, 64:65], 1.0), writes=[("vb", v)])
        sk = [0]

        def load_half(src, b, half):
            j = sk[0]
            sk[0] += 1
            slot = j % 4
            sv = src[b, half * 1024:(half + 1) * 1024, :].rearrange("(k p) d -> p k d", p=128)
            P.add("sp", lambda e, slot=slot, sv=sv: e.dma_start(out=stgk[slot], in_=sv),
                  writes=[("stgk", slot)], stream="sk%d" % slot)
            return slot

        def sview(blk, h):
            par, hh = h % 2, h // 2
            if blk < NBLK:
                off = blk * 32 + hh * 8
                return psMain[par][:, off:off + 8]
            return psNew[par][:, hh * 8:hh * 8 + 8]

        def prep(b):
            vv = vb[b % 2]
            vkey = ("vb", b % 2)
            kT_b = kTs2[b % 2]
            kkey = ("kTs", b % 2)
            for half in range(2):
                slot = load_half(io["kcache"], b, half)
                if half == 0:
                    P.add("dve", lambda e, slot=slot, half=half: e.tensor_copy(
                        kb[:, half * 8:(half + 1) * 8, :], stgk[slot]),
                        reads=[("stgk", slot)], writes=[("kb", half)])
                else:
                    P.add("act", lambda e, slot=slot, half=half: e.copy(
                        kb[:, half * 8:(half + 1) * 8, :], stgk[slot]),
                        reads=[("stgk", slot)], writes=[("kb", half)])
            for half in range(2):
                slot = load_half(io["vcache"], b, half)
                dstv = vv[:, half * 8:(half + 1) * 8, :, 0:64]
                srcv = stgk[slot].rearrange("p k (h c) -> p k h c", h=8)
                if half == 0:
                    P.add("act", lambda e, dstv=dstv, srcv=srcv: e.copy(dstv, srcv),
                          reads=[("stgk", slot)], writes=[vkey])
                else:
                    P.add("dve", lambda e, dstv=dstv, srcv=srcv: e.tensor_copy(dstv, srcv),
                          reads=[("stgk", slot)], writes=[vkey])
            P.add("sp", lambda e, b=b: e.dma_start(out=vnb[0:8, :], in_=vtok[b * 8:(b + 1) * 8, :]),
                  writes=["vnb"], stream="vn")
            P.add("dve", lambda e, vv=vv: e.tensor_copy(vv[0:8, NBLK, :, 0:64],
                                                        vnb[0:8, :].rearrange("p (h c) -> p h c", h=8)),
                  reads=["vnb"], writes=[vkey])
            P.add("pool", lambda e, b=b: e.tensor_copy(kT_b[:, :, WBUF:WBUF + 8], kTn[:, :, b * 8:(b + 1) * 8]),
                  reads=["kTn"], writes=[kkey])
            for blk in range(NBLK):
                pk = blk % 2
                for c in range(4):
                    P.add("pe", lambda e, blk=blk, c=c, pk=pk: e.transpose(
                        out=psK[pk][:, c * 128:(c + 1) * 128], in_=kb[:, blk, c * 128:(c + 1) * 128],
                        identity=self.ident[:, :]),
                        reads=[("kb", blk // 8), "ident"], writes=[("psK", pk)])
                src = psK[pk][:, 0:512].rearrange("p (c k) -> p c k", c=4)
                dst = kT_b[:, :, blk * 128:(blk + 1) * 128]
                if blk % 2 == 0:
                    P.add("act", lambda e, d=dst, s=src: e.copy(d, s), reads=[("psK", pk)], writes=[kkey])
                else:
                    P.add("dve", lambda e, d=dst, s=src: e.tensor_copy(d, s), reads=[("psK", pk)],
                          writes=[kkey])

        def scores(b):
            kT_b = kTs2[b % 2]
            kkey = ("kTs", b % 2)
            for blk in range(NBLK + 1):
                nk = 128 if blk < NBLK else 8
                for hh in range(4):
                    for par in range(2):
                        h = 2 * hh + par
                        rows = slice(par * 64, par * 64 + 64)
                        P.add("pe", lambda e, blk=blk, h=h, rows=rows, nk=nk: e.matmul(
                            sview(blk, h)[0:nk, :], lhsT=kT_b[rows, h // 2, blk * 128:blk * 128 + nk],
                            rhs=qTs[rows, h // 2, b * 8:(b + 1) * 8], start=True, stop=True),
                            reads=[kkey, "qTs"], writes=["psSs"])

        def softmax_pv(b):
            vv = vb[b % 2]
            vkey = ("vb", b % 2)
            pp = pTs[b % 2]
            pkey = ("pTs", b % 2)
            for par in range(2):
                P.add("act", lambda e, par=par, pp=pp: e.activation(
                    out=pp[:, 0:NBLK, par::2, :],
                    in_=psMain[par].rearrange("p (k h t) -> p k h t", k=NBLK, h=4), func=AF.Exp),
                    reads=["psSs"], writes=[pkey])
                P.add("act", lambda e, par=par, pp=pp: e.activation(
                    out=pp[0:8, NBLK, par::2, :],
                    in_=psNew[par][0:8, 0:32].rearrange("p (h t) -> p h t", h=4), func=AF.Exp),
                    reads=["psSs"], writes=[pkey])
            P.add("dve", lambda e, pp=pp: e.tensor_tensor(out=pp[:, 0:NBLK, :, :], in0=pp[:, 0:NBLK, :, :],
                                                          in1=masks[:, 0:NBLK, :, :], op=ALU.mult),
                  reads=[pkey, "masks"], writes=[pkey])
            P.add("dve", lambda e, pp=pp: e.tensor_tensor(out=pp[0:8, NBLK, :, :], in0=pp[0:8, NBLK, :, :],
                                                          in1=masks[0:8, NBLK, :, :], op=ALU.mult),
                  reads=[pkey, "masks"], writes=[pkey])
            po = psO[b % 2]
            for h in range(8):
                for blk in range(NBLK + 1):
                    nk = 128 if blk < NBLK else 8
                    P.add("pe", lambda e, blk=blk, h=h, nk=nk, po=po, vv=vv, pp=pp: e.matmul(
                        po[0:65, h * 8:(h + 1) * 8], lhsT=vv[0:nk, blk, h, :], rhs=pp[0:nk, blk, h, :],
                        start=(blk == 0), stop=(blk == NBLK)),
                        reads=[vkey, pkey], writes=[("psOs", b % 2)])
            P.add("dve", lambda e, b=b, po=po: e.tensor_copy(
                accs[0:65, :, b * 8:(b + 1) * 8], po[0:65, :].rearrange("p (h t) -> p h t", h=8)),
                reads=[("psOs", b % 2)], writes=["accs"])

        prep(0)
        for b in range(NB):
            scores(b)
            if b + 1 < NB:
                prep(b + 1)
            softmax_pv(b)
        for h in range(8):
            P.add("dve", lambda e, h=h: e.reciprocal(accs[64:65, h, :], accs[64:65, h, :]),
                  reads=["accs"], writes=["accs"])
            P.add("pe", lambda e, h=h: e.matmul(psB[0:64, 0:NS], lhsT=onesf[64:65, 0:64], rhs=accs[64:65, h, :],
                                                start=True, stop=True),
                  reads=["onesf", "accs"], writes=[("psOs", 0)])
            P.add("dve", lambda e, h=h: e.tensor_tensor(out=ast[h % 2][0:64, :], in0=accs[0:64, h, :],
                                                        in1=psB[0:64, 0:NS], op=ALU.mult),
                  reads=["accs", ("psOs", 0)], writes=[("ast", h % 2)])
            P.add("sp", lambda e, h=h: e.dma_start(out=attnT[h // 2, (h % 2) * 64:(h % 2) * 64 + 64, S:S + NS],
                                                   in_=ast[h % 2][0:64, :]),
                  reads=[("ast", h % 2)], stream="aso%d" % (h % 2))
        P.barrier()

    def mix_out_phase(self, x1, attnT, rnnT, x2, io):
        P = self.P
        self.arena_off = self.arena_mark
        self.common_tile_bufs()
        WoA = self.alloc([128, 4, D], BF16)
        WoR = self.alloc([128, 4, D], BF16)
        at = [self.alloc([128, 4, TT], BF16), self.alloc([128, 4, TT], BF16)]
        rt = [self.alloc([128, 4, TT], BF16), self.alloc([128, 4, TT], BF16)]
        psD = [self.pb[4], self.pb[5]]
        wo = io["w_out"]
        for cc in range(4):
            self.load_cast(WoA[:, cc, :], wo[cc * 128:(cc + 1) * 128, :], D, 1.0, "WoA", cc)
        for cc in range(4):
            self.load_cast(WoR[:, cc, :], wo[512 + cc * 128:512 + (cc + 1) * 128, :], D, 1.0, "WoR", cc)
        tl = self.tiles()
        nt = len(tl)

        def load(i):
            t0, n = tl[i]
            slot = i % 2
            self.t_load(x1, tl, i)
            P.add("sp", lambda e: e.dma_start(out=at[slot][:, :, 0:n],
                                              in_=attnT[:, :, t0:t0 + n].rearrange("c p t -> p c t")),
                  writes=[("at", slot)], stream="at%d" % slot)
            P.add("sp", lambda e: e.dma_start(out=rt[slot][:, :, 0:n],
                                              in_=rnnT[:, :, t0:t0 + n].rearrange("c p t -> p c t")),
                  writes=[("rt", slot)], stream="rt%d" % slot)

        load(0)
        if nt > 1:
            load(1)
        j = 0
        for i in range(nt):
            t0, n = tl[i]
            ns = n // 128
            slot = i % 2
            xs = self.xbuf[slot]
            for s in range(ns):
                for hf in range(2):
                    pd = psD[j % 2]
                    pk = ("psD", j % 2)
                    j += 1
                    for cc in range(4):
                        P.add("pe", lambda e, cc=cc, s=s, hf=hf, pd=pd, slot=slot: e.matmul(
                            pd[:, :], lhsT=at[slot][:, cc, s * 128:(s + 1) * 128],
                            rhs=WoA[:, cc, hf * 512:(hf + 1) * 512], start=(cc == 0), stop=False),
                            reads=["WoA", ("at", slot)], writes=[pk])
                    for cc in range(4):
                        P.add("pe", lambda e, cc=cc, s=s, hf=hf, pd=pd, slot=slot: e.matmul(
                            pd[:, :], lhsT=rt[slot][:, cc, s * 128:(s + 1) * 128],
                            rhs=WoR[:, cc, hf * 512:(hf + 1) * 512], start=False, stop=(cc == 3)),
                            reads=["WoR", ("rt", slot)], writes=[pk])
                    P.add("dve", lambda e, s=s, hf=hf, pd=pd, xs=xs: e.tensor_tensor(
                        out=xs[:, s, hf * 512:(hf + 1) * 512], in0=pd[:, :],
                        in1=xs[:, s, hf * 512:(hf + 1) * 512], op=ALU.add),
                        reads=[pk, ("x", slot)], writes=[("x", slot)])
            dst = x2[t0:t0 + n, :].rearrange("(s p) d -> p s d", p=128)
            P.add("sp", lambda e, d=dst, o=xs[:, 0:ns, :]: e.dma_start(out=d, in_=o),
                  reads=[("x", slot)], stream="xo%d" % slot)
            if i + 2 < nt:
                load(i + 2)
        P.barrier()

    def build(self):
        nc = self.nc
        P = self.P
        NT, S, NB, NS = self.NT, self.S, self.NB, self.NS
        xin = self.dram_in("xin", [NT, D])
        w1g = self.dram_in("w1g", [D, DFF]); w1u = self.dram_in("w1u", [D, DFF]); w1d = self.dram_in("w1d", [DFF, D])
        w2g = self.dram_in("w2g", [D, DFF]); w2u = self.dram_in("w2u", [D, DFF]); w2d = self.dram_in("w2d", [DFF, D])
        gains_d = self.dram_in("gains", [128, 24])
        lnf_d = self.dram_in("lnf", [D])
        ident_d = self.dram_in("ident", [128, 128], BF16)
        io = {}
        io["w_in"] = self.dram_in("w_in", [D, INC])
        io["w_out"] = self.dram_in("w_out", [D, D])
        io["wabd"] = self.dram_in("wabd", [128, 4, 128])
        io["wxbd"] = self.dram_in("wxbd", [128, 4, 128])
        io["cw"] = self.dram_in("cw", [128, 4, 4])
        io["cvec"] = self.dram_in("cvec", [128, 4, 4])
        io["convst"] = self.dram_in("convst", [128, 4, NB, 3])
        io["h0"] = self.dram_in("h0", [128, 4, NB])
        io["kcache"] = self.dram_in("kcache", [NB, WBUF, 512])
        io["vcache"] = self.dram_in("vcache", [NB, WBUF, 512])
        io["maskp"] = self.dram_in("maskp", [128, 2, 256], BF16)
        io["masks"] = self.dram_in("masks", [128, 17, 8, 8], BF16)
        y = self.dram_out("y", [NT, D])
        io["kwin"] = self.dram_out("kwin", [WBUF, 512])
        io["vwin"] = self.dram_out("vwin", [WBUF, 512])
        io["lru_h_p"] = self.dram_out("lru_h_p", [128, 4])
        io["conv_p"] = self.dram_out("conv_p", [3, 512])
        io["knew"] = self.dram_out("knew", [NS, 512])
        io["vnew"] = self.dram_out("vnew", [NS, 512])
        io["lru_h_s"] = self.dram_out("lru_h_s", [128, 4, NB])
        io["conv_s"] = self.dram_out("conv_s", [NB, 3, 512])
        x1 = self.dram_tmp("x1", [NT, D], F32)
        x2 = self.dram_tmp("x2", [NT, D], F32)
        zT = self.dram_tmp("zT", [12, 128, NT], BF16)
        rnnT = self.dram_tmp("rnnT", [4, 128, NT], BF16)
        attnT = self.dram_tmp("attnT", [4, 128, NT], BF16)
        vtok = self.dram_tmp("vtok", [NS, 512], BF16)
        with ExitStack() as stack:
            self.stack = stack
            self.arena = stack.enter_context(nc.sbuf_tensor("arena", [128, ARENA_WORDS], F32))
            self.pbig = stack.enter_context(nc.psum_tensor("pbig", [128, 4096], F32))
            self.pb = [self.pbig[:, i * 512:(i + 1) * 512] for i in range(8)]
            self.arena_off = 0
            self.gains = self.alloc([128, 24], F32)
            self.lnf = self.alloc([128, D], F32)
            self.ident = self.alloc([128, 128], BF16)
            self.epsc = self.alloc([128, 1], F32)
            self.arena_mark = self.arena_off

            P.add("dve", lambda e: e.memset(self.epsc, EPS), writes=["epsc"])
            P.add("sp", lambda e: e.dma_start(out=self.gains, in_=gains_d), writes=["gains"], stream="c_gains")
            P.add("sp", lambda e: e.dma_start(out=self.lnf, in_=_bcast_rows(lnf_d, 128)), writes=["lnf"],
                  stream="c_lnf")
            P.add("sp", lambda e: e.dma_start(out=self.ident, in_=ident_d), writes=["ident"], stream="c_ident")
            P.barrier()

            src = xin
            if "ffn1" in self.phases:
                self.ffn_phase("f1", src, x1, w1g, w1u, w1d, 0, final_norm=False)
                src = x1
            ph = self.phases
            if "mix" in ph or "mix_in" in ph:
                self.mix_in_phase(src, zT, rnnT, vtok, io)
            if "mix" in ph or "attn_p" in ph:
                self.attn_prompt_phase(zT, attnT, io)
            if "mix" in ph or "attn_s" in ph:
                self.attn_sample_phase(zT, vtok, attnT, io)
            if "mix" in ph or "mix_out" in ph:
                self.mix_out_phase(src, attnT, rnnT, x2, io)
                src = x2
            if "ffn2" in self.phases:
                self.ffn_phase("f2", src, y, w2g, w2u, w2d, 2, final_norm=True)

            P.finalize_outputs()
            P.emit(nc, lambda name: stack.enter_context(nc.semaphore(name)))
        return nc


def _fm(v):
    return np.ascontiguousarray(np.asarray(v, np.float32).reshape(-1, 128).T)


def _blockdiag(w):
    out = np.zeros((128, 4, 128), np.float32)
    for c in range(4):
        out[0:64, c, 0:64] = w[2 * c]
        out[64:128, c, 64:128] = w[2 * c + 1]
    return out


def _mask_prompt():
    j = np.arange(128)[:, None]
    i = np.arange(128)[None, :]
    m = np.concatenate([(j <= i), (j >= i)], axis=1).astype(np.float32)
    return np.ascontiguousarray(np.broadcast_to(m[:, None, :], (128, 2, 256))).astype(ml_dtypes.bfloat16)


def _mask_sample():
    m = np.zeros((128, 17, 8, 8), np.float32)
    p = np.arange(128)
    for blk in range(17):
        for t in range(8):
            s = blk * 128 + p
            dd = WBUF + t - s
            mult = ((dd >= 0) & (dd <= 128)).astype(np.float32)
            mult += ((dd >= 0) & (dd % 4 == 0) & (dd <= 512))
            mult += ((dd >= 0) & (dd % 16 == 0) & (dd <= 2048))
            if blk == 16:
                mult = np.where(p < 8, mult, 0.0)
            m[:, blk, :, t] = mult[:, None]
    return m.astype(ml_dtypes.bfloat16)


def make_in_maps(inputs, n_cores, S, NB):
    f = lambda a: np.asarray(a, np.float32)
    xp, xs = f(inputs["x_prompt"]), f(inputs["x_sample"])
    shared = dict(
        w1g=f(inputs["w_ffn1_gate"])[0], w1u=f(inputs["w_ffn1_up"])[0], w1d=f(inputs["w_ffn1_down"])[0],
        w2g=f(inputs["w_ffn2_gate"])[0], w2u=f(inputs["w_ffn2_up"])[0], w2d=f(inputs["w_ffn2_down"])[0],
        gains=np.ascontiguousarray(np.concatenate(
            [_fm(inputs["ln_ffn1"][0]), _fm(inputs["ln_mix"][0]), _fm(inputs["ln_ffn2"][0])], axis=1)),
        lnf=f(inputs["ln_final"]),
        ident=np.eye(128, dtype=np.float32).astype(ml_dtypes.bfloat16),
        w_in=f(inputs["w_in"])[0], w_out=f(inputs["w_out"])[0],
        wabd=_blockdiag(f(inputs["w_gate_a"])[0]), wxbd=_blockdiag(f(inputs["w_gate_x"])[0]),
        cw=np.ascontiguousarray(f(inputs["conv_w"])[0].reshape(4, 4, 128).transpose(2, 1, 0)),
        cvec=np.ascontiguousarray(np.stack(
            [_fm(inputs["conv_b"][0]), _fm(f(inputs["b_gate_a"])[0].reshape(-1)),
             _fm(f(inputs["b_gate_x"])[0].reshape(-1)), _fm(inputs["lru_lambda"][0])], axis=1)),
        maskp=_mask_prompt(), masks=_mask_sample(),
    )
    maps = []
    for c in range(n_cores):
        bs = slice(c * NB, (c + 1) * NB)
        m = dict(shared)
        m["xin"] = np.ascontiguousarray(np.concatenate([xp[c, :S], xs[bs].reshape(NB * 8, D)], axis=0))
        cs = f(inputs["state_lru_conv"])[0, bs]
        m["convst"] = np.ascontiguousarray(cs.reshape(NB, 3, 4, 128).transpose(3, 2, 0, 1))
        hh = f(inputs["state_lru_h"])[0, bs]
        m["h0"] = np.ascontiguousarray(hh.reshape(NB, 4, 128).transpose(2, 1, 0))
        m["kcache"] = np.ascontiguousarray(f(inputs["cache_k_win"])[0, bs].reshape(NB, WBUF, 512))
        m["vcache"] = np.ascontiguousarray(f(inputs["cache_v_win"])[0, bs].reshape(NB, WBUF, 512))
        maps.append(m)
    return maps


def assemble(results, n_cores, S, NB):
    ys = [r["y"] for r in results]
    y_prompt = np.stack([y[:S] for y in ys])
    y_sample = np.concatenate([y[S:].reshape(NB, 8, D) for y in ys])
    kwin = np.stack([r["kwin"].reshape(WBUF, NH, HD) for r in results])[None]
    vwin = np.stack([r["vwin"].reshape(WBUF, NH, HD) for r in results])[None]
    hp = np.stack([r["lru_h_p"].T.reshape(DR) for r in results])[None]
    cp = np.stack([r["conv_p"] for r in results])[None]
    knew = np.concatenate([r["knew"].reshape(NB, 8, NH, HD) for r in results])[None]
    vnew = np.concatenate([r["vnew"].reshape(NB, 8, NH, HD) for r in results])[None]
    hs = np.concatenate([r["lru_h_s"].transpose(2, 1, 0).reshape(NB, DR) for r in results])[None]
    cs = np.concatenate([r["conv_s"] for r in results])[None]
    outs = (y_prompt, y_sample, kwin, vwin, hp, cp, knew, vnew, hs, cs)
    return tuple(np.ascontiguousarray(o, dtype=np.float32) for o in outs)


_NC_CACHE = {}


def kernel(**inputs):
    n_cores = 8
    S = inputs["x_prompt"].shape[1]
    NB = inputs["x_sample"].shape[0] // n_cores
    key = (S, NB)
    if key not in _NC_CACHE:
        _NC_CACHE[key] = Builder(S, NB).build()
    nc = _NC_CACHE[key]
    maps = make_in_maps(inputs, n_cores, S, NB)
    res = run_bass_kernel_spmd(nc, maps, core_ids=list(range(n_cores)))
    return assemble(res.results, n_cores, S, NB)
```

```python
from contextlib import ExitStack
import numpy as np
import ml_dtypes
import concourse.bass as bass
import concourse.mybir as mybir
from concourse.bass_utils import run_bass_kernel_spmd

F32 = mybir.dt.float32
BF16 = mybir.dt.bfloat16
AF = mybir.ActivationFunctionType
ALU = mybir.AluOpType

D = 1024
DFF = 2816
NFC = DFF // 128
NH = 8
HD = 64
DR = 512
INC = 2560
EPS = 1e-6
SEM_CAP = 30000
TT = 256
NSUB = TT // 128
WBUF = 2048
ARENA_WORDS = 51 * 1024
GELU_C = 1.5957691216057308


class _Op:
    __slots__ = ("eng", "fn", "reads", "writes", "stream", "is_dma", "deps", "signal", "sig")

    def __init__(self, eng, fn, reads, writes, stream):
        self.eng = eng
        self.fn = fn
        self.reads = tuple(reads)
        self.writes = tuple(writes)
        self.stream = stream
        self.is_dma = stream is not None
        self.deps = []
        self.signal = False
        self.sig = None


class Prog:
    ENGS = ("pe", "act", "dve", "pool", "sp")

    def __init__(self):
        self.ops = []
        self.res_w = {}
        self.res_r = {}
        self.bar_deps = []
        self.bar_pending = set()

    def barrier(self):
        last = {}
        for i, op in enumerate(self.ops):
            last[("dma", op.stream) if op.is_dma else ("eng", op.eng)] = i
        self.bar_deps = sorted(last.values())
        self.bar_pending = set(self.ENGS)
        self.res_w = {}
        self.res_r = {}

    def add(self, eng, fn, reads=(), writes=(), stream=None):
        op = _Op(eng, fn, reads, writes, stream)
        idx = len(self.ops)
        raw = set()
        other = set()
        for r in op.reads:
            raw.update(self.res_w.get(r, ()))
        for w in op.writes:
            other.update(self.res_w.get(w, ()))
            other.update(self.res_r.get(w, ()))
        deps = set()
        for d in raw | other:
            dop = self.ops[d]
            if dop.is_dma or op.is_dma or dop.eng != op.eng:
                deps.add(d)
            elif d in raw:
                deps.add(d)
        if eng in self.bar_pending:
            self.bar_pending.discard(eng)
            for d in self.bar_deps:
                dop = self.ops[d]
                if dop.is_dma or dop.eng != eng:
                    deps.add(d)
        op.deps = sorted(deps)
        for d in op.deps:
            self.ops[d].signal = True
        for r in op.reads:
            self.res_r.setdefault(r, []).append(idx)
        for w in op.writes:
            if self.res_r.get(w):
                self.res_w[w] = [idx]
                self.res_r[w] = []
            else:
                self.res_w.setdefault(w, []).append(idx)
        self.ops.append(op)
        return idx

    def finalize_outputs(self):
        last = {}
        for i, op in enumerate(self.ops):
            if op.is_dma:
                last[op.stream] = i
        for op in self.ops:
            if op.is_dma:
                op.signal = True
        self.final_dma = sorted(last.values())

    def emit(self, nc, sem_ctx):
        cnt = {}
        for op in self.ops:
            if not op.signal:
                continue
            key = ("dma", op.stream) if op.is_dma else ("eng", op.eng)
            c = cnt.get(key, 0)
            epoch, within = divmod(c, SEM_CAP if not op.is_dma else SEM_CAP // 16)
            cnt[key] = c + 1
            semname = "%s_%s_%d" % (key[0], key[1], epoch)
            val = (within + 1) * (16 if op.is_dma else 1)
            op.sig = (semname, val)
        per_eng = {e: [] for e in self.ENGS}
        for i, op in enumerate(self.ops):
            per_eng[op.eng].append(i)
        sems = {}

        def sem(name):
            if name not in sems:
                sems[name] = sem_ctx(name)
            return sems[name]

        for op in self.ops:
            if op.sig is not None:
                sem(op.sig[0])
        final_dma = self.final_dma

        with nc.Block() as block:
            def body(ename):
                def _run(engine):
                    seen = {}

                    def wait(i):
                        sname, val = self.ops[i].sig
                        if seen.get(sname, 0) >= val:
                            return
                        seen[sname] = val
                        engine.wait_ge(sem(sname), val)

                    for i in per_eng[ename]:
                        op = self.ops[i]
                        for d in op.deps:
                            wait(d)
                        ins = op.fn(engine)
                        if op.sig is not None:
                            ins.then_inc(sem(op.sig[0]), 16 if op.is_dma else 1)
                    if ename == "sp":
                        for i in final_dma:
                            wait(i)
                return _run

            block.tensor(body("pe"))
            block.scalar(body("act"))
            block.vector(body("dve"))
            block.gpsimd(body("pool"))
            block.sync(body("sp"))


def _bcast_rows(ap1d, nparts):
    n = ap1d.shape[-1]
    return bass.AP(ap1d.tensor, ap1d.offset, [[0, nparts], [1, n]])


class Builder:
    def __init__(self, S, NB, phases=("ffn1", "mix", "ffn2")):
        assert S % 2048 == 0 and (NB * 8) % 128 == 0
        self.S = S
        self.NB = NB
        self.NS = NB * 8
        self.NT = S + self.NS
        self.phases = phases
        self.nc = bass.Bass("TRN2", target_bir_lowering=False)
        self.P = Prog()
        self.stack = None
        self._stg_i = 0

    def dram_in(self, name, shape, dt=F32):
        return self.nc.dram_tensor(name, list(shape), dt, kind="ExternalInput").ap()

    def dram_out(self, name, shape, dt=F32):
        return self.nc.dram_tensor(name, list(shape), dt, kind="ExternalOutput").ap()

    def dram_tmp(self, name, shape, dt):
        return self.nc.dram_tensor(name, list(shape), dt, kind="Internal").ap()

    def alloc(self, shape, dt):
        esz = 4 if dt == F32 else 2
        n = 1
        for s in shape[1:]:
            n *= s
        nbytes = (n * esz + 31) // 32 * 32
        w0 = self.arena_off
        nw = nbytes // 4
        assert w0 + nw <= ARENA_WORDS, "SBUF arena overflow: %d" % (w0 + nw)
        self.arena_off = w0 + nw
        v = self.arena[:, w0:w0 + nw]
        if dt != F32:
            v = v.bitcast(dt)
        v = v[:, 0:n]
        if len(shape) == 3:
            v = v.rearrange("p (a b) -> p a b", a=shape[1])
        elif len(shape) == 4:
            v = v.rearrange("p (a b c) -> p a b c", a=shape[1], b=shape[2])
        return v

    def tiles(self):
        out = []
        t = 0
        while t < self.S:
            n = min(TT, self.S - t)
            out.append((t, n))
            t += n
        if self.NS:
            out.append((self.S, self.NS))
        return out

    def load_cast(self, dst_ap, src_ap, width, scale, key, idx):
        P = self.P
        slot = self._stg_i % 2
        self._stg_i += 1
        np_ = dst_ap.shape[0]
        view = self.stg[slot][0:np_, 0:width]
        if len(dst_ap.shape) == 3:
            view = view.rearrange("p (a b) -> p a b", a=dst_ap.shape[1])
        P.add("sp", lambda e, v=view, s=src_ap: e.dma_start(out=v, in_=s),
              writes=[("stg", slot)], stream="stg%d" % slot)
        rd = [("stg", slot)] + ([] if isinstance(scale, float) else ["gains"])
        if idx % 2 == 1:
            P.add("act", lambda e, o=dst_ap, v=view, sc=scale: e.mul(o, v, sc), reads=rd, writes=[key])
        else:
            P.add("dve", lambda e, o=dst_ap, v=view, sc=scale: e.tensor_scalar_mul(o, v, sc), reads=rd, writes=[key])

    def t_load(self, x_src, tl, i):
        t0, n = tl[i]
        ns = n // 128
        slot = i % 2
        src = x_src[t0:t0 + n, :].rearrange("(s p) d -> p s d", p=128)
        self.P.add("sp", lambda e, o=self.xbuf[slot][:, 0:ns, :], s=src: e.dma_start(out=o, in_=s),
                   writes=[("x", slot)], stream="x%d" % slot)

    def t_prep(self, tl, i):
        P = self.P
        t0, n = tl[i]
        ns = n // 128
        slot = i % 2
        xs = self.xbuf[slot]
        for s in range(ns):
            P.add("act", lambda e, s=s, xs=xs: e.activation(
                out=self.junk[:, :], in_=xs[:, s, :], func=AF.Square, scale=1.0 / 32.0,
                accum_out=self.ss[:, s:s + 1]),
                reads=[("x", slot)], writes=["junk", "ss"])
        P.add("act", lambda e: e.activation(out=self.rstd[:, 0:ns], in_=self.ss[:, 0:ns], func=AF.Sqrt,
                                            bias=self.epsc[:, 0:1], scale=1.0),
              reads=["ss", "epsc"], writes=["rstd"])
        P.add("dve", lambda e: e.reciprocal(self.rstd[:, 0:ns], self.rstd[:, 0:ns]),
              reads=["rstd"], writes=["rstd"])
        for s in range(ns):
            if s % 2 == 0:
                P.add("dve", lambda e, s=s, xs=xs: e.tensor_scalar_mul(self.xn[:, s, :], xs[:, s, :],
                                                                       self.rstd[:, s:s + 1]),
                      reads=[("x", slot), "rstd"], writes=[("xn", s)])
            else:
                P.add("act", lambda e, s=s, xs=xs: e.mul(self.xn[:, s, :], xs[:, s, :], self.rstd[:, s:s + 1]),
                      reads=[("x", slot), "rstd"], writes=[("xn", s)])

    def t_transposes(self, tl, i):
        P = self.P
        t0, n = tl[i]
        ns = n // 128
        for s in range(ns):
            bank = self.psT[s % 2]
            for kc in range(8):
                P.add("pe", lambda e, s=s, kc=kc, bank=bank: e.transpose(
                    out=bank[:, kc * 128:(kc + 1) * 128], in_=self.xn[:, s, kc * 128:(kc + 1) * 128],
                    identity=self.ident[:, :]),
                    reads=[("xn", s), "ident"], writes=[("psT", s % 2)])
            src = bank[:, :].rearrange("p (k t) -> p k t", k=8)
            dst = self.xnT[:, :, s * 128:(s + 1) * 128]
            if s % 2 == 0:
                P.add("act", lambda e, d=dst, sr=src: e.copy(d, sr), reads=[("psT", s % 2)], writes=["xnT"])
            else:
                P.add("dve", lambda e, d=dst, sr=src: e.tensor_copy(d, sr), reads=[("psT", s % 2)],
                      writes=["xnT"])

    def common_tile_bufs(self):
        self.xbuf = [self.alloc([128, NSUB, D], F32), self.alloc([128, NSUB, D], F32)]
        self.xn = self.alloc([128, NSUB, D], BF16)
        self.xnT = self.alloc([128, 8, TT], BF16)
        self.junk = self.alloc([128, D], BF16)
        self.ss = self.alloc([128, 4], F32)
        self.rstd = self.alloc([128, 4], F32)
        self.ss2 = self.alloc([128, 4], F32)
        self.rstd2 = self.alloc([128, 4], F32)
        self.stg = [self.alloc([128, DFF // 2], F32), self.alloc([128, DFF // 2], F32)]

    def ffn_phase(self, tag, x_src, x_dst, wg, wu, wd, gcol, final_norm):
        P = self.P
        self.arena_off = self.arena_mark
        self.common_tile_bufs()
        self.wA = self.alloc([128, 8, DFF], BF16)
        self.wB = self.alloc([128, 8, DFF], BF16)
        self.wC = self.alloc([128, NFC, D], BF16)
        self.aT = self.alloc([128, NFC, TT], BF16)
        self.sg = [self.alloc([128, TT], F32), self.alloc([128, TT], F32)]
        psG = [self.pb[0], self.pb[1]]
        psU = [self.pb[2], self.pb[3]]
        psD = [self.pb[4], self.pb[5]]
        self.psT = [self.pb[6].bitcast(BF16), self.pb[7].bitcast(BF16)]
        HW = DFF // 2
        k = 0
        for kc in range(8):
            for hh in range(2):
                self.load_cast(self.wA[:, kc, hh * HW:(hh + 1) * HW],
                               wg[kc * 128:(kc + 1) * 128, hh * HW:(hh + 1) * HW],
                               HW, self.gains[:, gcol * 8 + kc:gcol * 8 + kc + 1], "wA", k)
                k += 1
        for kc in range(8):
            for hh in range(2):
                self.load_cast(self.wB[:, kc, hh * HW:(hh + 1) * HW],
                               wu[kc * 128:(kc + 1) * 128, hh * HW:(hh + 1) * HW],
                               HW, self.gains[:, gcol * 8 + kc:gcol * 8 + kc + 1], "wB", k)
                k += 1
        for fc in range(NFC):
            self.load_cast(self.wC[:, fc, :], wd[fc * 128:(fc + 1) * 128, :], D, 0.5, "wC", k)
            k += 1
        tl = self.tiles()
        xbuf = self.xbuf

        def gate_up(i):
            t0, n = tl[i]
            for fc in range(NFC):
                pb = fc % 2
                pg, pu = psG[pb], psU[pb]
                for kc in range(8):
                    P.add("pe", lambda e, fc=fc, kc=kc, pg=pg: e.matmul(
                        pg[:, 0:n], lhsT=self.wA[:, kc, fc * 128:(fc + 1) * 128], rhs=self.xnT[:, kc, 0:n],
                        start=(kc == 0), stop=(kc == 7)),
                        reads=["wA", "xnT"], writes=[("psG", pb)])
                for kc in range(8):
                    P.add("pe", lambda e, fc=fc, kc=kc, pu=pu: e.matmul(
                        pu[:, 0:n], lhsT=self.wB[:, kc, fc * 128:(fc + 1) * 128], rhs=self.xnT[:, kc, 0:n],
                        start=(kc == 0), stop=(kc == 7)),
                        reads=["wB", "xnT"], writes=[("psU", pb)])
                sg = self.sg[pb]
                P.add("act", lambda e, pg=pg, sg=sg: e.activation(out=sg[:, 0:n], in_=pg[:, 0:n], func=AF.Silu),
                      reads=[("psG", pb)], writes=[("sg", pb)])
                P.add("dve", lambda e, fc=fc, pu=pu, sg=sg: e.tensor_tensor(
                    out=self.aT[:, fc, 0:n], in0=sg[:, 0:n], in1=pu[:, 0:n], op=ALU.mult),
                    reads=[("psU", pb), ("sg", pb)], writes=[("aT", fc)])

        def down(i):
            t0, n = tl[i]
            ns = n // 128
            slot = i % 2
            xs = xbuf[slot]
            j = 0
            for s in range(ns):
                for h in range(2):
                    pb = j % 2
                    j += 1
                    pd = psD[pb]
                    for fc in range(NFC):
                        P.add("pe", lambda e, fc=fc, s=s, h=h, pd=pd: e.matmul(
                            pd[:, :], lhsT=self.aT[:, fc, s * 128:(s + 1) * 128],
                            rhs=self.wC[:, fc, h * 512:(h + 1) * 512], start=(fc == 0), stop=(fc == NFC - 1)),
                            reads=["wC", ("aT", fc)], writes=[("psD", pb)])
                    P.add("dve", lambda e, s=s, h=h, pd=pd, xs=xs: e.tensor_tensor(
                        out=xs[:, s, h * 512:(h + 1) * 512], in0=pd[:, :], in1=xs[:, s, h * 512:(h + 1) * 512],
                        op=ALU.add),
                        reads=[("psD", pb), ("x", slot)], writes=[("x", slot)])
            if final_norm:
                for s in range(ns):
                    P.add("act", lambda e, s=s, xs=xs: e.activation(
                        out=self.junk[:, :], in_=xs[:, s, :], func=AF.Square, scale=1.0 / 32.0,
                        accum_out=self.ss2[:, s:s + 1]),
                        reads=[("x", slot)], writes=["junk", "ss2"])
                P.add("act", lambda e: e.activation(out=self.rstd2[:, 0:ns], in_=self.ss2[:, 0:ns], func=AF.Sqrt,
                                                    bias=self.epsc[:, 0:1], scale=1.0),
                      reads=["ss2", "epsc"], writes=["rstd2"])
                P.add("dve", lambda e: e.reciprocal(self.rstd2[:, 0:ns], self.rstd2[:, 0:ns]),
                      reads=["rstd2"], writes=["rstd2"])
                for s in range(ns):
                    P.add("dve", lambda e, s=s, xs=xs: e.scalar_tensor_tensor(
                        out=xs[:, s, :], in0=xs[:, s, :], scalar=self.rstd2[:, s:s + 1], in1=self.lnf[:, :],
                        op0=ALU.mult, op1=ALU.mult),
                        reads=[("x", slot), "rstd2", "lnf"], writes=[("x", slot)])
            dst = x_dst[t0:t0 + n, :].rearrange("(s p) d -> p s d", p=128)
            P.add("sp", lambda e, d=dst, o=xbuf[slot][:, 0:ns, :]: e.dma_start(out=d, in_=o),
                  reads=[("x", slot)], stream="xo%d" % slot)

        nt = len(tl)
        self.t_load(x_src, tl, 0)
        if nt > 1:
            self.t_load(x_src, tl, 1)
        self.t_prep(tl, 0)
        self.t_transposes(tl, 0)
        for i in range(nt):
            gate_up(i)
            if i + 1 < nt:
                self.t_prep(tl, i + 1)
            down(i)
            if i + 2 < nt:
                self.t_load(x_src, tl, i + 2)
            if i + 1 < nt:
                self.t_transposes(tl, i + 1)
        P.barrier()

    def mix_in_phase(self, x1, zT, rnnT, vtok, io):
        P = self.P
        S, NB, NS = self.S, self.NB, self.NS
        self.arena_off = self.arena_mark
        self.common_tile_bufs()
        Win = self.alloc([128, 8, INC], BF16)
        WaBD = self.alloc([128, 4, 128], BF16)
        WxBD = self.alloc([128, 4, 128], BF16)
        cw = self.alloc([128, 4, 4], F32)
        cvec = self.alloc([128, 4, 4], F32)
        lamc = self.alloc([128, 4], F32)
        onec = self.alloc([128, 1], F32)
        zst = [self.alloc([128, 12, TT], BF16), self.alloc([128, 12, TT], BF16)]
        ubuf = self.alloc([128, 4, TT + 3], F32)
        ubs = self.alloc([128, 4, NB, 11], F32)
        h0s = self.alloc([128, 4, NB], F32)
        hprev = self.alloc([128, 4], F32)
        hfin = self.alloc([128, 4, NB], F32)
        gs = self.alloc([128, 4, TT], F32)
        gt = self.alloc([128, 4, TT], F32)
        B = [self.alloc([128, 4, TT], F32) for _ in range(8)]
        xcb4 = self.alloc([128, 4, TT], BF16)
        rst = [self.alloc([128, 4, TT], BF16), self.alloc([128, 4, TT], BF16)]
        tok = [self.alloc([128, 512], F32), self.alloc([128, 512], F32)]
        tokb = self.alloc([128, 512], BF16)
        psZ = [self.pb[0], self.pb[1]]
        psA4 = [self.pb[2][:, 0:256], self.pb[2][:, 256:512], self.pb[3][:, 0:256], self.pb[3][:, 256:512]]
        psX4 = [self.pb[4][:, 0:256], self.pb[4][:, 256:512], self.pb[5][:, 0:256], self.pb[5][:, 256:512]]
        self.psT = [self.pb[6].bitcast(BF16), self.pb[7].bitcast(BF16)]
        NZ = len(psZ)

        k = 0
        for kc in range(8):
            for hh in range(2):
                self.load_cast(Win[:, kc, hh * 1280:(hh + 1) * 1280],
                               io["w_in"][kc * 128:(kc + 1) * 128, hh * 1280:(hh + 1) * 1280],
                               1280, self.gains[:, 8 + kc:8 + kc + 1], "Win", k)
                k += 1
        self.load_cast(WaBD, io["wabd"], 512, 1.0, "WaBD", 0)
        self.load_cast(WxBD, io["wxbd"], 512, 1.0, "WxBD", 1)
        P.add("sp", lambda e: e.dma_start(out=cw, in_=io["cw"]), writes=["cw"], stream="c_cw")
        P.add("sp", lambda e: e.dma_start(out=cvec, in_=io["cvec"]), writes=["cvec"], stream="c_cvec")
        P.add("sp", lambda e: e.dma_start(out=ubs[:, :, :, 0:3], in_=io["convst"]), writes=["ubs"], stream="c_ubs")
        P.add("sp", lambda e: e.dma_start(out=h0s, in_=io["h0"]), writes=["h0s"], stream="c_h0s")
        P.add("dve", lambda e: e.memset(onec, 1.0), writes=["onec"])
        P.add("dve", lambda e: e.memset(hprev, 0.0), writes=["hprev"])
        P.add("dve", lambda e: e.memset(ubuf[:, :, 0:3], 0.0), writes=[("ub", c) for c in range(4)])
        P.add("act", lambda e: e.activation(out=lamc, in_=cvec[:, 3, :], func=AF.Exp, scale=-1.0),
              reads=["cvec"], writes=["lamc"])
        P.add("act", lambda e: e.activation(out=lamc, in_=lamc, func=AF.Ln, bias=onec[:, 0:1], scale=1.0),
              reads=["lamc", "onec"], writes=["lamc"])
        P.add("dve", lambda e: e.tensor_scalar_mul(lamc, lamc, -8.0), reads=["lamc"], writes=["lamc"])

        tl = self.tiles()
        nt = len(tl)
        nprompt = nt - 1 if NS else nt

        def zproj(i, pending):
            t0, n = tl[i]
            is_s = (t0 >= S)
            zs = zst[i % 2]
            j = 0
            pending = list(pending)
            for c in range(20):
                if c >= 1 and pending:
                    pending.pop(0)()
                pz = psZ[j % NZ]
                pk = ("psZ", j % NZ)
                j += 1
                for kc in range(8):
                    P.add("pe", lambda e, c=c, kc=kc, pz=pz: e.matmul(
                        pz[:, 0:n], lhsT=Win[:, kc, c * 128:(c + 1) * 128], rhs=self.xnT[:, kc, 0:n],
                        start=(kc == 0), stop=(kc == 7)),
                        reads=["Win", "xnT"], writes=[pk])
                if c < 4:
                    P.add("act", lambda e, c=c, pz=pz: e.mul(zs[:, c, 0:n], pz[:, 0:n], 0.125),
                          reads=[pk], writes=[("zst", i % 2)])
                elif c < 12:
                    if c % 2 == 0:
                        P.add("act", lambda e, c=c, pz=pz: e.copy(zs[:, c, 0:n], pz[:, 0:n]),
                              reads=[pk], writes=[("zst", i % 2)])
                    else:
                        P.add("dve", lambda e, c=c, pz=pz: e.tensor_copy(zs[:, c, 0:n], pz[:, 0:n]),
                              reads=[pk], writes=[("zst", i % 2)])
                elif c < 16:
                    cc = c - 12
                    if is_s:
                        P.add("act", lambda e, cc=cc, pz=pz: e.copy(
                            ubs[:, cc, :, 3:11], pz[:, 0:n].rearrange("p (b t) -> p b t", t=8)),
                            reads=[pk], writes=[("ubs", cc)])
                    else:
                        P.add("act", lambda e, cc=cc, pz=pz: e.copy(ubuf[:, cc, 3:3 + n], pz[:, 0:n]),
                              reads=[pk], writes=[("ub", cc)])
                else:
                    cc = c - 16
                    P.add("act", lambda e, cc=cc, pz=pz: e.copy(gs[:, cc, 0:n], pz[:, 0:n]),
                          reads=[pk], writes=[("gs", cc)])
            dst = zT[:, :, t0:t0 + n].rearrange("c p t -> p c t")
            P.add("sp", lambda e, d=dst, o=zs[:, :, 0:n]: e.dma_start(out=d, in_=o),
                  reads=[("zst", i % 2)], stream="zo%d" % (i % 2))
            for f in pending:
                f()

        def gelu_gate(i):
            t0, n = tl[i]
            g = gs[:, :, 0:n]
            t = B[0][:, :, 0:n]
            P.add("dve", lambda e: e.tensor_tensor(out=t, in0=g, in1=g, op=ALU.mult),
                  reads=[("gs", c) for c in range(4)], writes=["t0"])
            P.add("dve", lambda e: e.tensor_scalar(out=t, in0=t, scalar1=0.044715, scalar2=1.0,
                                                   op0=ALU.mult, op1=ALU.add),
                  reads=["t0"], writes=["t0"])
            P.add("dve", lambda e: e.tensor_tensor(out=t, in0=t, in1=g, op=ALU.mult),
                  reads=["t0"] + [("gs", c) for c in range(4)], writes=["t0"])
            P.add("act", lambda e: e.activation(out=t, in_=t, func=AF.Sigmoid, scale=GELU_C),
                  reads=["t0"], writes=["t0"])
            P.add("dve", lambda e: e.tensor_tensor(out=gt[:, :, 0:n], in0=t, in1=g, op=ALU.mult),
                  reads=["t0"] + [("gs", c) for c in range(4)], writes=["gt"])

        def lru_prompt_steps(i):
            t0, n = tl[i]
            rs = rst[i % 2]
            XC, RG, IG, A, OM, BT, G0 = B[1], B[2], B[3], B[4], B[6], B[7], B[0]
            allxc = [("xc", c) for c in range(4)]
            allub = [("ub", c) for c in range(4)]
            allgs = [("gs", c) for c in range(4)]
            allrg = [("rg", c) for c in range(4)]
            g = gs[:, :, 0:n]
            t = G0[:, :, 0:n]
            om = OM[:, :, 0:n]
            bt = BT[:, :, 0:n]
            steps = []

            def conv0():
                for cc in range(4):
                    P.add("dve", lambda e, cc=cc: e.tensor_scalar(
                        out=XC[:, cc, 0:n], in0=ubuf[:, cc, 0:n], scalar1=cw[:, cc, 0:1],
                        scalar2=cvec[:, 0, cc:cc + 1], op0=ALU.mult, op1=ALU.add),
                        reads=[("ub", cc), "cw", "cvec"], writes=[("xc", cc)])
            steps.append(conv0)

            def convj(j):
                def f():
                    for cc in range(4):
                        P.add("dve", lambda e, cc=cc: e.scalar_tensor_tensor(
                            out=XC[:, cc, 0:n], in0=ubuf[:, cc, j:j + n], scalar=cw[:, cc, j:j + 1],
                            in1=XC[:, cc, 0:n], op0=ALU.mult, op1=ALU.add),
                            reads=[("ub", cc), "cw", ("xc", cc)], writes=[("xc", cc)])
                return f
            for j in range(1, 4):
                steps.append(convj(j))

            def cast():
                P.add("pool", lambda e: e.tensor_copy(ubuf[:, :, 0:3], ubuf[:, :, n:n + 3]), reads=allub,
                      writes=allub)
                P.add("act", lambda e: e.copy(xcb4[:, :, 0:n], XC[:, :, 0:n]), reads=allxc, writes=["xcb"])
            steps.append(cast)

            def gel1():
                P.add("dve", lambda e: e.tensor_tensor(out=t, in0=g, in1=g, op=ALU.mult), reads=allgs, writes=["t0"])
                P.add("dve", lambda e: e.tensor_scalar(out=t, in0=t, scalar1=0.044715, scalar2=1.0,
                                                       op0=ALU.mult, op1=ALU.add), reads=["t0"], writes=["t0"])
                P.add("dve", lambda e: e.tensor_tensor(out=t, in0=t, in1=g, op=ALU.mult), reads=["t0"] + allgs,
                      writes=["t0"])
            steps.append(gel1)

            def gates():
                for cc in range(4):
                    P.add("pe", lambda e, cc=cc: e.matmul(psA4[cc][:, 0:n], lhsT=WaBD[:, cc, :],
                                                          rhs=xcb4[:, cc, 0:n], start=True, stop=True),
                          reads=["WaBD", "xcb"], writes=[("psA", cc // 2)])
                    P.add("pe", lambda e, cc=cc: e.matmul(psX4[cc][:, 0:n], lhsT=WxBD[:, cc, :],
                                                          rhs=xcb4[:, cc, 0:n], start=True, stop=True),
                          reads=["WxBD", "xcb"], writes=[("psX", cc // 2)])
            steps.append(gates)

            def sig1():
                P.add("act", lambda e: e.activation(out=t, in_=t, func=AF.Sigmoid, scale=GELU_C),
                      reads=["t0"], writes=["t0"])
                for cc in range(4):
                    P.add("act", lambda e, cc=cc: e.activation(
                        out=RG[:, cc, 0:n], in_=psA4[cc][:, 0:n], func=AF.Sigmoid,
                        bias=cvec[:, 1, cc:cc + 1], scale=1.0),
                        reads=[("psA", cc // 2), "cvec"], writes=[("rg", cc)])
            steps.append(sig1)

            def sig2():
                for cc in range(4):
                    P.add("act", lambda e, cc=cc: e.activation(
                        out=IG[:, cc, 0:n], in_=psX4[cc][:, 0:n], func=AF.Sigmoid,
                        bias=cvec[:, 2, cc:cc + 1], scale=1.0),
                        reads=[("psX", cc // 2), "cvec"], writes=[("ig", cc)])
                P.add("dve", lambda e: e.tensor_tensor(out=gt[:, :, 0:n], in0=t, in1=g, op=ALU.mult),
                      reads=["t0"] + allgs, writes=["gt"])
            steps.append(sig2)

            def expa():
                for cc in range(4):
                    P.add("act", lambda e, cc=cc: e.activation(out=A[:, cc, 0:n], in_=RG[:, cc, 0:n], func=AF.Exp,
                                                               scale=lamc[:, cc:cc + 1]),
                          reads=[("rg", cc), "lamc"], writes=[("a", cc)])
            steps.append(expa)

            def om1():
                alla = [("a", c) for c in range(4)]
                P.add("dve", lambda e: e.tensor_tensor(out=om, in0=A[:, :, 0:n], in1=A[:, :, 0:n], op=ALU.mult),
                      reads=alla, writes=["om"])
                P.add("dve", lambda e: e.tensor_scalar(out=om, in0=om, scalar1=-1.0, scalar2=1.0,
                                                       op0=ALU.mult, op1=ALU.add), reads=["om"], writes=["om"])
                P.add("act", lambda e: e.activation(out=om, in_=om, func=AF.Sqrt), reads=["om"], writes=["om"])
                P.add("dve", lambda e: e.tensor_tensor(out=bt, in0=IG[:, :, 0:n], in1=XC[:, :, 0:n], op=ALU.mult),
                      reads=[("ig", c) for c in range(4)] + allxc, writes=["bt"])
            steps.append(om1)

            def bt2():
                P.add("dve", lambda e: e.tensor_tensor(out=bt, in0=bt, in1=om, op=ALU.mult),
                      reads=["bt", "om"], writes=["bt"])
            steps.append(bt2)

            def scans():
                for cc in range(4):
                    P.add("dve", lambda e, cc=cc: e.tensor_tensor_scan(
                        out=RG[:, cc, 0:n], data0=A[:, cc, 0:n], data1=BT[:, cc, 0:n],
                        initial=hprev[:, cc:cc + 1], op0=ALU.mult, op1=ALU.add),
                        reads=[("a", cc), "bt", "hprev"], writes=[("rg", cc)])
            steps.append(scans)

            def fin():
                P.add("dve", lambda e: e.tensor_copy(hprev[:, :], RG[:, :, n - 1]), reads=allrg, writes=["hprev"])
                P.add("dve", lambda e: e.tensor_tensor(out=rs[:, :, 0:n], in0=RG[:, :, 0:n], in1=gt[:, :, 0:n],
                                                       op=ALU.mult),
                      reads=allrg + ["gt"], writes=[("rst", i % 2)])
                dst = rnnT[:, :, t0:t0 + n].rearrange("c p t -> p c t")
                P.add("sp", lambda e, d=dst, o=rs[:, :, 0:n]: e.dma_start(out=d, in_=o),
                      reads=[("rst", i % 2)], stream="ro%d" % (i % 2))
                if i == nprompt - 1:
                    P.add("sp", lambda e: e.dma_start(out=io["lru_h_p"], in_=hprev), reads=["hprev"],
                          stream="o_hp")
            steps.append(fin)
            return steps

        def lru_sample(i):
            t0, n = tl[i]
            rs = rst[i % 2]
            tmp = [B[k][:, 0, :] for k in range(8)]
            psA, psX = psA4[0], psX4[0]
            xcb = xcb4[:, 0, :]
            for cc in range(4):
                xc = tmp[1][:, 0:n]
                xc3 = xc.rearrange("p (b t) -> p b t", t=8)
                uv = [ubs[:, cc, :, j:j + 8] for j in range(4)]
                ukey = ("ubs", cc)
                P.add("dve", lambda e, cc=cc, xc3=xc3, uv=uv: e.tensor_scalar(
                    out=xc3, in0=uv[0], scalar1=cw[:, cc, 0:1], scalar2=cvec[:, 0, cc:cc + 1],
                    op0=ALU.mult, op1=ALU.add),
                    reads=[ukey, "cw", "cvec"], writes=["xc"])
                for j in range(1, 4):
                    P.add("dve", lambda e, cc=cc, j=j, xc3=xc3, uv=uv: e.scalar_tensor_tensor(
                        out=xc3, in0=uv[j], scalar=cw[:, cc, j:j + 1], in1=xc3, op0=ALU.mult, op1=ALU.add),
                        reads=[ukey, "cw", "xc"], writes=["xc"])
                P.add("act", lambda e, xc=xc: e.copy(xcb[:, 0:n], xc), reads=["xc"], writes=["xcb"])
                P.add("pe", lambda e, cc=cc: e.matmul(psA[:, 0:n], lhsT=WaBD[:, cc, :], rhs=xcb[:, 0:n],
                                                      start=True, stop=True),
                      reads=["WaBD", "xcb"], writes=[("psA", 0)])
                P.add("pe", lambda e, cc=cc: e.matmul(psX[:, 0:n], lhsT=WxBD[:, cc, :], rhs=xcb[:, 0:n],
                                                      start=True, stop=True),
                      reads=["WxBD", "xcb"], writes=[("psX", 0)])
                rg = tmp[2][:, 0:n]
                ig = tmp[3][:, 0:n]
                P.add("act", lambda e, cc=cc, rg=rg: e.activation(out=rg, in_=psA[:, 0:n], func=AF.Sigmoid,
                                                                  bias=cvec[:, 1, cc:cc + 1], scale=1.0),
                      reads=[("psA", 0), "cvec"], writes=["rg"])
                P.add("act", lambda e, cc=cc, ig=ig: e.activation(out=ig, in_=psX[:, 0:n], func=AF.Sigmoid,
                                                                  bias=cvec[:, 2, cc:cc + 1], scale=1.0),
                      reads=[("psX", 0), "cvec"], writes=["ig"])
                a = tmp[4][:, 0:n]
                th = tmp[5][:, 0:n]
                P.add("act", lambda e, cc=cc, a=a, rg=rg: e.activation(out=a, in_=rg, func=AF.Exp,
                                                                       scale=lamc[:, cc:cc + 1]),
                      reads=["rg", "lamc"], writes=["a"])
                P.add("act", lambda e, cc=cc, th=th, rg=rg: e.activation(out=th, in_=rg, func=AF.Tanh,
                                                                         scale=lamc[:, cc:cc + 1]),
                      reads=["rg", "lamc"], writes=["th"])
                om = tmp[6][:, 0:n]
                P.add("dve", lambda e, th=th, om=om: e.tensor_scalar(out=om, in0=th, scalar1=-1.0, scalar2=1.0,
                                                                     op0=ALU.mult, op1=ALU.add),
                      reads=["th"], writes=["om"])
                P.add("dve", lambda e, om=om: e.reciprocal(om, om), reads=["om"], writes=["om"])
                P.add("dve", lambda e, th=th, om=om: e.scalar_tensor_tensor(
                    out=om, in0=th, scalar=-2.0, in1=om, op0=ALU.mult, op1=ALU.mult),
                    reads=["th", "om"], writes=["om"])
                P.add("act", lambda e, om=om: e.activation(out=om, in_=om, func=AF.Sqrt), reads=["om"],
                      writes=["om"])
                bt = tmp[7][:, 0:n]
                P.add("dve", lambda e, om=om, ig=ig, bt=bt: e.tensor_tensor(out=bt, in0=om, in1=ig, op=ALU.mult),
                      reads=["om", "ig"], writes=["bt"])
                P.add("dve", lambda e, xc=xc, bt=bt: e.tensor_tensor(out=bt, in0=bt, in1=xc, op=ALU.mult),
                      reads=["bt", "xc"], writes=["bt"])
                hs = tmp[2][:, 0:n]
                a3 = a.rearrange("p (b t) -> p b t", t=8)
                bt3 = bt.rearrange("p (b t) -> p b t", t=8)
                t3 = tmp[3][:, 0:NB]
                P.add("dve", lambda e, cc=cc, a3=a3, t3=t3: e.tensor_tensor(
                    out=t3, in0=a3[:, :, 0], in1=h0s[:, cc, :], op=ALU.mult),
                    reads=["a", "h0s", "bt"], writes=["ig"])
                P.add("dve", lambda e, bt3=bt3, t3=t3: e.tensor_tensor(
                    out=bt3[:, :, 0], in0=bt3[:, :, 0], in1=t3, op=ALU.add),
                    reads=["ig", "bt"], writes=["bt"])
                P.add("dve", lambda e, a3=a3: e.memset(a3[:, :, 0], 0.0), reads=["ig"], writes=["a"])
                P.add("dve", lambda e, a=a, bt=bt, hs=hs: e.tensor_tensor_scan(
                    out=hs, data0=a, data1=bt, initial=0.0, op0=ALU.mult, op1=ALU.add),
                    reads=["a", "bt", "th"], writes=["rg"])
                hs3 = hs.rearrange("p (b t) -> p b t", t=8)
                P.add("dve", lambda e, cc=cc, hs3=hs3: e.tensor_copy(hfin[:, cc, :], hs3[:, :, 7]),
                      reads=["rg"], writes=["hfin"])
                P.add("dve", lambda e, cc=cc, hs=hs: e.tensor_tensor(out=rs[:, cc, 0:n], in0=hs,
                                                                     in1=gt[:, cc, 0:n], op=ALU.mult),
                      reads=["rg", "gt"], writes=[("rst", i % 2)])
            dst = rnnT[:, :, t0:t0 + n].rearrange("c p t -> p c t")
            P.add("sp", lambda e, d=dst, o=rs[:, :, 0:n]: e.dma_start(out=d, in_=o),
                  reads=[("rst", i % 2)], stream="ro%d" % (i % 2))
            P.add("sp", lambda e: e.dma_start(out=io["lru_h_s"], in_=hfin), reads=["hfin"], stream="o_hs")

        tk = [0]

        def tok_outputs(i):
            t0, n = tl[i]
            is_s = (t0 >= S)
            if not is_s and t0 < S - WBUF:
                return
            ns = n // 128
            for s in range(ns):
                tt = t0 + s * 128
                last = (not is_s) and (tt + 128 == S)
                sects = [0, 1] + ([2] if (is_s or last) else [])
                for sec in sects:
                    j = tk[0]
                    tk[0] += 1
                    pz = psZ[j % NZ]
                    pk = ("psZ", j % NZ)
                    c0 = 512 * (1 + sec)
                    for kc in range(8):
                        P.add("pe", lambda e, kc=kc, pz=pz, s=s, c0=c0: e.matmul(
                            pz[:, :], lhsT=self.xnT[:, kc, s * 128:(s + 1) * 128], rhs=Win[:, kc, c0:c0 + 512],
                            start=(kc == 0), stop=(kc == 7)),
                            reads=["Win", "xnT"], writes=[pk])
                    tb = tok[j % 2]
                    tkey = ("tok", j % 2)
                    P.add("act", lambda e, tb=tb, pz=pz: e.copy(tb, pz[:, :]), reads=[pk], writes=[tkey])
                    if is_s:
                        if sec < 2:
                            dst = (io["knew"], io["vnew"])[sec]
                            P.add("sp", lambda e, d=dst, tb=tb: e.dma_start(out=d, in_=tb), reads=[tkey],
                                  stream="tk%d" % (j % 2))
                            if sec == 1:
                                P.add("dve", lambda e, tb=tb: e.tensor_copy(tokb, tb), reads=[tkey],
                                      writes=["tokb"])
                                P.add("sp", lambda e: e.dma_start(out=vtok, in_=tokb), reads=["tokb"],
                                      stream="tkb")
                        else:
                            for b in range(NB):
                                P.add("sp", lambda e, b=b, tb=tb: e.dma_start(
                                    out=io["conv_s"][b], in_=tb[b * 8 + 5:b * 8 + 8, :]),
                                    reads=[tkey], stream="tk%d" % (j % 2))
                    else:
                        if sec < 2:
                            r0 = tt - (S - WBUF)
                            dst = (io["kwin"], io["vwin"])[sec][r0:r0 + 128, :]
                            P.add("sp", lambda e, d=dst, tb=tb: e.dma_start(out=d, in_=tb), reads=[tkey],
                                  stream="tk%d" % (j % 2))
                        else:
                            P.add("sp", lambda e, tb=tb: e.dma_start(out=io["conv_p"], in_=tb[125:128, :]),
                                  reads=[tkey], stream="tk%d" % (j % 2))

        self.t_load(x1, tl, 0)
        if nt > 1:
            self.t_load(x1, tl, 1)
        self.t_prep(tl, 0)
        self.t_transposes(tl, 0)
        pending = []
        for i in range(nt):
            zproj(i, pending)
            pending = []
            tok_outputs(i)
            if i + 2 < nt:
                self.t_load(x1, tl, i + 2)
            if i + 1 < nt:
                self.t_prep(tl, i + 1)
                self.t_transposes(tl, i + 1)
            if tl[i][0] >= S:
                gelu_gate(i)
                lru_sample(i)
            else:
                pending = lru_prompt_steps(i)
        for f in pending:
            f()
        P.barrier()

    def attn_prompt_phase(self, zT, attnT, io):
        P = self.P
        S = self.S
        self.arena_off = self.arena_mark
        qZ = self.alloc([128, 2, S], BF16)
        kT = self.alloc([128, S], BF16)
        vT = self.alloc([128, S], BF16)
        acc = self.alloc([128, 2, S], F32)
        mask = self.alloc([128, 2, 256], BF16)
        onesf = self.alloc([128, 64], F32)
        pT = [self.alloc([128, 2, 256], BF16) for _ in range(3)]
        vblk = [self.alloc([128, 2, 65], BF16) for _ in range(3)]
        ast = [self.alloc([128, 512], BF16), self.alloc([128, 512], BF16)]
        psS = [self.pb[0], self.pb[1], self.pb[2], self.pb[3]]
        psV = [self.pb[4].bitcast(BF16)[:, 0:128], self.pb[5].bitcast(BF16)[:, 0:128]]
        psO = [self.pb[6], self.pb[7]]
        psB = [self.pb[6], self.pb[7]]
        P.add("sp", lambda e: e.dma_start(out=mask, in_=io["maskp"]), writes=["mask"], stream="c_mask")
        P.add("dve", lambda e: e.memset(onesf, 1.0), writes=["onesf"])
        for v in range(3):
            P.add("dve", lambda e, v=v: e.memset(vblk[v][:, :, 64:65], 1.0), writes=[("vblk", v)])
        it = [0]
        P.add("pool", lambda e: e.memset(qZ[64:128, 0, :], 0.0), writes=["qT"])
        P.add("pool", lambda e: e.memset(qZ[0:64, 1, :], 0.0), writes=["qT"])
        for c in range(4):
            for h0 in range(0, S, 4096):
                h1 = min(S, h0 + 4096)
                P.add("sp", lambda e, c=c, h0=h0, h1=h1: e.dma_start(
                    out=qZ[0:64, 0, h0:h1], in_=zT[c, 0:64, h0:h1]), writes=["qT"], stream="ld_qT")
                P.add("sp", lambda e, c=c, h0=h0, h1=h1: e.dma_start(
                    out=qZ[64:128, 1, h0:h1], in_=zT[c, 64:128, h0:h1]), writes=["qT"], stream="ld_qT")
            for nm, buf, ch in (("kT", kT, 4 + c), ("vT", vT, 8 + c)):
                for h0 in range(0, S, 4096):
                    h1 = min(S, h0 + 4096)
                    P.add("sp", lambda e, buf=buf, ch=ch, h0=h0, h1=h1: e.dma_start(
                        out=buf[:, h0:h1], in_=zT[ch, :, h0:h1]), writes=[nm], stream="ld_" + nm)
            iters = []
            for d in (1, 4, 16):
                span = 128 * d
                nsb = S // span
                for nb in range(nsb):
                    nq = 256 if nb + 1 < nsb else 128
                    for r in range(d):
                        base = nb * span + r
                        iters.append((base, d, nq))

            def stage1(base, d, nq, j):
                ks = slice(base, base + d * 127 + 1, d)
                qs = slice(base, base + d * (nq - 1) + 1, d)
                s4, s3, s2 = j % 4, j % 3, j % 2
                sv = psS[s4].rearrange("p (e q) -> p e q", e=2)[:, :, 0:nq]
                P.add("pe", lambda e: e.matmul(sv, lhsT=kT[:, ks], rhs=qZ[:, :, qs], start=True, stop=True),
                      reads=["kT", "qT"], writes=[("psS", s4)])
                P.add("pe", lambda e: e.transpose(out=psV[s2], in_=vT[:, ks], identity=self.ident[:, :]),
                      reads=["vT", "ident"], writes=[("psV", s2)])
                P.add("act", lambda e: e.activation(out=pT[s3][:, :, 0:nq], in_=sv, func=AF.Exp),
                      reads=[("psS", s4)], writes=[("pT", s3, 0), ("pT", s3, 1)])
                P.add("pool", lambda e: e.tensor_tensor(
                    out=pT[s3][:, 0, 0:nq], in0=pT[s3][:, 0, 0:nq], in1=mask[:, 0, 0:nq], op=ALU.mult),
                    reads=[("pT", s3, 0), "mask"], writes=[("pT", s3, 0)])
                P.add("dve", lambda e: e.tensor_tensor(
                    out=pT[s3][:, 1, 0:nq], in0=pT[s3][:, 1, 0:nq], in1=mask[:, 1, 0:nq], op=ALU.mult),
                    reads=[("pT", s3, 1), "mask"], writes=[("pT", s3, 1)])
                P.add("act", lambda e: e.copy(vblk[s3][:, :, 0:64], psV[s2].rearrange("p (e c) -> p e c", e=2)),
                      reads=[("psV", s2)], writes=[("vblk", s3)])

            def stage2(base, d, nq, j):
                qs = slice(base, base + d * (nq - 1) + 1, d)
                s3, s2 = j % 3, j % 2
                for e2 in range(2):
                    P.add("pe", lambda e, e2=e2: e.matmul(
                        psO[s2][0:65, e2 * 256:e2 * 256 + nq], lhsT=vblk[s3][:, e2, :],
                        rhs=pT[s3][:, e2, 0:nq], start=True, stop=True),
                        reads=[("vblk", s3), ("pT", s3, e2)], writes=[("psO", s2)])
                ov = psO[s2][0:65, :].rearrange("p (e q) -> p e q", e=2)
                if d == 1:
                    a0 = acc[0:65, :, base:base + 128]
                    if base == 0:
                        P.add("dve", lambda e: e.tensor_copy(a0, ov[:, :, 0:128]),
                              reads=[("psO", s2), "acc"], writes=["acc"])
                    else:
                        P.add("dve", lambda e: e.tensor_tensor(out=a0, in0=ov[:, :, 0:128], in1=a0, op=ALU.add),
                              reads=[("psO", s2), "acc"], writes=["acc"])
                    if nq == 256:
                        a1 = acc[0:65, :, base + 128:base + 256]
                        P.add("dve", lambda e: e.tensor_copy(a1, ov[:, :, 128:256]),
                              reads=[("psO", s2), "acc"], writes=["acc"])
                else:
                    av = acc[0:65, :, qs]
                    P.add("dve", lambda e: e.tensor_tensor(out=av, in0=ov[:, :, 0:nq], in1=av, op=ALU.add),
                          reads=[("psO", s2), "acc"], writes=["acc"])

            SK = 2
            for idx, (base, d, nq) in enumerate(iters):
                stage1(base, d, nq, idx)
                if idx >= SK:
                    pbase, pd, pnq = iters[idx - SK]
                    stage2(pbase, pd, pnq, idx - SK)
            for idx in range(max(0, len(iters) - SK), len(iters)):
                pbase, pd, pnq = iters[idx]
                stage2(pbase, pd, pnq, idx)
            for e2 in range(2):
                P.add("act", lambda e, e2=e2: e.activation(out=acc[64:65, e2, :], in_=acc[64:65, e2, :],
                                                           func=AF.Ln),
                      reads=["acc"], writes=["acc"])
                P.add("act", lambda e, e2=e2: e.activation(out=acc[64:65, e2, :], in_=acc[64:65, e2, :],
                                                           func=AF.Exp, scale=-1.0),
                      reads=["acc"], writes=["acc"])
            k = 0
            for e2 in range(2):
                for p0 in range(0, S, 512):
                    pbb = k % 2
                    k += 1
                    P.add("pe", lambda e, e2=e2, p0=p0, pbb=pbb: e.matmul(
                        psB[pbb][0:64, :], lhsT=onesf[64:65, 0:64], rhs=acc[64:65, e2, p0:p0 + 512],
                        start=True, stop=True),
                        reads=["onesf", "acc"], writes=[("psO", pbb)])
                    P.add("dve", lambda e, e2=e2, p0=p0, pbb=pbb: e.tensor_tensor(
                        out=ast[pbb][0:64, :], in0=acc[0:64, e2, p0:p0 + 512], in1=psB[pbb][0:64, :],
                        op=ALU.mult),
                        reads=["acc", ("psO", pbb)], writes=[("ast", pbb)])
                    P.add("sp", lambda e, e2=e2, p0=p0, pbb=pbb, c=c: e.dma_start(
                        out=attnT[c, e2 * 64:(e2 + 1) * 64, p0:p0 + 512], in_=ast[pbb][0:64, :]),
                        reads=[("ast", pbb)], stream="ao%d" % pbb)
        P.barrier()

    def attn_sample_phase(self, zT, vtok, attnT, io):
        P = self.P
        S, NB, NS = self.S, self.NB, self.NS
        self.arena_off = self.arena_mark
        NBLK = WBUF // 128
        stgk = [self.alloc([128, 8, 512], F32) for _ in range(4)]
        kb = self.alloc([128, NBLK, 512], BF16)
        kTs2 = [self.alloc([128, 4, WBUF + 8], BF16), self.alloc([128, 4, WBUF + 8], BF16)]
        vb = [self.alloc([128, NBLK + 1, 8, 65], BF16), self.alloc([128, NBLK + 1, 8, 65], BF16)]
        qTs = self.alloc([128, 4, NS], BF16)
        kTn = self.alloc([128, 4, NS], BF16)
        vnb = self.alloc([128, 512], BF16)
        masks = self.alloc([128, NBLK + 1, 8, 8], BF16)
        pTs = [self.alloc([128, NBLK + 1, 8, 8], BF16), self.alloc([128, NBLK + 1, 8, 8], BF16)]
        accs = self.alloc([128, 8, NS], F32)
        onesf = self.alloc([128, 64], F32)
        ast = [self.alloc([128, NS], BF16), self.alloc([128, NS], BF16)]
        psK = [self.pb[0].bitcast(BF16), self.pb[1].bitcast(BF16)]
        psMain = [self.pb[2], self.pb[3]]
        psNew = [self.pb[4], self.pb[5]]
        psO = [self.pb[6][:, 0:64], self.pb[7][:, 0:64]]
        psB = self.pb[6][:, 128:256]
        P.add("sp", lambda e: e.dma_start(out=masks, in_=io["masks"]), writes=["masks"], stream="c_masks")
        P.add("sp", lambda e: e.dma_start(out=qTs, in_=zT[0:4, :, S:S + NS].rearrange("c p t -> p c t")),
              writes=["qTs"], stream="c_qTs")
        P.add("sp", lambda e: e.dma_start(out=kTn, in_=zT[4:8, :, S:S + NS].rearrange("c p t -> p c t")),
              writes=["kTn"], stream="c_kTn")
        P.add("dve", lambda e: e.memset(onesf, 1.0), writes=["onesf"])
        for v in range(2):
            P.add("pool", lambda e, v=v: e.memset(vb[v][:, :, :, 64:65], 1.0), writes=[("vb", v)])
        sk = [0]

        def load_half(src, b, half):
            j = sk[0]
            sk[0] += 1
            slot = j % 4
            sv = src[b, half * 1024:(half + 1) * 1024, :].rearrange("(k p) d -> p k d", p=128)
            P.add("sp", lambda e, slot=slot, sv=sv: e.dma_start(out=stgk[slot], in_=sv),
                  writes=[("stgk", slot)], stream="sk%d" % slot)
            return slot

        def sview(blk, h):
            par, hh = h % 2, h // 2
            if blk < NBLK:
                off = blk * 32 + hh * 8
                return psMain[par][:, off:off + 8]
            return psNew[par][:, hh * 8:hh * 8 + 8]

        def prep(b):
            vv = vb[b % 2]
            vkey = ("vb", b % 2)
            kT_b = kTs2[b % 2]
            kkey = ("kTs", b % 2)
            for half in range(2):
                slot = load_half(io["kcache"], b, half)
                if half == 0:
                    P.add("dve", lambda e, slot=slot, half=half: e.tensor_copy(
                        kb[:, half * 8:(half + 1) * 8, :], stgk[slot]),
                        reads=[("stgk", slot)], writes=[("kb", half)])
                else:
                    P.add("act", lambda e, slot=slot, half=half: e.copy(
                        kb[:, half * 8:(half + 1) * 8, :], stgk[slot]),
                        reads=[("stgk", slot)], writes=[("kb", half)])
            for half in range(2):
                slot = load_half(io["vcache"], b, half)
                dstv = vv[:, half * 8:(half + 1) * 8, :, 0:64]
                srcv = stgk[slot].rearrange("p k (h c) -> p k h c", h=8)
                if half == 0:
                    P.add("act", lambda e, dstv=dstv, srcv=srcv: e.copy(dstv, srcv),
                          reads=[("stgk", slot)], writes=[vkey])
                else:
                    P.add("dve", lambda e, dstv=dstv, srcv=srcv: e.tensor_copy(dstv, srcv),
                          reads=[("stgk", slot)], writes=[vkey])
            P.add("sp", lambda e, b=b: e.dma_start(out=vnb[0:8, :], in_=vtok[b * 8:(b + 1) * 8, :]),
                  writes=["vnb"], stream="vn")
            P.add("dve", lambda e, vv=vv: e.tensor_copy(vv[0:8, NBLK, :, 0:64],
                                                        vnb[0:8, :].rearrange("p (h c) -> p h c", h=8)),
                  reads=["vnb"], writes=[vkey])
            P.add("pool", lambda e, b=b: e.tensor_copy(kT_b[:, :, WBUF:WBUF + 8], kTn[:, :, b * 8:(b + 1) * 8]),
                  reads=["kTn"], writes=[kkey])
            for blk in range(NBLK):
                pk = blk % 2
                for c in range(4):
                    P.add("pe", lambda e, blk=blk, c=c, pk=pk: e.transpose(
                        out=psK[pk][:, c * 128:(c + 1) * 128], in_=kb[:, blk, c * 128:(c + 1) * 128],
                        identity=self.ident[:, :]),
                        reads=[("kb", blk // 8), "ident"], writes=[("psK", pk)])
                src = psK[pk][:, 0:512].rearrange("p (c k) -> p c k", c=4)
                dst = kT_b[:, :, blk * 128:(blk + 1) * 128]
                if blk % 2 == 0:
                    P.add("act", lambda e, d=dst, s=src: e.copy(d, s), reads=[("psK", pk)], writes=[kkey])
                else:
                    P.add("dve", lambda e, d=dst, s=src: e.tensor_copy(d, s), reads=[("psK", pk)],
                          writes=[kkey])

        def scores(b):
            kT_b = kTs2[b % 2]
            kkey = ("kTs", b % 2)
            for blk in range(NBLK + 1):
                nk = 128 if blk < NBLK else 8
                for hh in range(4):
                    for par in range(2):
                        h = 2 * hh + par
                        rows = slice(par * 64, par * 64 + 64)
                        P.add("pe", lambda e, blk=blk, h=h, rows=rows, nk=nk: e.matmul(
                            sview(blk, h)[0:nk, :], lhsT=kT_b[rows, h // 2, blk * 128:blk * 128 + nk],
                            rhs=qTs[rows, h // 2, b * 8:(b + 1) * 8], start=True, stop=True),
                            reads=[kkey, "qTs"], writes=["psSs"])

        def softmax_pv(b):
            vv = vb[b % 2]
            vkey = ("vb", b % 2)
            pp = pTs[b % 2]
            pkey = ("pTs", b % 2)
            for par in range(2):
                P.add("act", lambda e, par=par, pp=pp: e.activation(
                    out=pp[:, 0:NBLK, par::2, :],
                    in_=psMain[par].rearrange("p (k h t) -> p k h t", k=NBLK, h=4), func=AF.Exp),
                    reads=["psSs"], writes=[pkey])
                P.add("act", lambda e, par=par, pp=pp: e.activation(
                    out=pp[0:8, NBLK, par::2, :],
                    in_=psNew[par][0:8, 0:32].rearrange("p (h t) -> p h t", h=4), func=AF.Exp),
                    reads=["psSs"], writes=[pkey])
            P.add("dve", lambda e, pp=pp: e.tensor_tensor(out=pp[:, 0:NBLK, :, :], in0=pp[:, 0:NBLK, :, :],
                                                          in1=masks[:, 0:NBLK, :, :], op=ALU.mult),
                  reads=[pkey, "masks"], writes=[pkey])
            P.add("dve", lambda e, pp=pp: e.tensor_tensor(out=pp[0:8, NBLK, :, :], in0=pp[0:8, NBLK, :, :],
                                                          in1=masks[0:8, NBLK, :, :], op=ALU.mult),
                  reads=[pkey, "masks"], writes=[pkey])
            po = psO[b % 2]
            for h in range(8):
                for blk in range(NBLK + 1):
                    nk = 128 if blk < NBLK else 8
                    P.add("pe", lambda e, blk=blk, h=h, nk=nk, po=po, vv=vv, pp=pp: e.matmul(
                        po[0:65, h * 8:(h + 1) * 8], lhsT=vv[0:nk, blk, h, :], rhs=pp[0:nk, blk, h, :],
                        start=(blk == 0), stop=(blk == NBLK)),
                        reads=[vkey, pkey], writes=[("psOs", b % 2)])
            P.add("dve", lambda e, b=b, po=po: e.tensor_copy(
                accs[0:65, :, b * 8:(b + 1) * 8], po[0:65, :].rearrange("p (h t) -> p h t", h=8)),
                reads=[("psOs", b % 2)], writes=["accs"])

        prep(0)
        for b in range(NB):
            scores(b)
            if b + 1 < NB:
                prep(b + 1)
            softmax_pv(b)
        for h in range(8):
            P.add("dve", lambda e, h=h: e.reciprocal(accs[64:65, h, :], accs[64:65, h, :]),
                  reads=["accs"], writes=["accs"])
            P.add("pe", lambda e, h=h: e.matmul(psB[0:64, 0:NS], lhsT=onesf[64:65, 0:64], rhs=accs[64:65, h, :],
                                                start=True, stop=True),
                  reads=["onesf", "accs"], writes=[("psOs", 0)])
            P.add("dve", lambda e, h=h: e.tensor_tensor(out=ast[h % 2][0:64, :], in0=accs[0:64, h, :],
                                                        in1=psB[0:64, 0:NS], op=ALU.mult),
                  reads=["accs", ("psOs", 0)], writes=[("ast", h % 2)])
            P.add("sp", lambda e, h=h: e.dma_start(out=attnT[h // 2, (h % 2) * 64:(h % 2) * 64 + 64, S:S + NS],
                                                   in_=ast[h % 2][0:64, :]),
                  reads=[("ast", h % 2)], stream="aso%d" % (h % 2))
        P.barrier()

    def mix_out_phase(self, x1, attnT, rnnT, x2, io):
        P = self.P
        self.arena_off = self.arena_mark
        self.common_tile_bufs()
        WoA = self.alloc([128, 4, D], BF16)
        WoR = self.alloc([128, 4, D], BF16)
        at = [self.alloc([128, 4, TT], BF16), self.alloc([128, 4, TT], BF16)]
        rt = [self.alloc([128, 4, TT], BF16), self.alloc([128, 4, TT], BF16)]
        psD = [self.pb[4], self.pb[5]]
        wo = io["w_out"]
        for cc in range(4):
            self.load_cast(WoA[:, cc, :], wo[cc * 128:(cc + 1) * 128, :], D, 1.0, "WoA", cc)
        for cc in range(4):
            self.load_cast(WoR[:, cc, :], wo[512 + cc * 128:512 + (cc + 1) * 128, :], D, 1.0, "WoR", cc)
        tl = self.tiles()
        nt = len(tl)

        def load(i):
            t0, n = tl[i]
            slot = i % 2
            self.t_load(x1, tl, i)
            P.add("sp", lambda e: e.dma_start(out=at[slot][:, :, 0:n],
                                              in_=attnT[:, :, t0:t0 + n].rearrange("c p t -> p c t")),
                  writes=[("at", slot)], stream="at%d" % slot)
            P.add("sp", lambda e: e.dma_start(out=rt[slot][:, :, 0:n],
                                              in_=rnnT[:, :, t0:t0 + n].rearrange("c p t -> p c t")),
                  writes=[("rt", slot)], stream="rt%d" % slot)

        load(0)
        if nt > 1:
            load(1)
        j = 0
        for i in range(nt):
            t0, n = tl[i]
            ns = n // 128
            slot = i % 2
            xs = self.xbuf[slot]
            for s in range(ns):
                for hf in range(2):
                    pd = psD[j % 2]
                    pk = ("psD", j % 2)
                    j += 1
                    for cc in range(4):
                        P.add("pe", lambda e, cc=cc, s=s, hf=hf, pd=pd, slot=slot: e.matmul(
                            pd[:, :], lhsT=at[slot][:, cc, s * 128:(s + 1) * 128],
                            rhs=WoA[:, cc, hf * 512:(hf + 1) * 512], start=(cc == 0), stop=False),
                            reads=["WoA", ("at", slot)], writes=[pk])
                    for cc in range(4):
                        P.add("pe", lambda e, cc=cc, s=s, hf=hf, pd=pd, slot=slot: e.matmul(
                            pd[:, :], lhsT=rt[slot][:, cc, s * 128:(s + 1) * 128],
                            rhs=WoR[:, cc, hf * 512:(hf + 1) * 512], start=False, stop=(cc == 3)),
                            reads=["WoR", ("rt", slot)], writes=[pk])
                    P.add("dve", lambda e, s=s, hf=hf, pd=pd, xs=xs: e.tensor_tensor(
                        out=xs[:, s, hf * 512:(hf + 1) * 512], in0=pd[:, :],
                        in1=xs[:, s, hf * 512:(hf + 1) * 512], op=ALU.add),
                        reads=[pk, ("x", slot)], writes=[("x", slot)])
            dst = x2[t0:t0 + n, :].rearrange("(s p) d -> p s d", p=128)
            P.add("sp", lambda e, d=dst, o=xs[:, 0:ns, :]: e.dma_start(out=d, in_=o),
                  reads=[("x", slot)], stream="xo%d" % slot)
            if i + 2 < nt:
                load(i + 2)
        P.barrier()

    def build(self):
        nc = self.nc
        P = self.P
        NT, S, NB, NS = self.NT, self.S, self.NB, self.NS
        xin = self.dram_in("xin", [NT, D])
        w1g = self.dram_in("w1g", [D, DFF]); w1u = self.dram_in("w1u", [D, DFF]); w1d = self.dram_in("w1d", [DFF, D])
        w2g = self.dram_in("w2g", [D, DFF]); w2u = self.dram_in("w2u", [D, DFF]); w2d = self.dram_in("w2d", [DFF, D])
        gains_d = self.dram_in("gains", [128, 24])
        lnf_d = self.dram_in("lnf", [D])
        ident_d = self.dram_in("ident", [128, 128], BF16)
        io = {}
        io["w_in"] = self.dram_in("w_in", [D, INC])
        io["w_out"] = self.dram_in("w_out", [D, D])
        io["wabd"] = self.dram_in("wabd", [128, 4, 128])
        io["wxbd"] = self.dram_in("wxbd", [128, 4, 128])
        io["cw"] = self.dram_in("cw", [128, 4, 4])
        io["cvec"] = self.dram_in("cvec", [128, 4, 4])
        io["convst"] = self.dram_in("convst", [128, 4, NB, 3])
        io["h0"] = self.dram_in("h0", [128, 4, NB])
        io["kcache"] = self.dram_in("kcache", [NB, WBUF, 512])
        io["vcache"] = self.dram_in("vcache", [NB, WBUF, 512])
        io["maskp"] = self.dram_in("maskp", [128, 2, 256], BF16)
        io["masks"] = self.dram_in("masks", [128, 17, 8, 8], BF16)
        y = self.dram_out("y", [NT, D])
        io["kwin"] = self.dram_out("kwin", [WBUF, 512])
        io["vwin"] = self.dram_out("vwin", [WBUF, 512])
        io["lru_h_p"] = self.dram_out("lru_h_p", [128, 4])
        io["conv_p"] = self.dram_out("conv_p", [3, 512])
        io["knew"] = self.dram_out("knew", [NS, 512])
        io["vnew"] = self.dram_out("vnew", [NS, 512])
        io["lru_h_s"] = self.dram_out("lru_h_s", [128, 4, NB])
        io["conv_s"] = self.dram_out("conv_s", [NB, 3, 512])
        x1 = self.dram_tmp("x1", [NT, D], F32)
        x2 = self.dram_tmp("x2", [NT, D], F32)
        zT = self.dram_tmp("zT", [12, 128, NT], BF16)
        rnnT = self.dram_tmp("rnnT", [4, 128, NT], BF16)
        attnT = self.dram_tmp("attnT", [4, 128, NT], BF16)
        vtok = self.dram_tmp("vtok", [NS, 512], BF16)
        with ExitStack() as stack:
            self.stack = stack
            self.arena = stack.enter_context(nc.sbuf_tensor("arena", [128, ARENA_WORDS], F32))
            self.pbig = stack.enter_context(nc.psum_tensor("pbig", [128, 4096], F32))
            self.pb = [self.pbig[:, i * 512:(i + 1) * 512] for i in range(8)]
            self.arena_off = 0
            self.gains = self.alloc([128, 24], F32)
            self.lnf = self.alloc([128, D], F32)
            self.ident = self.alloc([128, 128], BF16)
            self.epsc = self.alloc([128, 1], F32)
            self.arena_mark = self.arena_off

            P.add("dve", lambda e: e.memset(self.epsc, EPS), writes=["epsc"])
            P.add("sp", lambda e: e.dma_start(out=self.gains, in_=gains_d), writes=["gains"], stream="c_gains")
            P.add("sp", lambda e: e.dma_start(out=self.lnf, in_=_bcast_rows(lnf_d, 128)), writes=["lnf"],
                  stream="c_lnf")
            P.add("sp", lambda e: e.dma_start(out=self.ident, in_=ident_d), writes=["ident"], stream="c_ident")
            P.barrier()

            src = xin
            if "ffn1" in self.phases:
                self.ffn_phase("f1", src, x1, w1g, w1u, w1d, 0, final_norm=False)
                src = x1
            ph = self.phases
            if "mix" in ph or "mix_in" in ph:
                self.mix_in_phase(src, zT, rnnT, vtok, io)
            if "mix" in ph or "attn_p" in ph:
                self.attn_prompt_phase(zT, attnT, io)
            if "mix" in ph or "attn_s" in ph:
                self.attn_sample_phase(zT, vtok, attnT, io)
            if "mix" in ph or "mix_out" in ph:
                self.mix_out_phase(src, attnT, rnnT, x2, io)
                src = x2
            if "ffn2" in self.phases:
                self.ffn_phase("f2", src, y, w2g, w2u, w2d, 2, final_norm=True)

            P.finalize_outputs()
            P.emit(nc, lambda name: stack.enter_context(nc.semaphore(name)))
        return nc


def _fm(v):
    return np.ascontiguousarray(np.asarray(v, np.float32).reshape(-1, 128).T)


def _blockdiag(w):
    out = np.zeros((128, 4, 128), np.float32)
    for c in range(4):
        out[0:64, c, 0:64] = w[2 * c]
        out[64:128, c, 64:128] = w[2 * c + 1]
    return out


def _mask_prompt():
    j = np.arange(128)[:, None]
    i = np.arange(128)[None, :]
    m = np.concatenate([(j <= i), (j >= i)], axis=1).astype(np.float32)
    return np.ascontiguousarray(np.broadcast_to(m[:, None, :], (128, 2, 256))).astype(ml_dtypes.bfloat16)


def _mask_sample():
    m = np.zeros((128, 17, 8, 8), np.float32)
    p = np.arange(128)
    for blk in range(17):
        for t in range(8):
            s = blk * 128 + p
            dd = WBUF + t - s
            mult = ((dd >= 0) & (dd <= 128)).astype(np.float32)
            mult += ((dd >= 0) & (dd % 4 == 0) & (dd <= 512))
            mult += ((dd >= 0) & (dd % 16 == 0) & (dd <= 2048))
            if blk == 16:
                mult = np.where(p < 8, mult, 0.0)
            m[:, blk, :, t] = mult[:, None]
    return m.astype(ml_dtypes.bfloat16)


def make_in_maps(inputs, n_cores, S, NB):
    f = lambda a: np.asarray(a, np.float32)
    xp, xs = f(inputs["x_prompt"]), f(inputs["x_sample"])
    shared = dict(
        w1g=f(inputs["w_ffn1_gate"])[0], w1u=f(inputs["w_ffn1_up"])[0], w1d=f(inputs["w_ffn1_down"])[0],
        w2g=f(inputs["w_ffn2_gate"])[0], w2u=f(inputs["w_ffn2_up"])[0], w2d=f(inputs["w_ffn2_down"])[0],
        gains=np.ascontiguousarray(np.concatenate(
            [_fm(inputs["ln_ffn1"][0]), _fm(inputs["ln_mix"][0]), _fm(inputs["ln_ffn2"][0])], axis=1)),
        lnf=f(inputs["ln_final"]),
        ident=np.eye(128, dtype=np.float32).astype(ml_dtypes.bfloat16),
        w_in=f(inputs["w_in"])[0], w_out=f(inputs["w_out"])[0],
        wabd=_blockdiag(f(inputs["w_gate_a"])[0]), wxbd=_blockdiag(f(inputs["w_gate_x"])[0]),
        cw=np.ascontiguousarray(f(inputs["conv_w"])[0].reshape(4, 4, 128).transpose(2, 1, 0)),
        cvec=np.ascontiguousarray(np.stack(
            [_fm(inputs["conv_b"][0]), _fm(f(inputs["b_gate_a"])[0].reshape(-1)),
             _fm(f(inputs["b_gate_x"])[0].reshape(-1)), _fm(inputs["lru_lambda"][0])], axis=1)),
        maskp=_mask_prompt(), masks=_mask_sample(),
    )
    maps = []
    for c in range(n_cores):
        bs = slice(c * NB, (c + 1) * NB)
        m = dict(shared)
        m["xin"] = np.ascontiguousarray(np.concatenate([xp[c, :S], xs[bs].reshape(NB * 8, D)], axis=0))
        cs = f(inputs["state_lru_conv"])[0, bs]
        m["convst"] = np.ascontiguousarray(cs.reshape(NB, 3, 4, 128).transpose(3, 2, 0, 1))
        hh = f(inputs["state_lru_h"])[0, bs]
        m["h0"] = np.ascontiguousarray(hh.reshape(NB, 4, 128).transpose(2, 1, 0))
        m["kcache"] = np.ascontiguousarray(f(inputs["cache_k_win"])[0, bs].reshape(NB, WBUF, 512))
        m["vcache"] = np.ascontiguousarray(f(inputs["cache_v_win"])[0, bs].reshape(NB, WBUF, 512))
        maps.append(m)
    return maps


def assemble(results, n_cores, S, NB):
    ys = [r["y"] for r in results]
    y_prompt = np.stack([y[:S] for y in ys])
    y_sample = np.concatenate([y[S:].reshape(NB, 8, D) for y in ys])
    kwin = np.stack([r["kwin"].reshape(WBUF, NH, HD) for r in results])[None]
    vwin = np.stack([r["vwin"].reshape(WBUF, NH, HD) for r in results])[None]
    hp = np.stack([r["lru_h_p"].T.reshape(DR) for r in results])[None]
    cp = np.stack([r["conv_p"] for r in results])[None]
    knew = np.concatenate([r["knew"].reshape(NB, 8, NH, HD) for r in results])[None]
    vnew = np.concatenate([r["vnew"].reshape(NB, 8, NH, HD) for r in results])[None]
    hs = np.concatenate([r["lru_h_s"].transpose(2, 1, 0).reshape(NB, DR) for r in results])[None]
    cs = np.concatenate([r["conv_s"] for r in results])[None]
    outs = (y_prompt, y_sample, kwin, vwin, hp, cp, knew, vnew, hs, cs)
    return tuple(np.ascontiguousarray(o, dtype=np.float32) for o in outs)


_NC_CACHE = {}


def kernel(**inputs):
    n_cores = 8
    S = inputs["x_prompt"].shape[1]
    NB = inputs["x_sample"].shape[0] // n_cores
    key = (S, NB)
    if key not in _NC_CACHE:
        _NC_CACHE[key] = Builder(S, NB).build()
    nc = _NC_CACHE[key]
    maps = make_in_maps(inputs, n_cores, S, NB)
    res = run_bass_kernel_spmd(nc, maps, core_ids=list(range(n_cores)))
    return assemble(res.results, n_cores, S, NB)
```
